# Optimizing a Trainium2 kernel written in Bass

```python
import jax, jax.numpy as jnp
from jax import lax
import numpy as np

D_MODEL = 2048
BATCH = 8
SEQ = 2048
DEPTH = 2

GRID_W = 64
CTX_LEN = 256
MIX_WIDTH = D_MODEL
ATTN_WIDTH = MIX_WIDTH // 2
FOURIER_WIDTH = MIX_WIDTH - ATTN_WIDTH
HEAD_DIM = 128
N_Q_HEADS = ATTN_WIDTH // HEAD_DIM
N_KV_HEADS = max(1, N_Q_HEADS // 4)
Q_PER_KV = N_Q_HEADS // N_KV_HEADS
FOURIER_GROUP = 128
N_FOURIER_GROUPS = FOURIER_WIDTH // FOURIER_GROUP
FFN_DIM = ((8 * D_MODEL // 3 + 255) // 256) * 256
ROPE_THETA = 10000.0
Q_BLOCK = 128
EPS = 1e-6
N_MOD = 6
Q_END = N_Q_HEADS * HEAD_DIM
K_END = Q_END + N_KV_HEADS * HEAD_DIM
V_END = K_END + N_KV_HEADS * HEAD_DIM
IN_WIDTH = V_END + FOURIER_WIDTH

kernel_name = "hymba_style_attn_fnet_convglu_dit"


def rms_norm(x, g):
    xf = x.astype(jnp.float32)
    y = xf * lax.rsqrt(jnp.mean(xf * xf, axis=-1, keepdims=True) + EPS)
    return (y * g.astype(jnp.float32)).astype(x.dtype)


def modulate(x, g_pre, shift, scale):
    return rms_norm(x, g_pre) * (1 + scale) + shift


def rope_tables(n):
    rows = n // GRID_W
    row = jnp.repeat(jnp.arange(rows), GRID_W).astype(jnp.float32)
    col = jnp.tile(jnp.arange(GRID_W), rows).astype(jnp.float32)
    n_pairs_axis = HEAD_DIM // 4
    freqs = ROPE_THETA ** (-jnp.arange(n_pairs_axis, dtype=jnp.float32) / n_pairs_axis)
    ang = jnp.concatenate([row[:, None] * freqs, col[:, None] * freqs], axis=-1)
    return jnp.cos(ang), jnp.sin(ang)


def apply_rope(x, cos, sin):
    n = x.shape[1]
    bshape = (1, n) + (1,) * (x.ndim - 3) + (HEAD_DIM // 2,)
    c, s = cos.reshape(bshape), sin.reshape(bshape)
    xf = x.astype(jnp.float32).reshape(x.shape[:-1] + (HEAD_DIM // 2, 2))
    e, o = xf[..., 0], xf[..., 1]
    out = jnp.stack([e * c - o * s, e * s + o * c], axis=-1)
    return out.reshape(x.shape).astype(x.dtype)


def attend(q, k, v):
    s = jnp.einsum('bqkgd,bskd->bkgqs', q, k).astype(jnp.float32) * (HEAD_DIM ** -0.5)
    p = jax.nn.softmax(s, axis=-1).astype(v.dtype)
    return jnp.einsum('bkgqs,bskd->bqkgd', p, v)


def latent_attention(q, k_lat, v_lat, k_ctx, v_ctx):
    b, n = q.shape[:2]
    k_all = jnp.concatenate([k_lat, k_ctx], axis=1)
    v_all = jnp.concatenate([v_lat, v_ctx], axis=1)
    nb = n // Q_BLOCK
    qb = jnp.moveaxis(q.reshape((b, nb, Q_BLOCK) + q.shape[2:]), 1, 0)
    out = lax.map(lambda blk: attend(blk, k_all, v_all), qb)
    return jnp.moveaxis(out, 0, 1).reshape(b, n, ATTN_WIDTH)


def split_proj(p):
    b, t = p.shape[:2]
    q = p[..., :Q_END].reshape(b, t, N_KV_HEADS, Q_PER_KV, HEAD_DIM)
    k = p[..., Q_END:K_END].reshape(b, t, N_KV_HEADS, HEAD_DIM)
    v = p[..., K_END:V_END].reshape(b, t, N_KV_HEADS, HEAD_DIM)
    f = p[..., V_END:]
    return q, k, v, f


def fourier_mix(u, w_four):
    b, t = u.shape[:2]
    g = u.reshape(b, t, N_FOURIER_GROUPS, FOURIER_GROUP).astype(jnp.float32)
    f = jnp.fft.fft2(g, axes=(1, 3), norm='ortho').real.astype(u.dtype)
    return jnp.einsum('btgc,gcd->btgd', f, w_four).reshape(b, t, FOURIER_WIDTH)


def merge_groups(attn_out, four_out, g_attn, g_four, w_out):
    y = jnp.concatenate([rms_norm(attn_out, g_attn), rms_norm(four_out, g_four)], axis=-1)
    return y @ w_out


def dwconv3(u, w, b):
    up = jnp.pad(u, ((0, 0), (1, 1), (0, 0)))
    return up[:, :-2] * w[0] + up[:, 1:-1] * w[1] + up[:, 2:] * w[2] + b


def conv_glu(h, w_up, conv_w, conv_b, w_down):
    u = h @ w_up
    gate, val = u[..., :FFN_DIM], u[..., FFN_DIM:]
    return (jax.nn.gelu(dwconv3(gate, conv_w, conv_b)) * val) @ w_down


def setup_inputs(seed: int = 0) -> dict:
    key = jax.random.key(seed)
    ks = jax.random.split(key, 21)
    f32 = jnp.float32

    def nrm(k, shape, s):
        return s * jax.random.normal(k, shape, f32)

    def gain(k, shape):
        return 1.0 + 0.1 * jax.random.normal(k, shape, f32)

    return {
        "x": nrm(ks[0], (BATCH, SEQ, D_MODEL), 1.0),
        "c": nrm(ks[1], (BATCH, D_MODEL), 1.0),
        "ctx": nrm(ks[2], (BATCH, CTX_LEN, D_MODEL), 1.0),
        "c_ctx": nrm(ks[3], (D_MODEL,), 1.0),
        "w_mod": nrm(ks[4], (DEPTH, D_MODEL, N_MOD * D_MODEL), 0.5 * D_MODEL ** -0.5),
        "b_mod": nrm(ks[5], (DEPTH, N_MOD * D_MODEL), 0.02),
        "g_pre_mix": gain(ks[6], (DEPTH, D_MODEL)),
        "g_post_mix": gain(ks[7], (DEPTH, D_MODEL)),
        "g_pre_ffn": gain(ks[8], (DEPTH, D_MODEL)),
        "g_post_ffn": gain(ks[9], (DEPTH, D_MODEL)),
        "w_in": nrm(ks[10], (DEPTH, D_MODEL, IN_WIDTH), D_MODEL ** -0.5),
        "q_norm": gain(ks[11], (DEPTH, HEAD_DIM)),
        "k_norm": gain(ks[12], (DEPTH, HEAD_DIM)),
        "w_four": nrm(ks[13], (DEPTH, N_FOURIER_GROUPS, FOURIER_GROUP, FOURIER_GROUP), FOURIER_GROUP ** -0.5),
        "g_attn_out": gain(ks[14], (DEPTH, ATTN_WIDTH)),
        "g_four_out": gain(ks[15], (DEPTH, FOURIER_WIDTH)),
        "w_out": nrm(ks[16], (DEPTH, MIX_WIDTH, D_MODEL), MIX_WIDTH ** -0.5),
        "w_up": nrm(ks[17], (DEPTH, D_MODEL, 2 * FFN_DIM), D_MODEL ** -0.5),
        "conv_w": nrm(ks[18], (DEPTH, 3, FFN_DIM), 3 ** -0.5),
        "conv_b": nrm(ks[19], (DEPTH, FFN_DIM), 0.02),
        "w_down": nrm(ks[20], (DEPTH, FFN_DIM, D_MODEL), FFN_DIM ** -0.5),
    }


def reference(x, c, ctx, c_ctx, w_mod, b_mod, g_pre_mix, g_post_mix, g_pre_ffn, g_post_ffn,
              w_in, q_norm, k_norm, w_four, g_attn_out, g_four_out, w_out,
              w_up, conv_w, conv_b, w_down):
    b, n, d = x.shape
    ROWS = n // GRID_W
    cos, sin = rope_tables(ROWS * GRID_W)
    xc = ctx
    silu_c = jax.nn.silu(c)
    silu_cc = jax.nn.silu(c_ctx)

    for i in range(DEPTH):
        last = i == DEPTH - 1
        mod_l = (silu_c @ w_mod[i] + b_mod[i]).reshape(b, N_MOD, 1, d)
        mod_c = (silu_cc @ w_mod[i] + b_mod[i]).reshape(N_MOD, d)

        h_l = modulate(x, g_pre_mix[i], mod_l[:, 0], mod_l[:, 1])
        h_c = modulate(xc, g_pre_mix[i], mod_c[0], mod_c[1])

        q_l, k_l, v_l, f_l = split_proj(h_l @ w_in[i])
        q_l = apply_rope(rms_norm(q_l, q_norm[i]), cos, sin)
        k_l = apply_rope(rms_norm(k_l, k_norm[i]), cos, sin)

        if last:
            p_kv = h_c @ w_in[i][:, Q_END:V_END]
            k_c = rms_norm(p_kv[..., :K_END - Q_END].reshape(b, -1, N_KV_HEADS, HEAD_DIM), k_norm[i])
            v_c = p_kv[..., K_END - Q_END:].reshape(b, -1, N_KV_HEADS, HEAD_DIM)
        else:
            q_c, k_c, v_c, f_c = split_proj(h_c @ w_in[i])
            q_c = rms_norm(q_c, q_norm[i])
            k_c = rms_norm(k_c, k_norm[i])

        attn_l = latent_attention(q_l, k_l, v_l, k_c, v_c)
        four_l = fourier_mix(f_l, w_four[i])
        mix_l = merge_groups(attn_l, four_l, g_attn_out[i], g_four_out[i], w_out[i])
        x_new = x + mod_l[:, 2] * rms_norm(mix_l, g_post_mix[i])

        if not last:
            attn_c = attend(q_c, k_c, v_c).reshape(b, -1, ATTN_WIDTH)
            four_c = fourier_mix(f_c, w_four[i])
            mix_c = merge_groups(attn_c, four_c, g_attn_out[i], g_four_out[i], w_out[i])
            xc = xc + mod_c[2] * rms_norm(mix_c, g_post_mix[i])
        x = x_new

        h_l = modulate(x, g_pre_ffn[i], mod_l[:, 3], mod_l[:, 4])
        y_l = conv_glu(h_l, w_up[i], conv_w[i], conv_b[i], w_down[i])
        x = x + mod_l[:, 5] * rms_norm(y_l, g_post_ffn[i])
        if not last:
            h_c = modulate(xc, g_pre_ffn[i], mod_c[3], mod_c[4])
            y_c = conv_glu(h_c, w_up[i], conv_w[i], conv_b[i], w_down[i])
            xc = xc + mod_c[5] * rms_norm(y_c, g_post_ffn[i])

    return x
```

```python
import contextlib
import numpy as np
import ml_dtypes
import concourse.bass as bass
import concourse.mybir as mybir
from concourse.bass_utils import run_bass_kernel_spmd

F32 = mybir.dt.float32
BF16 = mybir.dt.bfloat16
AF = mybir.ActivationFunctionType
ALU = mybir.AluOpType

D = 2048
T = 2048
CTX = 256
TA = T + CTX
DEPTH = 2
FFN = 5632
NFC = FFN // 128
EPS = 1e-6
NCORES = 8

COMPUTE = ("pe", "act", "dve", "pool")
DMAQ = ("sp", "pq")
NSLOT = 8
NEXTMOD = True


class Op:
    __slots__ = ("eng", "idx", "fn", "waits", "signal", "is_dma", "slot", "slot_val", "sigcount")

    def __init__(self, eng, idx, fn, is_dma):
        self.eng = eng
        self.idx = idx
        self.fn = fn
        self.waits = []
        self.signal = False
        self.is_dma = is_dma
        self.slot = None
        self.slot_val = None
        self.sigcount = None


class Sched:
    def __init__(self, nc):
        self.nc = nc
        self.streams = {e: [] for e in COMPUTE + ("sp",)}
        self.last_writer = {}
        self.readers = {}
        self.waited = {e: {} for e in COMPUTE + ("sp",)}
        self.waited_dma = {e: set() for e in COMPUTE + ("sp",)}
        self.dma_count = {q: 0 for q in DMAQ}
        self.dma_ops = {q: [] for q in DMAQ}
        self.last_compute = {e: None for e in COMPUTE}

    @staticmethod
    def _stream_of(eng):
        return "pool" if eng == "pq" else eng

    def op(self, eng, fn, reads=(), writes=(), relax=False):
        sname = self._stream_of(eng)
        stream = self.streams[sname]
        is_dma = eng in DMAQ
        o = Op(eng, len(stream), fn, is_dma)
        writes = list(writes) + [t for t in reads if isinstance(t, tuple) and t[0] == "ps" and t not in writes]
        deps = []
        for t in reads:
            w = self.last_writer.get(t)
            if w is not None:
                deps.append((w, "raw"))
        for t in writes:
            w = self.last_writer.get(t)
            if w is not None:
                deps.append((w, "waw"))
            for r in self.readers.get(t, ()):
                deps.append((r, "war"))
        if is_dma:
            n = self.dma_count[eng]
            o.slot = n % NSLOT
            o.slot_val = 16 * (n // NSLOT + 1)
            if n >= NSLOT:
                deps.append((self.dma_ops[eng][n - NSLOT], "slot"))
            self.dma_count[eng] = n + 1
            self.dma_ops[eng].append(o)
        best = {}
        for d, kind in deps:
            if d is o:
                continue
            if d.is_dma:
                self._add_wait(o, sname, d, kind)
                continue
            if d.eng == sname and not is_dma:
                if sname == "pe" or kind == "slot":
                    continue
                if relax and kind != "raw":
                    continue
            cur = best.get(d.eng)
            if cur is None or d.idx > cur.idx:
                best[d.eng] = d
        for d in best.values():
            self._add_wait(o, sname, d, "raw")
        stream.append(o)
        if not is_dma and eng in COMPUTE:
            self.last_compute[eng] = o
        for t in reads:
            self.readers.setdefault(t, []).append(o)
        for t in writes:
            self.last_writer[t] = o
            self.readers[t] = []
        return o

    def _add_wait(self, o, sname, d, kind):
        if d.is_dma:
            if d in self.waited_dma[sname]:
                return
            self.waited_dma[sname].add(d)
            o.waits.append(d)
            return
        dname = d.eng
        if dname == sname and not o.is_dma:
            if sname == "pe":
                return
            if kind == "slot":
                return
        if self.waited[sname].get(dname, -1) >= d.idx:
            return
        self.waited[sname][dname] = d.idx
        d.signal = True
        o.waits.append(d)

    def barrier(self):
        lasts = [o for o in self.last_compute.values() if o is not None]
        dmas = []
        for q in DMAQ:
            dmas += self.dma_ops[q][-NSLOT:]
        for sname in list(self.streams.keys()):
            stream = self.streams[sname]
            o = Op(sname, len(stream), lambda e: e.nop(), False)
            o.signal = False
            for d in lasts:
                if d.eng != sname:
                    self._add_wait(o, sname, d, "raw")
            for d in dmas:
                self._add_wait(o, sname, d, "raw")
            stream.append(o)
            o.eng = "__nop__"

    def emit(self, final_ops):
        nc = self.nc
        with contextlib.ExitStack() as st:
            sems = {e: st.enter_context(nc.semaphore("s_" + e)) for e in COMPUTE}
            dsems = {q: [st.enter_context(nc.semaphore(f"d_{q}{i}")) for i in range(NSLOT)] for q in DMAQ}
            for e in COMPUTE:
                c = 0
                for o in self.streams[e]:
                    if o.is_dma or o.eng == "__nop__":
                        continue
                    if o.signal:
                        c += 1
                        o.sigcount = c
            block = st.enter_context(nc.Block())
            engmap = {"pe": block.tensor, "act": block.scalar, "dve": block.vector,
                      "pool": block.gpsimd, "sp": block.sync}

            def make(sname):
                def body(eng):
                    for o in self.streams[sname]:
                        for d in o.waits:
                            if d.is_dma:
                                eng.wait_ge(dsems[d.eng][d.slot], d.slot_val)
                            else:
                                eng.wait_ge(sems[d.eng], d.sigcount)
                        ins = o.fn(eng)
                        if o.is_dma:
                            ins.then_inc(dsems[o.eng][o.slot], 16)
                        elif o.signal:
                            ins.then_inc(sems[o.eng], 1)
                    for d in final_ops:
                        if self._stream_of(d.eng) == sname:
                            eng.wait_ge(dsems[d.eng][d.slot], d.slot_val)
                return body

            for sname in ("pe", "act", "dve", "pool", "sp"):
                engmap[sname](make(sname))


class Arena:
    def __init__(self, nc, lo, hi):
        self.nc = nc
        self.lo = lo
        self.hi = hi
        self.off = lo
        self.n = 0

    def mark(self):
        return self.off

    def reset(self, to):
        self.off = to

    def alloc(self, shape, dtype, name="t"):
        size = int(np.prod(shape[1:])) * (4 if dtype == F32 else 2)
        off = (self.off + 63) // 64 * 64
        assert off + size <= self.hi, f"SBUF arena overflow: {name} {shape} off={off} size={size} hi={self.hi}"
        self.off = off + size
        self.n += 1
        return self.nc.alloc_sbuf_tensor_at(f"{name}_{self.n}", list(shape), dtype, offset=off)


class Pipe:
    def __init__(self):
        self.q = []

    def defer(self, delay, fn):
        self.q.append([delay, fn])

    def tick(self):
        cur, self.q = self.q, []
        for item in cur:
            item[0] -= 1
            if item[0] <= 0:
                item[1]()
            else:
                self.q.append(item)

    def flush(self):
        while self.q:
            self.tick()


def build_nc(nlayers=DEPTH, debug=False, gelu_mode="tanh_act", stop=None):
    nc = bass.Bass("TRN2", target_bir_lowering=False)

    def din(name, shape, dt=F32):
        return nc.dram_tensor(name, list(shape), dt, kind="ExternalInput").ap()

    def dscr(name, shape, dt):
        if debug:
            return nc.dram_tensor(name, list(shape), dt, kind="ExternalOutput").ap()
        return nc.dram_tensor(name, list(shape), dt).ap()

    xt = din("xt", [128, 16, TA])
    cc = din("cc", [128, 32])
    wmod = din("wmod", [DEPTH, 48, 128, 16, 256])
    bmod2 = din("bmod2", [DEPTH, 128, 192])
    g2 = din("g2", [DEPTH, 4, 128, 32])
    wqkf = din("wqkf", [DEPTH, 18, 128, 16, 128])
    wv = din("wv", [DEPTH, 128, 16, 256])
    qkn = din("qkn", [DEPTH, 128, 2])
    w4 = din("w4", [DEPTH, 128, 8, 128])
    gy = din("gy", [DEPTH, 128, 16])
    wout = din("wout", [DEPTH, 128, 16, D])
    wup = din("wup", [DEPTH, NFC, 128, 2, 16, 128])
    cw = din("cw", [DEPTH, 128, NFC * 3])
    cbias = din("cb", [DEPTH, 128, NFC])
    wdn = din("wdn", [DEPTH, 16, 128, NFC, 128])
    ones_d = din("ones", [128, 128], BF16)
    swp_d = din("swp", [128, 128], BF16)
    cs_d = din("cs", [128, 256], BF16)
    ropec_d = din("ropec", [128, T])
    ropes_d = din("ropes", [128, T])
    dftc_d = din("dftc", [128, 16, T], BF16)
    dfts_d = din("dfts", [128, 16, T], BF16)
    dftcc_d = din("dftcc", [128, 2, CTX], BF16)
    dftsc_d = din("dftsc", [128, 2, CTX], BF16)
    out_d = nc.dram_tensor("out", [128, 16, T], F32, kind="ExternalOutput").ap()

    R1 = dscr("R1", [128, 16, TA], F32)
    R2 = dscr("R2", [128, 16, TA], F32)
    QK = dscr("QK", [128, 10, TA], BF16)
    VD = dscr("VD", [128, 18, 256], BF16)
    GCS = dscr("GCS", [128, 8, 18, 256], BF16)
    YT = dscr("YT", [128, 16, TA], BF16)
    ACTD = dscr("ACTD", [128, NFC, TA], BF16)
    HT = dscr("HT", [128, 16, TA], BF16)

    S = Sched(nc)
    lo = (int(nc.sbuf_base) + 63) // 64 * 64
    hi = int(nc.sbuf_top)
    AR = Arena(nc, lo, hi)
    ps = [nc.alloc_psum_tensor(f"psb{i}", [128, 512], F32) for i in range(8)]

    def P(i):
        return ("ps", i)

    def MM(out, lhsT, rhs, start, stop, r, w):
        return S.op("pe", lambda e: e.matmul(out, lhsT, rhs, start=start, stop=stop), r, w)

    def ACT(out, in_, func, r, w, relax=False, **kw):
        return S.op("act", lambda e: e.activation(out=out, in_=in_, func=func, **kw), r, w, relax=relax)

    def TS(out, in0, s1, s2, op0, op1, r, w):
        if op1 is None:
            return S.op("dve", lambda e: e.tensor_scalar(out=out, in0=in0, scalar1=s1, scalar2=None, op0=op0), r, w)
        return S.op("dve", lambda e: e.tensor_scalar(out=out, in0=in0, scalar1=s1, scalar2=s2, op0=op0, op1=op1), r, w)

    def STT(out, in0, scalar, in1, op0, op1, r, w, relax=False):
        return S.op("dve", lambda e: e.scalar_tensor_tensor(out=out, in0=in0, scalar=scalar, in1=in1, op0=op0, op1=op1), r, w, relax=relax)

    def TT(out, in0, in1, op, r, w):
        return S.op("dve", lambda e: e.tensor_tensor(out=out, in0=in0, in1=in1, op=op), r, w)

    def CP(out, in_, r, w):
        return S.op("dve", lambda e: e.tensor_copy(out=out, in_=in_), r, w)

    def RCP(out, in_, r, w):
        return S.op("dve", lambda e: e.reciprocal(out=out, in_=in_), r, w)

    def DMA(q, out, in_, r, w):
        return S.op(q, lambda e: e.dma_start(out=out, in_=in_), r, w)

    uid = [0]

    def tok(name):
        uid[0] += 1
        return (name, uid[0])

    ones_t = AR.alloc([128, 128], BF16, "ones")
    swp_t = AR.alloc([128, 128], BF16, "swp")
    cs_t = AR.alloc([128, 256], BF16, "cs")
    modt = [AR.alloc([128, 192], F32, "modt") for _ in range(DEPTH)]
    A1 = [AR.alloc([128, 32], F32, "A1") for _ in range(DEPTH)]
    G1 = [AR.alloc([128, 32], F32, "G1") for _ in range(DEPTH)]
    A2 = [AR.alloc([128, 32], F32, "A2") for _ in range(DEPTH)]
    G2 = [AR.alloc([128, 32], F32, "G2") for _ in range(DEPTH)]
    qkn_t = [AR.alloc([128, 2], F32, "qkn") for _ in range(DEPTH)]
    gy_t = [AR.alloc([128, 16], F32, "gy") for _ in range(DEPTH)]
    silu_t = AR.alloc([128, 32], BF16, "silu")
    bm_t = [AR.alloc([128, 192], F32, "bm") for _ in range(DEPTH)]
    g2_t = [[AR.alloc([128, 32], F32, "g2") for _ in range(4)] for _ in range(DEPTH)]
    rs_t = AR.alloc([128, 1024], F32, "rs")
    base = AR.mark()

    DMA("sp", ones_t[:, :], ones_d[:, :], [], ["ones"])
    DMA("sp", swp_t[:, :], swp_d[:, :], [], ["swp"])
    DMA("sp", cs_t[:, :], cs_d[:, :], [], ["cs"])
    for l in range(DEPTH):
        DMA("sp", qkn_t[l][:, :], qkn[l], [], [("qkn", l)])
        DMA("sp", gy_t[l][:, :], gy[l], [], [("gy", l)])

    def rstd(ps_ap, n, inv_count, r, off=0):
        ACT(rs_t[:, off:off + n], ps_ap, AF.Ln, list(r) + ["eps"], [("rs", off)], scale=inv_count, bias=eps_t[:, 0:1])
        ACT(rs_t[:, off:off + n], rs_t[:, off:off + n], AF.Exp, [("rs", off)], [("rs", off)], scale=-0.5)

    eps_t = AR.alloc([128, 1], F32, "eps")
    base = AR.mark()
    S.op("dve", lambda e: e.memset(eps_t[:, :], EPS), [], ["eps"])

    class ModBG:
        def __init__(self):
            self.wt = None
            self.cnt = 0
            self.todo = []

        def setup(self, wt):
            self.wt = wt

        def load(self, l, it_):
            nb_ = len(self.wt)
            w = self.wt[self.cnt % nb_]
            wtok = ("wm", id(w))
            self.cnt += 1
            DMA("pq", w[:, :, :], wmod[l, it_], [], [wtok])
            return (w, wtok, it_)

        def compute(self, pend):
            w, wtok, it_ = pend
            for m in range(2):
                j = it_ * 2 + m
                for k in range(16):
                    MM(ps[7][:, j * 2:j * 2 + 2], w[:, k, m * 128:(m + 1) * 128], silu_t[:, k * 2:k * 2 + 2],
                       k == 0, k == 15, [wtok, "silu"], [P(7)])

        def item(self, l, it_):
            self.compute(self.load(l, it_))

        def evac_A(self, l):
            TT(modt[l][:, 0:64], ps[7][:, 0:64], bm_t[l][:, 0:64], ALU.add, [P(7), ("bm", l)], [("modtA", l)])
            STT(A1[l][:, :], modt[l][:, 32:64], 1.0, g2_t[l][0][:, :], ALU.add, ALU.mult,
                [("modtA", l), ("g2", l, 0)], [("A1", l)])

        def evac_B(self, l):
            TT(modt[l][:, 64:192], ps[7][:, 64:192], bm_t[l][:, 64:192], ALU.add, [P(7), ("bm", l)], [("modtB", l)])
            TT(G1[l][:, :], modt[l][:, 64:96], g2_t[l][1][:, :], ALU.mult, [("modtB", l), ("g2", l, 1)], [("G1", l)])
            STT(A2[l][:, :], modt[l][:, 128:160], 1.0, g2_t[l][2][:, :], ALU.add, ALU.mult,
                [("modtB", l), ("g2", l, 2)], [("A2", l)])
            TT(G2[l][:, :], modt[l][:, 160:192], g2_t[l][3][:, :], ALU.mult, [("modtB", l), ("g2", l, 3)], [("G2", l)])

        def plan_background(self):
            self.pending = None
            for it_ in range(16, 48):
                self.todo.append(("item", 0, it_))
            self.todo.append(("fn", lambda: self.evac_B(0)))
            for l in range(1, nlayers):
                for it_ in range(48):
                    self.todo.append(("item", l, it_))
                self.todo.append(("fn", lambda l=l: (self.evac_A(l), self.evac_B(l))))

        def step(self, n=1):
            for _ in range(n):
                if self.pending is not None:
                    self.compute(self.pending)
                    self.pending = None
                if self.todo:
                    e = self.todo.pop(0)
                    if e[0] == "item":
                        self.pending = self.load(e[1], e[2])
                    else:
                        e[1]()

        def drain_pending(self):
            if self.pending is not None:
                self.compute(self.pending)
                self.pending = None

        def flush(self):
            while self.todo or self.pending is not None:
                self.step(1)

    modbg = ModBG()

    def mod_prologue(wt):
        cc_t = AR.alloc([128, 32], F32, "cc")
        DMA("sp", cc_t[:, :], cc[:, :], [], ["cc"])
        ACT(silu_t[:, :], cc_t[:, :], AF.Silu, ["cc"], ["silu"])
        for l in range(nlayers):
            DMA("sp", bm_t[l][:, :], bmod2[l], [], [("bm", l)])
            for i in range(4):
                DMA("sp", g2_t[l][i][:, :], g2[l, i], [], [("g2", l, i)])
        modbg.setup(wt)
        for it_ in range(16):
            modbg.item(0, it_)
        modbg.evac_A(0)
        modbg.plan_background()

    def modulate(l, src, Avec, sh_off, hT, blocks, xb, sq, tmp, after_block=None):
        atok = ("A1" if Avec is A1[l] else "A2", l)
        mtok = ("modtA" if sh_off == 0 else "modtB", l)
        for bi, (t0, n, s) in enumerate(blocks):
            x_ = xb[bi % len(xb)]
            xtok = ("xb", bi % len(xb))
            DMA("sp", x_[:, :, :n], src[:, :, t0:t0 + n], [("R", src.tensor.name)], [xtok])
            sq_ = sq[bi % len(sq)]
            sqtok = ("sq", bi % len(sq))
            ACT(sq_[:, :, :n], x_[:, :, :n], AF.Square, [xtok], [sqtok])
            for k in range(16):
                MM(ps[6][:, :n], ones_t[:, :], sq_[:, k, :n], k == 0, k == 15, [sqtok, "ones"], [P(6)])
            ro = (bi % 2) * 512
            rstd(ps[6][:, :n], n, 1.0 / D, [P(6)], off=ro)
            TT(x_[:, :, :n], x_[:, :, :n], rs_t[:, ro:ro + n].unsqueeze(1).broadcast_to([128, 16, n]), ALU.mult,
               [xtok, ("rs", ro)], [xtok])
            for k in range(16):
                ACT(hT[:, k, t0:t0 + n], x_[:, k, :n], AF.Identity, [xtok, atok, mtok], [("hT", t0 // 256)], relax=True,
                    scale=Avec[:, k * 2 + s:k * 2 + s + 1], bias=modt[l][:, sh_off + k * 2 + s:sh_off + k * 2 + s + 1])
            if after_block is not None:
                after_block()

    def hT_tokens(t0, n):
        return [("hT", b) for b in range(t0 // 256, (t0 + n + 255) // 256)]

    def mod_stats_and_apply(lm, Avec, atok, sh_off, x3, xtok, s, t0, n, sqm, hst, bank, ro):
        mtok = ("modtA" if sh_off == 0 else "modtB", lm)
        for k in range(16):
            MM(ps[bank][:, :n], ones_t[:, :], sqm[:, k, :n], k == 0, k == 15, ["sqm", "ones"], [P(bank)])
        rstd(ps[bank][:, :n], n, 1.0 / D, [P(bank)], off=ro)
        TT(x3, x3, rs_t[:, ro:ro + n].unsqueeze(1).broadcast_to([128, 16, n]), ALU.mult, [xtok, ("rs", ro)], [xtok])
        for k in range(16):
            ACT(hst[:, k, :n], x3[:, k, :], AF.Identity, [xtok, atok, mtok], ["hst"], relax=True,
                scale=Avec[:, k * 2 + s:k * 2 + s + 1], bias=modt[lm][:, sh_off + k * 2 + s:sh_off + k * 2 + s + 1])
        DMA("sp", HT[:, :, t0:t0 + n], hst[:, :, :n], ["hst"], ["HT"])

    def load_hT(hT, ntok):
        for c0 in range(0, ntok, 768):
            c1 = min(c0 + 768, ntok)
            DMA("sp", hT[:, :, c0:c1], HT[:, :, c0:c1], ["HT"], [("hT", b) for b in range(c0 // 256, c1 // 256)])

    def phase_A(l, src):
        last = (l == DEPTH - 1)
        AR.reset(base)
        hT = AR.alloc([128, 16, TA], BF16, "hT")
        xb = [AR.alloc([128, 16, 256], F32, "xb") for _ in range(2)]
        sq = [AR.alloc([128, 16, 256], BF16, "sq")]
        if l == 0:
            wm_t = [AR.alloc([128, 16, 256], BF16, "wm") for _ in range(2)]
        ropec = AR.alloc([128, T], F32, "ropec")
        ropes = AR.alloc([128, T], F32, "ropes")
        wb = [AR.alloc([128, 16, 128], BF16, "wqkf") for _ in range(3)]
        wv_t = AR.alloc([128, 16, 256], BF16, "wv")
        sqh = [AR.alloc([128, 512], BF16, "sqh") for _ in range(2)]
        qn = [AR.alloc([128, 512], BF16, "qn") for _ in range(2)]
        t1 = [AR.alloc([128, 512], F32, "t1") for _ in range(2)]
        t2 = [AR.alloc([128, 512], F32, "t2") for _ in range(2)]
        qo = [AR.alloc([128, 512], BF16, "qo") for _ in range(2)]
        fT = [AR.alloc([128, 512], BF16, "fT") for _ in range(2)]
        gst = [AR.alloc([128, 4, 256], BF16, "gst") for _ in range(2)]
        vst = AR.alloc([128, 18, 256], BF16, "vst")

        DMA("sp", ropec[:, :], ropec_d[:, :], [], ["ropec"])
        DMA("sp", ropes[:, :], ropes_d[:, :], [], ["ropes"])
        blocks = [(i * 256, 256, 0) for i in range(8)] + [(T, 256, 1)]
        keep = 41 if nlayers > 1 else 0
        if l == 0:
            mod_prologue(wm_t)
            modulate(l, src, A1[l], 0, hT, blocks, xb, sq, None, after_block=lambda: modbg.step(2 if len(modbg.todo) > keep + 2 else 0))
        elif NEXTMOD:
            load_hT(hT, TA)
        else:
            modulate(l, src, A1[l], 0, hT, blocks, xb, sq, None)
        def bgstep():
            if l == 0 and len(modbg.todo) > keep:
                modbg.step(1)

        chunks = [(0, 512), (512, 512), (1024, 512), (1536, 512), (T, 256)]
        it = 0
        DMA("pq", wv_t[:, :, :], wv[l], [], ["wv"])
        pipe = Pipe()

        def qk_stage2(i, pa, cb, kind, t0, n, is_ctx):
            pb = 3 + i % 2
            ro = (i % 2) * 512
            gcol = 0 if kind == "q" else 1
            b2 = i % 2
            MM(ps[pb][:, :n], ones_t[:, :], sqh[b2][:, :n], True, True, [("sqh", b2), "ones"], [P(pb)])
            rstd(ps[pb][:, :n], n, 1.0 / 128, [P(pb)], off=ro)
            dst = qn[b2] if not is_ctx else qo[b2]
            dtok = ("qn", b2) if not is_ctx else ("qo", b2)
            STT(dst[:, :n], ps[pa][:, :n], qkn_t[l][:, gcol:gcol + 1], rs_t[:, ro:ro + n], ALU.mult, ALU.mult,
                [P(pa), ("qkn", l), ("rs", ro)], [dtok])
            if is_ctx:
                DMA("sp", QK[:, cb, t0:t0 + n], qo[b2][:, :n], [("qo", b2)], [("QK", cb)])
            else:
                pipe.defer(1, lambda: qk_stage3(i, cb, t0, n))

        def qk_stage3(i, cb, t0, n):
            b2 = i % 2
            pc = 5 + i % 2
            MM(ps[pc][:, :n], swp_t[:, :], qn[b2][:, :n], True, True, [("qn", b2), "swp"], [P(pc)])
            TT(t1[b2][:, :n], qn[b2][:, :n], ropec[:, t0:t0 + n], ALU.mult, [("qn", b2), "ropec"], [("t1", b2)])
            TT(t2[b2][:, :n], ps[pc][:, :n], ropes[:, t0:t0 + n], ALU.mult, [P(pc), "ropes"], [("t2", b2)])
            TT(qo[b2][:, :n], t1[b2][:, :n], t2[b2][:, :n], ALU.add, [("t1", b2), ("t2", b2)], [("qo", b2)])
            DMA("sp", QK[:, cb, t0:t0 + n], qo[b2][:, :n], [("qo", b2)], [("QK", cb)])

        def f_stage2(i, g, t0, n):
            b2 = i % 2
            nt = n // 128
            for j in range(nt):
                pb = (3 + i % 2) if j < 2 else (5 + i % 2)
                jj = j % 2
                MM(ps[pb][:, jj * 256:(jj + 1) * 256], fT[b2][:, j * 128:(j + 1) * 128], cs_t[:, :], True, True,
                   [("fT", b2), "cs"], [P(pb)])
            CP(gst[b2][:, 0:2, :], ps[3 + i % 2][:, :].rearrange("p (a b) -> p a b", a=2), [P(3 + i % 2)], [("gst", b2)])
            if nt > 2:
                CP(gst[b2][:, 2:4, :], ps[5 + i % 2][:, :].rearrange("p (a b) -> p a b", a=2), [P(5 + i % 2)], [("gst", b2)])
            DMA("sp", GCS[:, g, t0 // 128:t0 // 128 + nt, :], gst[b2][:, 0:nt, :], [("gst", b2)], [("GCS", g)])

        for cb in range(18):
            w = wb[cb % 3]
            wtok = ("wqkf", cb % 3)
            DMA("pq", w[:, :, :], wqkf[l, cb], [], [wtok])
            kind = "q" if cb < 8 else ("k" if cb < 10 else "f")
            for (t0, n) in chunks:
                is_ctx = t0 >= T
                if is_ctx and last and kind != "k":
                    continue
                i = it
                pa = it % 3
                it += 1
                for k in range(16):
                    MM(ps[pa][:, :n], w[:, k, :], hT[:, k, t0:t0 + n], k == 0, k == 15,
                       [wtok] + hT_tokens(t0, n), [P(pa)])
                pipe.tick()
                bgstep()
                if kind in ("q", "k"):
                    ACT(sqh[i % 2][:, :n], ps[pa][:, :n], AF.Square, [P(pa)], [("sqh", i % 2)])
                    pipe.defer(1, lambda i=i, pa=pa, cb=cb, kind=kind, t0=t0, n=n, is_ctx=is_ctx:
                               qk_stage2(i, pa, cb, kind, t0, n, is_ctx))
                else:
                    ACT(fT[i % 2][:, :n], ps[pa][:, :n], AF.Copy, [P(pa)], [("fT", i % 2)])
                    pipe.defer(1, lambda i=i, g=cb - 10, t0=t0, n=n: f_stage2(i, g, t0, n))
        pipe.flush()
        if l == 0:
            while len(modbg.todo) > keep:
                modbg.step(1)
            modbg.drain_pending()
        for tt in range(18):
            pa = tt % 2
            for k in range(16):
                MM(ps[pa][:, 0:256], hT[:, k, tt * 128:(tt + 1) * 128], wv_t[:, k, :], k == 0, k == 15,
                   ["wv"] + hT_tokens(tt * 128, 128), [P(pa)])
            ACT(vst[:, tt, :], ps[pa][:, 0:256], AF.Copy, [P(pa)], ["vst"], relax=True)
        DMA("sp", VD[:, :, :], vst[:, :, :], ["vst"], ["VD"])
        S.barrier()

    def phase_F(l):
        last = (l == DEPTH - 1)
        AR.reset(base)
        gcs = AR.alloc([128, 8, 18, 256], BF16, "gcs")
        tab = [AR.alloc([128, 2, 16, 512], BF16, "tab") for _ in range(2)]
        tabc = AR.alloc([128, 2, 2, 256], BF16, "tabc")
        w4_t = AR.alloc([128, 8, 128], BF16, "w4")
        fo = [AR.alloc([128, 8, 512], F32, "fo") for _ in range(2)]
        yst = AR.alloc([128, 8, 512], BF16, "yst")
        FTb = [AR.alloc([128, 512], BF16, "FT") for _ in range(2)]
        sqf = [AR.alloc([128, 512], BF16, "sqf") for _ in range(2)]
        DMA("pq", w4_t[:, :, :], w4[l], [], ["w4"])
        for g in range(8):
            DMA("sp", gcs[:, g, :, :], GCS[:, g, :, :], [("GCS", g)], [("gcs", g)])
        DMA("sp", tabc[:, 0, :, :], dftcc_d[:, :, :], [], ["tabc"])
        DMA("sp", tabc[:, 1, :, :], dftsc_d[:, :, :], [], ["tabc"])
        passes = [(kc * 512, 512, 16, 0) for kc in range(4)]
        if not last:
            passes.append((T, 256, 2, 16))
        def load_tab(pi):
            k0_, n_ = passes[pi][0], passes[pi][1]
            if k0_ >= T:
                return
            DMA("sp", tab[pi % 2][:, 0, :, :], dftc_d[:, :, k0_:k0_ + n_], [], [("tab", pi % 2)])
            DMA("sp", tab[pi % 2][:, 1, :, :], dfts_d[:, :, k0_:k0_ + n_], [], [("tab", pi % 2)])

        pipeF = Pipe()
        if l == 0 and modbg.todo:
            modbg.setup([AR.alloc([128, 16, 256], BF16, "wmF")])

        def f_stage2(g, pa, n, fo_, pi):
            MM(ps[2 + pa][:, :n], w4_t[:, g, :], FTb[pa][:, :n], True, True, [("FT", pa), "w4"], [P(2 + pa)])
            ACT(sqf[pa][:, :n], ps[2 + pa][:, :n], AF.Square, [P(2 + pa)], [("sqf", pa)])
            ACT(fo_[:, g, :n], ps[2 + pa][:, :n], AF.Copy, [P(2 + pa)], [("fo", pi % 2)])
            pipeF.defer(1, lambda: MM(ps[4 + pi % 2][:, :n], ones_t[:, :], sqf[pa][:, :n], g == 0, g == 7,
                                      [("sqf", pa), "ones"], [P(4 + pi % 2)]))

        load_tab(0)
        for pi, (k0, n, ntt, tt0) in enumerate(passes):
            is_ctx = k0 >= T
            if pi + 1 < len(passes):
                load_tab(pi + 1)
            tb = tab[pi % 2]
            tbtok = ("tab", pi % 2)
            fo_ = fo[pi % 2]
            for g in range(8):
                pa = g % 2
                for ti in range(ntt):
                    if is_ctx:
                        rc, rsn, rtok = tabc[:, 0, ti, :], tabc[:, 1, ti, :], "tabc"
                    else:
                        rc, rsn, rtok = tb[:, 0, ti, :], tb[:, 1, ti, :], tbtok
                    MM(ps[pa][:, :n], gcs[:, g, tt0 + ti, 0:128], rc, ti == 0, False, [("gcs", g), rtok], [P(pa)])
                    MM(ps[pa][:, :n], gcs[:, g, tt0 + ti, 128:256], rsn, False, ti == ntt - 1, [("gcs", g), rtok], [P(pa)])
                pipeF.tick()
                if l == 0:
                    modbg.step(1)
                ACT(FTb[pa][:, :n], ps[pa][:, :n], AF.Copy, [P(pa)], [("FT", pa)])
                pipeF.defer(1, lambda g=g, pa=pa, n=n, fo_=fo_, pi=pi: f_stage2(g, pa, n, fo_, pi))
            pipeF.flush()
            rstd(ps[4 + pi % 2][:, :n], n, 1.0 / 1024, [P(4 + pi % 2)])
            for g in range(8):
                STT(yst[:, g, :n], fo_[:, g, :n], gy_t[l][:, 8 + g:9 + g], rs_t[:, :n], ALU.mult, ALU.mult,
                    [("fo", pi % 2), ("gy", l), ("rs", 0)], ["yst"], relax=True)
            DMA("sp", YT[:, 8:16, k0:k0 + n], yst[:, :, :n], ["yst"], [("YT", "f")])
        if l == 0:
            modbg.flush()
        S.barrier()

    def phase_T(l):
        last = (l == DEPTH - 1)
        AR.reset(base)
        kT = AR.alloc([128, 2, TA], BF16, "kT")
        vt = AR.alloc([128, 18, 256], BF16, "vt")
        qT = AR.alloc([128, 8, TA], BF16, "qT")
        pT = [[AR.alloc([128, 512], BF16, "pT") for _ in range(2)] for _ in range(2)]
        ao = [AR.alloc([128, 8, 512], F32, "ao") for _ in range(2)]
        sqa = AR.alloc([128, 8, 512], BF16, "sqa")
        rd = [AR.alloc([128, 512], F32, "rd") for _ in range(2)]
        yst = AR.alloc([128, 8, 512], BF16, "ysta")
        DMA("sp", kT[:, :, :], QK[:, 8:10, :], [("QK", 8), ("QK", 9)], ["kT"])
        DMA("sp", vt[:, :, :], VD[:, :, :], ["VD"], ["vt"])
        for h in range(8):
            DMA("sp", qT[:, h, :], QK[:, h, :], [("QK", h)], [("qT", h)])
        scale = float(1.0 / np.sqrt(128.0))
        qchunks = [(i * 512, 512, list(range(18))) for i in range(4)]
        if not last:
            qchunks.append((T, 256, [16, 17]))
        for qi, (t0, n, kts) in enumerate(qchunks):
            ao_ = ao[qi % 2]
            nk = len(kts)
            for hp in range(4):
                heads = (2 * hp, 2 * hp + 1)
                kvh = heads[0] // 4

                def s_step(w, i):
                    h = heads[w]
                    st_ = kts[i]
                    bank = 2 * w + i % 2
                    MM(ps[bank][:, :n], kT[:, kvh, st_ * 128:(st_ + 1) * 128], qT[:, h, t0:t0 + n], True, True,
                       ["kT", ("qT", h)], [P(bank)])
                    ACT(pT[w][i % 2][:, :n], ps[bank][:, :n], AF.Exp, [P(bank)], [("pT", w, i % 2)], scale=scale)

                def pv_step(w, i):
                    st_ = kts[i]
                    po, pd = 4 + 2 * w, 5 + 2 * w
                    MM(ps[po][:, :n], vt[:, st_, kvh * 128:(kvh + 1) * 128], pT[w][i % 2][:, :n], i == 0, i == nk - 1,
                       ["vt", ("pT", w, i % 2)], [P(po)])
                    MM(ps[pd][:, :n], ones_t[:, :], pT[w][i % 2][:, :n], i == 0, i == nk - 1,
                       ["ones", ("pT", w, i % 2)], [P(pd)])

                s_step(0, 0)
                s_step(1, 0)
                for i in range(nk):
                    if i + 1 < nk:
                        s_step(0, i + 1)
                        s_step(1, i + 1)
                    pv_step(0, i)
                    pv_step(1, i)
                for w in range(2):
                    h = heads[w]
                    po, pd = 4 + 2 * w, 5 + 2 * w
                    ACT(rd[w][:, :n], ps[pd][:, :n], AF.Ln, [P(pd)], [("rd", w)])
                    ACT(rd[w][:, :n], rd[w][:, :n], AF.Exp, [("rd", w)], [("rd", w)], scale=-1.0)
                    TT(ao_[:, h, :n], ps[po][:, :n], rd[w][:, :n], ALU.mult, [P(po), ("rd", w)], [("ao", qi % 2, h)])
                    ACT(sqa[:, h, :n], ao_[:, h, :n], AF.Square, [("ao", qi % 2, h)], [("sqa", h)])
            for h in range(8):
                MM(ps[0][:, :n], ones_t[:, :], sqa[:, h, :n], h == 0, h == 7, [("sqa", h), "ones"], [P(0)])
            rstd(ps[0][:, :n], n, 1.0 / 1024, [P(0)])
            for h in range(8):
                STT(yst[:, h, :n], ao_[:, h, :n], gy_t[l][:, h:h + 1], rs_t[:, :n], ALU.mult, ALU.mult,
                    [("ao", qi % 2, h), ("gy", l), ("rs", 0)], ["ysta"], relax=True)
            DMA("sp", YT[:, 0:8, t0:t0 + n], yst[:, :, :n], ["ysta"], [("YT", "a")])
        S.barrier()

    def phase_O(l, src, dst):
        last = (l == DEPTH - 1)
        AR.reset(base)
        wo = AR.alloc([128, 16, D], BF16, "wo")
        yb = [AR.alloc([128, 16, 256], BF16, "yb") for _ in range(3)]
        xb = [AR.alloc([128, 16, 256], F32, "xbo") for _ in range(3)]
        mixs2 = [AR.alloc([128, 16, 256], F32, "mixs") for _ in range(2)]
        sqb = [AR.alloc([128, 256], BF16, "sqb") for _ in range(2)]
        sqm = AR.alloc([128, 16, 256], BF16, "sqm")
        hst = AR.alloc([128, 16, 256], BF16, "hst")
        for k in range(16):
            DMA("pq", wo[:, k, :], wout[l, :, k, :], [], [("wo", k)])
        wotoks = [("wo", k) for k in range(16)]
        nb = 8 if last else 9
        def load_blk(b):
            DMA("sp", yb[b % 3][:, :, :], YT[:, :, b * 256:b * 256 + 256], [("YT", "a"), ("YT", "f")], [("yb", b % 3)])
            DMA("sp", xb[b % 3][:, :, :], src[:, :, b * 256:b * 256 + 256], [("R", src.tensor.name)], [("xbo", b % 3)])

        pipeO = Pipe()

        def tailO(n, pss, mixs, mb, x_, b, t0):
            ro = (b % 2) * 512
            rstd(ps[pss][:, :n], n, 1.0 / D, [P(pss)], off=ro)
            for dc in range(16):
                TT(mixs[:, dc, :], mixs[:, dc, :], rs_t[:, ro:ro + n], ALU.mult, [("mixs", mb, dc), ("rs", ro)], [("mixs", mb, dc)])
            for dc in range(16):
                TT(x_[:, dc, :], mixs[:, dc, :], x_[:, dc, :], ALU.add, [("mixs", mb, dc), ("xbo", b % 3)], [("xbo", b % 3)])
            DMA("sp", dst[:, :, t0:t0 + n], x_[:, :, :], [("xbo", b % 3)], [("R", dst.tensor.name)])
            x3 = x_[:, :, :n]
            xtok = ("xbo", b % 3)
            S.op("pool", lambda e: e.tensor_tensor(out=sqm[:, :, :n], in0=x3, in1=x3, op=ALU.mult), [xtok], ["sqm"])
            s_ = 1 if t0 >= T else 0
            ro2 = 256 + (b % 2) * 512
            mtok = ("modtB", l)

            def p_stats():
                for k in range(16):
                    MM(ps[6][:, :n], ones_t[:, :], sqm[:, k, :n], k == 0, k == 15, ["sqm", "ones"], [P(6)])

            def p_rstd():
                rstd(ps[6][:, :n], n, 1.0 / D, [P(6)], off=ro2)
                TT(x3, x3, rs_t[:, ro2:ro2 + n].unsqueeze(1).broadcast_to([128, 16, n]), ALU.mult, [xtok, ("rs", ro2)], [xtok])

            def p_apply(k0):
                for k in range(k0, k0 + 4):
                    ACT(hst[:, k, :n], x_[:, k, :n], AF.Identity, [xtok, ("A2", l), mtok], ["hst"], relax=True,
                        scale=A2[l][:, k * 2 + s_:k * 2 + s_ + 1], bias=modt[l][:, 96 + k * 2 + s_:96 + k * 2 + s_ + 1])

            def p_store():
                DMA("sp", HT[:, :, t0:t0 + n], hst[:, :, :n], ["hst"], ["HT"])
                if b + 3 < nb:
                    load_blk(b + 3)

            pipeO.defer(7, p_stats)
            pipeO.defer(8, p_rstd)
            for q_ in range(4):
                pipeO.defer(10 + q_, lambda k0=4 * q_: p_apply(k0))
            pipeO.defer(14, p_store)

        load_blk(0)
        if nb > 1:
            load_blk(1)
        if nb > 2:
            load_blk(2)
        for b in range(nb):
            t0 = b * 256
            n = 256
            pss = 4 + b % 2
            s = 1 if t0 >= T else 0
            y_ = yb[b % 3]
            x_ = xb[b % 3]
            mixs = mixs2[b % 2]
            mb = b % 2
            for dc in range(16):
                pa = dc % 4
                for k in range(16):
                    MM(ps[pa][:, :n], wo[:, k, dc * 128:(dc + 1) * 128], y_[:, k, :], k == 0, k == 15,
                       [("wo", k), ("yb", b % 3)], [P(pa)])
                pipeO.tick()
                ACT(sqb[dc % 2][:, :n], ps[pa][:, :n], AF.Square, [P(pa)], [("sqb", dc % 2)])
                ACT(mixs[:, dc, :], ps[pa][:, :n], AF.Identity, [P(pa), ("G1", l)], [("mixs", mb, dc)],
                    scale=G1[l][:, dc * 2 + s:dc * 2 + s + 1])
                pipeO.defer(1, lambda dc=dc, n=n, pss=pss: MM(ps[pss][:, :n], ones_t[:, :], sqb[dc % 2][:, :n], dc == 0, dc == 15,
                                                     [("sqb", dc % 2), "ones"], [P(pss)]))
            pipeO.defer(1, lambda n=n, pss=pss, mixs=mixs, mb=mb, x_=x_, b=b, t0=t0: tailO(n, pss, mixs, mb, x_, b, t0))
        pipeO.flush()
        S.barrier()

    def phase_U(l, src):
        last = (l == DEPTH - 1)
        AR.reset(base)
        hT = AR.alloc([128, 16, TA], BF16, "hTu")
        xb = [AR.alloc([128, 16, 256], F32, "xbu") for _ in range(2)]
        sq = [AR.alloc([128, 16, 256], BF16, "squ") for _ in range(2)]
        tmp = [AR.alloc([128, 256], F32, "tmpu") for _ in range(2)]
        cw_t = AR.alloc([128, NFC * 3], F32, "cw")
        cb_t = AR.alloc([128, NFC], F32, "cbt")
        wb = [AR.alloc([128, 2, 16, 128], BF16, "wup") for _ in range(2)]
        NT = T if last else TA
        gbuf = [AR.alloc([128, TA], F32, "gbuf") for _ in range(2)]
        vbuf = [AR.alloc([128, TA], BF16, "vbuf") for _ in range(2)]
        cbuf = [AR.alloc([128, TA], F32, "cbuf") for _ in range(2)]
        ast = [AR.alloc([128, TA], BF16, "ast") for _ in range(2)]
        DMA("sp", cw_t[:, :], cw[l], [], ["cw"])
        DMA("sp", cb_t[:, :], cbias[l], [], ["cbt"])
        blocks = [(i * 256, 256, 0) for i in range(8)]
        if not last:
            blocks.append((T, 256, 1))
        load_hT(hT, T if last else TA)
        chunks = [(0, 512), (512, 512), (1024, 512), (1536, 512)]
        segs = [(0, T)]
        if not last:
            chunks.append((T, 256))
            segs.append((T, TA))
        it = 0
        for fc in range(NFC):
            w = wb[fc % 2]
            wtok = ("wup", fc % 2)
            DMA("pq", w[:, :, :, :], wup[l, fc], [], [wtok])
            gb, vb, cbf, as_ = gbuf[fc % 2], vbuf[fc % 2], cbuf[fc % 2], ast[fc % 2]
            gt, vtk, ct, at = ("gbuf", fc % 2), ("vbuf", fc % 2), ("cbuf", fc % 2), ("ast", fc % 2)
            for (t0, n) in chunks:
                pa = it % 4
                pb = 4 + it % 4
                it += 1
                for k in range(16):
                    MM(ps[pa][:, :n], w[:, 0, k, :], hT[:, k, t0:t0 + n], k == 0, k == 15, [wtok] + hT_tokens(t0, n), [P(pa)])
                ACT(gb[:, t0:t0 + n], ps[pa][:, :n], AF.Copy, [P(pa)], [gt])
                for k in range(16):
                    MM(ps[pb][:, :n], w[:, 1, k, :], hT[:, k, t0:t0 + n], k == 0, k == 15, [wtok] + hT_tokens(t0, n), [P(pb)])
                CP(vb[:, t0:t0 + n], ps[pb][:, :n], [P(pb)], [vtk])
            c0 = cw_t[:, fc * 3 + 0:fc * 3 + 1]
            c1 = cw_t[:, fc * 3 + 1:fc * 3 + 2]
            c2 = cw_t[:, fc * 3 + 2:fc * 3 + 3]
            for (a, b) in segs:
                ACT(cbf[:, a:b], gb[:, a:b], AF.Identity, [gt, "cw", "cbt"], [ct], scale=c1, bias=cb_t[:, fc:fc + 1])
                STT(cbf[:, a + 1:b], gb[:, a:b - 1], c0, cbf[:, a + 1:b], ALU.mult, ALU.add, [gt, ct, "cw"], [ct])
                STT(cbf[:, a:b - 1], gb[:, a + 1:b], c2, cbf[:, a:b - 1], ALU.mult, ALU.add, [gt, ct, "cw"], [ct])
            if gelu_mode == "tanh_act":
                ACT(cbf[:, :NT], cbf[:, :NT], AF.Gelu_apprx_tanh, [ct], [ct])
            else:
                gsc = gb
                TT(gsc[:, :NT], cbf[:, :NT], cbf[:, :NT], ALU.mult, [ct], [gt])
                TS(gsc[:, :NT], gsc[:, :NT], 0.044715, 1.0, ALU.mult, ALU.add, [gt], [gt])
                TT(gsc[:, :NT], gsc[:, :NT], cbf[:, :NT], ALU.mult, [gt, ct], [gt])
                ACT(gsc[:, :NT], gsc[:, :NT], AF.Sigmoid, [gt], [gt], scale=float(2.0 * np.sqrt(2.0 / np.pi)))
                TT(cbf[:, :NT], cbf[:, :NT], gsc[:, :NT], ALU.mult, [gt, ct], [ct])
            TT(as_[:, :NT], cbf[:, :NT], vb[:, :NT], ALU.mult, [ct, vtk], [at])
            DMA("sp", ACTD[:, fc, :NT], as_[:, :NT], [at], [("ACTD", fc)])
        S.barrier()

    def phase_D(l, src, dst, dst_is_out):
        last = (l == DEPTH - 1)
        AR.reset(base)
        ab = AR.alloc([128, NFC, 768], BF16, "ab")
        ys = AR.alloc([128, 16, 768], F32, "ys")
        wb = [AR.alloc([128, NFC, 128], BF16, "wdn") for _ in range(2)]
        xall = AR.alloc([128, 16, 768], F32, "xall")
        sqd = [AR.alloc([128, 384], BF16, "sqd") for _ in range(2)]
        if NEXTMOD and l + 1 < nlayers:
            sqm = AR.alloc([128, 16, 128], BF16, "sqmd")
            hst = AR.alloc([128, 16, 128], BF16, "hstd")
        print("D arena slack", AR.hi - AR.off)
        blocks = [(0, 768), (768, 768), (1536, 512 if last else 768)]
        wi = 0

        def load_ab(bi):
            t0_, n_ = blocks[bi]
            for f0 in range(0, NFC, 11):
                DMA("sp", ab[:, f0:f0 + 11, :n_], ACTD[:, f0:f0 + 11, t0_:t0_ + n_],
                    [("ACTD", f) for f in range(f0, f0 + 11)], [("ab", f0)])

        def load_x(bi):
            t0_, n_ = blocks[bi]
            DMA("sp", xall[:, :, :n_], src[:, :, t0_:t0_ + n_], [("R", src.tensor.name)], ["xall"])

        pipeD = Pipe()
        load_ab(0)
        load_x(0)
        for bi, (t0, n) in enumerate(blocks):
            cn = n // 2
            it = 0
            rngs = []
            if t0 + n <= T:
                rngs.append((0, n, 0))
            else:
                if t0 < T:
                    rngs.append((0, T - t0, 0))
                rngs.append((max(T - t0, 0), n, 1))
            for dc in range(16):
                w = wb[wi % 2]
                wtok = ("wdn", wi % 2)
                wi += 1
                DMA("pq", w[:, :, :], wdn[l, dc], [], [wtok])
                for ci in range(2):
                    c0 = ci * cn
                    pa = it % 4
                    it += 1
                    for fc in range(NFC):
                        MM(ps[pa][:, :cn], w[:, fc, :], ab[:, fc, c0:c0 + cn], fc == 0, fc == NFC - 1,
                           [wtok, ("ab", (fc // 11) * 11)], [P(pa)])
                    pipeD.tick()
                    ACT(sqd[pa % 2][:, :cn], ps[pa][:, :cn], AF.Square, [P(pa)], [("sqd", pa % 2)])
                    for (u0, u1, s_) in rngs:
                        a0, a1 = max(u0, c0), min(u1, c0 + cn)
                        if a1 > a0:
                            ACT(ys[:, dc, a0:a1], ps[pa][:, a0 - c0:a1 - c0], AF.Identity, [P(pa), ("G2", l)], [("ys", dc)],
                                scale=G2[l][:, dc * 2 + s_:dc * 2 + s_ + 1])
                    pipeD.defer(1, lambda ci=ci, cn=cn, pa=pa, dc=dc: MM(ps[4 + ci][:, :cn], ones_t[:, :], sqd[pa % 2][:, :cn],
                                                               dc == 0, dc == 15, [("sqd", pa % 2), "ones"], [P(4 + ci)]))
            pipeD.flush()
            if bi + 1 < len(blocks):
                load_ab(bi + 1)
            for ci in range(2):
                rstd(ps[4 + ci][:, :cn], cn, 1.0 / D, [P(4 + ci)], off=ci * cn)
            rstoks = [("rs", 0), ("rs", cn)]
            for dc in range(16):
                TT(ys[:, dc, :n], ys[:, dc, :n], rs_t[:, :n], ALU.mult, [("ys", dc)] + rstoks, [("ys", dc)])
            for dc in range(16):
                TT(xall[:, dc, :n], ys[:, dc, :n], xall[:, dc, :n], ALU.add, [("ys", dc), "xall"], ["xall"])
            o = DMA("sp", dst[:, :, t0:t0 + n], xall[:, :, :n], ["xall"], [("R", dst.tensor.name)])
            if dst_is_out:
                finals.append(o)
            if NEXTMOD and l + 1 < nlayers:
                nsub = n // 128

                def submod(j, bi=bi, t0=t0, nsub=nsub):
                    u0 = j * 128
                    tg = t0 + u0
                    s_ = 1 if tg >= T else 0
                    ACT(sqm[:, :, :], xall[:, :, u0:u0 + 128], AF.Square, ["xall"], ["sqm"])

                    def sub2():
                        mod_stats_and_apply(l + 1, A1[l + 1], ("A1", l + 1), 0, xall[:, :, u0:u0 + 128], "xall",
                                            s_, tg, 128, sqm, hst, 6, 768)
                        if j + 1 < nsub:
                            submod(j + 1)
                        elif bi + 1 < len(blocks):
                            load_x(bi + 1)
                    pipeD.defer(1, sub2)
                pipeD.defer(4, lambda: submod(0))
            elif bi + 1 < len(blocks):
                load_x(bi + 1)
        pipeD.flush()
        S.barrier()

    finals = []

    def run_all():
        src = xt
        for l in range(nlayers):
            is_out = (l == DEPTH - 1)
            for nm, fn in (("A", lambda: phase_A(l, src)), ("F", lambda: phase_F(l)), ("T", lambda: phase_T(l)),
                           ("O", lambda: phase_O(l, src, R1)), ("U", lambda: phase_U(l, R1)),
                           ("D", lambda: phase_D(l, R1, out_d if is_out else R2, is_out))):
                fn()
                if stop in (nm, nm + str(l)) and (len(stop) == 2 or l == 0):
                    return
            src = R2

    run_all()
    if debug:
        dbgm = nc.dram_tensor("dbgm", [128, 192 + 4 * 32], F32, kind="ExternalOutput").ap()
        o = DMA("sp", dbgm[:, 0:192], modt[0][:, :], [("modtA", 0), ("modtB", 0)], ["dbgm"])
        finals.append(o)
        for i, tl in enumerate((A1, G1, A2, G2)):
            nm = ("A1", "G1", "A2", "G2")[i]
            finals.append(DMA("sp", dbgm[:, 192 + i * 32:192 + (i + 1) * 32], tl[0][:, :], [(nm, 0)], ["dbgm"]))
    S.emit(finals)
    return nc


def _bf16(a):
    return np.ascontiguousarray(a.astype(np.float32)).astype(ml_dtypes.bfloat16)


def _consts():
    c = {}
    c["ones"] = _bf16(np.ones((128, 128)))
    sw = np.zeros((128, 128), np.float32)
    for i in range(128):
        sw[i ^ 1, i] = 1.0
    c["swp"] = _bf16(sw)
    cidx = np.arange(128)[:, None].astype(np.float64)
    m = np.arange(128)[None, :].astype(np.float64)
    ang = 2 * np.pi * cidx * m / 128.0
    c["cs"] = _bf16(np.concatenate([np.cos(ang), -np.sin(ang)], axis=1) / np.sqrt(128.0))
    pos = np.arange(T)
    row = (pos // 64).astype(np.float32)
    col = (pos % 64).astype(np.float32)
    freqs = (np.float32(10000.0) ** (-np.arange(32, dtype=np.float32) / np.float32(32))).astype(np.float32)
    angp = np.concatenate([row[:, None] * freqs, col[:, None] * freqs], axis=-1).astype(np.float32)
    cosv = np.cos(angp).astype(np.float32)
    sinv = np.sin(angp).astype(np.float32)
    rc = np.zeros((128, T), np.float32)
    rsn = np.zeros((128, T), np.float32)
    for j in range(64):
        rc[2 * j] = cosv[:, j]
        rc[2 * j + 1] = cosv[:, j]
        rsn[2 * j] = -sinv[:, j]
        rsn[2 * j + 1] = sinv[:, j]
    c["ropec"] = rc
    c["ropes"] = rsn

    def dft(n):
        t = np.arange(n, dtype=np.int64)
        kt = (t[:, None] * t[None, :]) % n
        a = 2 * np.pi * kt.astype(np.float64) / n
        cm = np.cos(a) / np.sqrt(n)
        sm = np.sin(a) / np.sqrt(n)
        nt = n // 128
        cm = cm.reshape(nt, 128, n).transpose(1, 0, 2)
        sm = sm.reshape(nt, 128, n).transpose(1, 0, 2)
        return _bf16(cm), _bf16(sm)

    c["dftc"], c["dfts"] = dft(T)
    c["dftcc"], c["dftsc"] = dft(CTX)
    return c


def _fm(v):
    sh = v.shape
    n = sh[-1] // 128
    return np.ascontiguousarray(np.swapaxes(v.reshape(sh[:-1] + (n, 128)), -1, -2))


def _prep_shared(c_ctx, w_mod, b_mod, g_pre_mix, g_post_mix, g_pre_ffn, g_post_ffn, w_in, q_norm, k_norm,
                 w_four, g_attn_out, g_four_out, w_out, w_up, conv_w, conv_b, w_down):
    f = np.float32
    sh = {}
    sh["wmod"] = np.ascontiguousarray(w_mod.reshape(DEPTH, 16, 128, 48, 256).transpose(0, 3, 2, 1, 4))
    bm = _fm(b_mod)
    sh["bmod2"] = np.ascontiguousarray(np.repeat(bm, 2, axis=-1))
    gs = np.stack([_fm(g_pre_mix), _fm(g_post_mix), _fm(g_pre_ffn), _fm(g_post_ffn)], axis=1)
    sh["g2"] = np.ascontiguousarray(np.repeat(gs, 2, axis=-1))
    wi = w_in.reshape(DEPTH, 16, 128, 2560)
    cols = list(range(0, 1280, 128)) + list(range(1536, 2560, 128))
    sh["wqkf"] = np.ascontiguousarray(
        np.stack([wi[:, :, :, c0:c0 + 128] for c0 in cols], axis=1).transpose(0, 1, 3, 2, 4))
    sh["wv"] = np.ascontiguousarray(wi[:, :, :, 1280:1536].transpose(0, 2, 1, 3))
    sh["qkn"] = np.ascontiguousarray(np.stack([q_norm, k_norm], axis=-1))
    sh["w4"] = np.ascontiguousarray(w_four.transpose(0, 2, 1, 3))
    sh["gy"] = np.ascontiguousarray(np.concatenate([_fm(g_attn_out), _fm(g_four_out)], axis=-1))
    sh["wout"] = np.ascontiguousarray(w_out.reshape(DEPTH, 16, 128, D).transpose(0, 2, 1, 3))
    wu = w_up.reshape(DEPTH, 16, 128, 2, NFC, 128)
    sh["wup"] = np.ascontiguousarray(wu.transpose(0, 4, 2, 3, 1, 5))
    cwf = conv_w.reshape(DEPTH, 3, NFC, 128).transpose(0, 3, 2, 1)
    sh["cw"] = np.ascontiguousarray(cwf.reshape(DEPTH, 128, NFC * 3))
    sh["cb"] = _fm(conv_b)
    wd = w_down.reshape(DEPTH, NFC, 128, 16, 128)
    sh["wdn"] = np.ascontiguousarray(wd.transpose(0, 3, 2, 1, 4))
    for k in sh:
        assert sh[k].dtype == f, k
    sh.update(_consts())
    return sh


def _prep_core(b, x, c, ctx, c_ctx):
    xa = np.concatenate([x[b].T, ctx[b].T], axis=1)
    xt = np.ascontiguousarray(xa.reshape(16, 128, TA).transpose(1, 0, 2))
    cc = np.stack([_fm(c[b]), _fm(c_ctx)], axis=-1).reshape(128, 32)
    return {"xt": xt, "cc": np.ascontiguousarray(cc)}


_NC_CACHE = {}


def kernel(x, c, ctx, c_ctx, w_mod, b_mod, g_pre_mix, g_post_mix, g_pre_ffn, g_post_ffn,
           w_in, q_norm, k_norm, w_four, g_attn_out, g_four_out, w_out,
           w_up, conv_w, conv_b, w_down):
    args = [np.asarray(a, dtype=np.float32) for a in
            (x, c, ctx, c_ctx, w_mod, b_mod, g_pre_mix, g_post_mix, g_pre_ffn, g_post_ffn, w_in, q_norm, k_norm,
             w_four, g_attn_out, g_four_out, w_out, w_up, conv_w, conv_b, w_down)]
    (x, c, ctx, c_ctx, w_mod, b_mod, g_pre_mix, g_post_mix, g_pre_ffn, g_post_ffn, w_in, q_norm, k_norm,
     w_four, g_attn_out, g_four_out, w_out, w_up, conv_w, conv_b, w_down) = args
    shared = _prep_shared(c_ctx, w_mod, b_mod, g_pre_mix, g_post_mix, g_pre_ffn, g_post_ffn, w_in, q_norm, k_norm,
                          w_four, g_attn_out, g_four_out, w_out, w_up, conv_w, conv_b, w_down)
    in_maps = []
    for b in range(NCORES):
        m = dict(shared)
        m.update(_prep_core(b, x, c, ctx, c_ctx))
        in_maps.append(m)
    nc = build_nc()
    res = run_bass_kernel_spmd(nc, in_maps, core_ids=list(range(NCORES)))
    outs = []
    for b in range(NCORES):
        o = np.asarray(res.results[b]["out"], dtype=np.float32)
        outs.append(o.transpose(1, 0, 2).reshape(D, T).T)
    return np.ascontiguousarray(np.stack(outs, axis=0)).astype(np.float32)
```

```python
import contextlib
import numpy as np
import ml_dtypes
import concourse.bass as bass
import concourse.mybir as mybir
from concourse.bass_utils import run_bass_kernel_spmd

F32 = mybir.dt.float32
BF16 = mybir.dt.bfloat16
AF = mybir.ActivationFunctionType
ALU = mybir.AluOpType

D = 2048
T = 2048
CTX = 256
TA = T + CTX
DEPTH = 2
FFN = 5632
NFC = FFN // 128
EPS = 1e-6
NCORES = 8

COMPUTE = ("pe", "act", "dve", "pool")
DMAQ = ("sp", "pq")
NSLOT = 8
NEXTMOD = True


class Op:
    __slots__ = ("eng", "idx", "fn", "waits", "signal", "is_dma", "slot", "slot_val", "sigcount")

    def __init__(self, eng, idx, fn, is_dma):
        self.eng = eng
        self.idx = idx
        self.fn = fn
        self.waits = []
        self.signal = False
        self.is_dma = is_dma
        self.slot = None
        self.slot_val = None
        self.sigcount = None


class Sched:
    def __init__(self, nc):
        self.nc = nc
        self.streams = {e: [] for e in COMPUTE + ("sp",)}
        self.last_writer = {}
        self.readers = {}
        self.waited = {e: {} for e in COMPUTE + ("sp",)}
        self.waited_dma = {e: set() for e in COMPUTE + ("sp",)}
        self.dma_count = {q: 0 for q in DMAQ}
        self.dma_ops = {q: [] for q in DMAQ}
        self.last_compute = {e: None for e in COMPUTE}

    @staticmethod
    def _stream_of(eng):
        return "pool" if eng == "pq" else eng

    def op(self, eng, fn, reads=(), writes=(), relax=False):
        sname = self._stream_of(eng)
        stream = self.streams[sname]
        is_dma = eng in DMAQ
        o = Op(eng, len(stream), fn, is_dma)
        writes = list(writes) + [t for t in reads if isinstance(t, tuple) and t[0] == "ps" and t not in writes]
        deps = []
        for t in reads:
            w = self.last_writer.get(t)
            if w is not None:
                deps.append((w, "raw"))
        for t in writes:
            w = self.last_writer.get(t)
            if w is not None:
                deps.append((w, "waw"))
            for r in self.readers.get(t, ()):
                deps.append((r, "war"))
        if is_dma:
            n = self.dma_count[eng]
            o.slot = n % NSLOT
            o.slot_val = 16 * (n // NSLOT + 1)
            if n >= NSLOT:
                deps.append((self.dma_ops[eng][n - NSLOT], "slot"))
            self.dma_count[eng] = n + 1
            self.dma_ops[eng].append(o)
        best = {}
        for d, kind in deps:
            if d is o:
                continue
            if d.is_dma:
                self._add_wait(o, sname, d, kind)
                continue
            if d.eng == sname and not is_dma:
                if sname == "pe" or kind == "slot":
                    continue
                if relax and kind != "raw":
                    continue
            cur = best.get(d.eng)
            if cur is None or d.idx > cur.idx:
                best[d.eng] = d
        for d in best.values():
            self._add_wait(o, sname, d, "raw")
        stream.append(o)
        if not is_dma and eng in COMPUTE:
            self.last_compute[eng] = o
        for t in reads:
            self.readers.setdefault(t, []).append(o)
        for t in writes:
            self.last_writer[t] = o
            self.readers[t] = []
        return o

    def _add_wait(self, o, sname, d, kind):
        if d.is_dma:
            if d in self.waited_dma[sname]:
                return
            self.waited_dma[sname].add(d)
            o.waits.append(d)
            return
        dname = d.eng
        if dname == sname and not o.is_dma:
            if sname == "pe":
                return
            if kind == "slot":
                return
        if self.waited[sname].get(dname, -1) >= d.idx:
            return
        self.waited[sname][dname] = d.idx
        d.signal = True
        o.waits.append(d)

    def barrier(self):
        lasts = [o for o in self.last_compute.values() if o is not None]
        dmas = []
        for q in DMAQ:
            dmas += self.dma_ops[q][-NSLOT:]
        for sname in list(self.streams.keys()):
            stream = self.streams[sname]
            o = Op(sname, len(stream), lambda e: e.nop(), False)
            o.signal = False
            for d in lasts:
                if d.eng != sname:
                    self._add_wait(o, sname, d, "raw")
            for d in dmas:
                self._add_wait(o, sname, d, "raw")
            stream.append(o)
            o.eng = "__nop__"

    def emit(self, final_ops):
        nc = self.nc
        with contextlib.ExitStack() as st:
            sems = {e: st.enter_context(nc.semaphore("s_" + e)) for e in COMPUTE}
            dsems = {q: [st.enter_context(nc.semaphore(f"d_{q}{i}")) for i in range(NSLOT)] for q in DMAQ}
            for e in COMPUTE:
                c = 0
                for o in self.streams[e]:
                    if o.is_dma or o.eng == "__nop__":
                        continue
                    if o.signal:
                        c += 1
                        o.sigcount = c
            block = st.enter_context(nc.Block())
            engmap = {"pe": block.tensor, "act": block.scalar, "dve": block.vector,
                      "pool": block.gpsimd, "sp": block.sync}

            def make(sname):
                def body(eng):
                    for o in self.streams[sname]:
                        for d in o.waits:
                            if d.is_dma:
                                eng.wait_ge(dsems[d.eng][d.slot], d.slot_val)
                            else:
                                eng.wait_ge(sems[d.eng], d.sigcount)
                        ins = o.fn(eng)
                        if o.is_dma:
                            ins.then_inc(dsems[o.eng][o.slot], 16)
                        elif o.signal:
                            ins.then_inc(sems[o.eng], 1)
                    for d in final_ops:
                        if self._stream_of(d.eng) == sname:
                            eng.wait_ge(dsems[d.eng][d.slot], d.slot_val)
                return body

            for sname in ("pe", "act", "dve", "pool", "sp"):
                engmap[sname](make(sname))


class Arena:
    def __init__(self, nc, lo, hi):
        self.nc = nc
        self.lo = lo
        self.hi = hi
        self.off = lo
        self.n = 0

    def mark(self):
        return self.off

    def reset(self, to):
        self.off = to

    def alloc(self, shape, dtype, name="t"):
        size = int(np.prod(shape[1:])) * (4 if dtype == F32 else 2)
        off = (self.off + 63) // 64 * 64
        assert off + size <= self.hi, f"SBUF arena overflow: {name} {shape} off={off} size={size} hi={self.hi}"
        self.off = off + size
        self.n += 1
        return self.nc.alloc_sbuf_tensor_at(f"{name}_{self.n}", list(shape), dtype, offset=off)


class Pipe:
    def __init__(self):
        self.q = []

    def defer(self, delay, fn):
        self.q.append([delay, fn])

    def tick(self):
        cur, self.q = self.q, []
        for item in cur:
            item[0] -= 1
            if item[0] <= 0:
                item[1]()
            else:
                self.q.append(item)

    def flush(self):
        while self.q:
            self.tick()


def build_nc(nlayers=DEPTH, debug=False, gelu_mode="tanh_act", stop=None):
    nc = bass.Bass("TRN2", target_bir_lowering=False)

    def din(name, shape, dt=F32):
        return nc.dram_tensor(name, list(shape), dt, kind="ExternalInput").ap()

    def dscr(name, shape, dt):
        if debug:
            return nc.dram_tensor(name, list(shape), dt, kind="ExternalOutput").ap()
        return nc.dram_tensor(name, list(shape), dt).ap()

    xt = din("xt", [128, 16, TA])
    cc = din("cc", [128, 32])
    wmod = din("wmod", [DEPTH, 48, 128, 16, 256])
    bmod2 = din("bmod2", [DEPTH, 128, 192])
    g2 = din("g2", [DEPTH, 4, 128, 32])
    wqkf = din("wqkf", [DEPTH, 18, 128, 16, 128])
    wv = din("wv", [DEPTH, 128, 16, 256])
    qkn = din("qkn", [DEPTH, 128, 2])
    w4 = din("w4", [DEPTH, 128, 8, 128])
    gy = din("gy", [DEPTH, 128, 16])
    wout = din("wout", [DEPTH, 128, 16, D])
    wup = din("wup", [DEPTH, NFC, 128, 2, 16, 128])
    cw = din("cw", [DEPTH, 128, NFC * 3])
    cbias = din("cb", [DEPTH, 128, NFC])
    wdn = din("wdn", [DEPTH, 16, 128, NFC, 128])
    ones_d = din("ones", [128, 128], BF16)
    swp_d = din("swp", [128, 128], BF16)
    cs_d = din("cs", [128, 256], BF16)
    ropec_d = din("ropec", [128, T])
    ropes_d = din("ropes", [128, T])
    dftc_d = din("dftc", [128, 16, T], BF16)
    dfts_d = din("dfts", [128, 16, T], BF16)
    dftcc_d = din("dftcc", [128, 2, CTX], BF16)
    dftsc_d = din("dftsc", [128, 2, CTX], BF16)
    out_d = nc.dram_tensor("out", [128, 16, T], F32, kind="ExternalOutput").ap()

    R1 = dscr("R1", [128, 16, TA], F32)
    R2 = dscr("R2", [128, 16, TA], F32)
    QK = dscr("QK", [128, 10, TA], BF16)
    VD = dscr("VD", [128, 18, 256], BF16)
    GCS = dscr("GCS", [128, 8, 18, 256], BF16)
    YT = dscr("YT", [128, 16, TA], BF16)
    ACTD = dscr("ACTD", [128, NFC, TA], BF16)
    HT = dscr("HT", [128, 16, TA], BF16)

    S = Sched(nc)
    lo = (int(nc.sbuf_base) + 63) // 64 * 64
    hi = int(nc.sbuf_top)
    AR = Arena(nc, lo, hi)
    ps = [nc.alloc_psum_tensor(f"psb{i}", [128, 512], F32) for i in range(8)]

    def P(i):
        return ("ps", i)

    def MM(out, lhsT, rhs, start, stop, r, w):
        return S.op("pe", lambda e: e.matmul(out, lhsT, rhs, start=start, stop=stop), r, w)

    def ACT(out, in_, func, r, w, relax=False, **kw):
        return S.op("act", lambda e: e.activation(out=out, in_=in_, func=func, **kw), r, w, relax=relax)

    def TS(out, in0, s1, s2, op0, op1, r, w):
        if op1 is None:
            return S.op("dve", lambda e: e.tensor_scalar(out=out, in0=in0, scalar1=s1, scalar2=None, op0=op0), r, w)
        return S.op("dve", lambda e: e.tensor_scalar(out=out, in0=in0, scalar1=s1, scalar2=s2, op0=op0, op1=op1), r, w)

    def STT(out, in0, scalar, in1, op0, op1, r, w, relax=False):
        return S.op("dve", lambda e: e.scalar_tensor_tensor(out=out, in0=in0, scalar=scalar, in1=in1, op0=op0, op1=op1), r, w, relax=relax)

    def TT(out, in0, in1, op, r, w):
        return S.op("dve", lambda e: e.tensor_tensor(out=out, in0=in0, in1=in1, op=op), r, w)

    def CP(out, in_, r, w):
        return S.op("dve", lambda e: e.tensor_copy(out=out, in_=in_), r, w)

    def RCP(out, in_, r, w):
        return S.op("dve", lambda e: e.reciprocal(out=out, in_=in_), r, w)

    def DMA(q, out, in_, r, w):
        return S.op(q, lambda e: e.dma_start(out=out, in_=in_), r, w)

    uid = [0]

    def tok(name):
        uid[0] += 1
        return (name, uid[0])

    ones_t = AR.alloc([128, 128], BF16, "ones")
    swp_t = AR.alloc([128, 128], BF16, "swp")
    cs_t = AR.alloc([128, 256], BF16, "cs")
    modt = [AR.alloc([128, 192], F32, "modt") for _ in range(DEPTH)]
    A1 = [AR.alloc([128, 32], F32, "A1") for _ in range(DEPTH)]
    G1 = [AR.alloc([128, 32], F32, "G1") for _ in range(DEPTH)]
    A2 = [AR.alloc([128, 32], F32, "A2") for _ in range(DEPTH)]
    G2 = [AR.alloc([128, 32], F32, "G2") for _ in range(DEPTH)]
    qkn_t = [AR.alloc([128, 2], F32, "qkn") for _ in range(DEPTH)]
    gy_t = [AR.alloc([128, 16], F32, "gy") for _ in range(DEPTH)]
    silu_t = AR.alloc([128, 32], BF16, "silu")
    bm_t = [AR.alloc([128, 192], F32, "bm") for _ in range(DEPTH)]
    g2_t = [[AR.alloc([128, 32], F32, "g2") for _ in range(4)] for _ in range(DEPTH)]
    rs_t = AR.alloc([128, 1024], F32, "rs")
    base = AR.mark()

    DMA("sp", ones_t[:, :], ones_d[:, :], [], ["ones"])
    DMA("sp", swp_t[:, :], swp_d[:, :], [], ["swp"])
    DMA("sp", cs_t[:, :], cs_d[:, :], [], ["cs"])
    for l in range(DEPTH):
        DMA("sp", qkn_t[l][:, :], qkn[l], [], [("qkn", l)])
        DMA("sp", gy_t[l][:, :], gy[l], [], [("gy", l)])

    def rstd(ps_ap, n, inv_count, r, off=0):
        ACT(rs_t[:, off:off + n], ps_ap, AF.Ln, list(r) + ["eps"], [("rs", off)], scale=inv_count, bias=eps_t[:, 0:1])
        ACT(rs_t[:, off:off + n], rs_t[:, off:off + n], AF.Exp, [("rs", off)], [("rs", off)], scale=-0.5)

    eps_t = AR.alloc([128, 1], F32, "eps")
    base = AR.mark()
    S.op("dve", lambda e: e.memset(eps_t[:, :], EPS), [], ["eps"])

    class ModBG:
        def __init__(self):
            self.wt = None
            self.cnt = 0
            self.todo = []

        def setup(self, wt):
            self.wt = wt

        def load(self, l, it_):
            nb_ = len(self.wt)
            w = self.wt[self.cnt % nb_]
            wtok = ("wm", id(w))
            self.cnt += 1
            DMA("pq", w[:, :, :], wmod[l, it_], [], [wtok])
            return (w, wtok, it_)

        def compute(self, pend):
            w, wtok, it_ = pend
            for m in range(2):
                j = it_ * 2 + m
                for k in range(16):
                    MM(ps[7][:, j * 2:j * 2 + 2], w[:, k, m * 128:(m + 1) * 128], silu_t[:, k * 2:k * 2 + 2],
                       k == 0, k == 15, [wtok, "silu"], [P(7)])

        def item(self, l, it_):
            self.compute(self.load(l, it_))

        def evac_A(self, l):
            TT(modt[l][:, 0:64], ps[7][:, 0:64], bm_t[l][:, 0:64], ALU.add, [P(7), ("bm", l)], [("modtA", l)])
            STT(A1[l][:, :], modt[l][:, 32:64], 1.0, g2_t[l][0][:, :], ALU.add, ALU.mult,
                [("modtA", l), ("g2", l, 0)], [("A1", l)])

        def evac_B(self, l):
            TT(modt[l][:, 64:192], ps[7][:, 64:192], bm_t[l][:, 64:192], ALU.add, [P(7), ("bm", l)], [("modtB", l)])
            TT(G1[l][:, :], modt[l][:, 64:96], g2_t[l][1][:, :], ALU.mult, [("modtB", l), ("g2", l, 1)], [("G1", l)])
            STT(A2[l][:, :], modt[l][:, 128:160], 1.0, g2_t[l][2][:, :], ALU.add, ALU.mult,
                [("modtB", l), ("g2", l, 2)], [("A2", l)])
            TT(G2[l][:, :], modt[l][:, 160:192], g2_t[l][3][:, :], ALU.mult, [("modtB", l), ("g2", l, 3)], [("G2", l)])

        def plan_background(self):
            self.pending = []
            for it_ in range(16, 48):
                self.todo.append(("item", 0, it_))
            self.todo.append(("fn", lambda: self.evac_B(0)))
            for l in range(1, nlayers):
                for it_ in range(48):
                    self.todo.append(("item", l, it_))
                self.todo.append(("fn", lambda l=l: (self.evac_A(l), self.evac_B(l))))

        def step(self, n=1):
            depth = len(self.wt)
            for _ in range(n):
                if len(self.pending) >= depth:
                    self.compute(self.pending.pop(0))
                if self.todo:
                    e = self.todo.pop(0)
                    if e[0] == "item":
                        self.pending.append(self.load(e[1], e[2]))
                    else:
                        self.drain_pending()
                        e[1]()
                elif self.pending:
                    self.compute(self.pending.pop(0))

        def drain_pending(self):
            while self.pending:
                self.compute(self.pending.pop(0))

        def flush(self):
            while self.todo or self.pending:
                self.step(1)

    modbg = ModBG()

    def mod_prologue(wt):
        cc_t = AR.alloc([128, 32], F32, "cc")
        DMA("sp", cc_t[:, :], cc[:, :], [], ["cc"])
        ACT(silu_t[:, :], cc_t[:, :], AF.Silu, ["cc"], ["silu"])
        for l in range(nlayers):
            DMA("sp", bm_t[l][:, :], bmod2[l], [], [("bm", l)])
            for i in range(4):
                DMA("sp", g2_t[l][i][:, :], g2[l, i], [], [("g2", l, i)])
        modbg.setup(wt)
        for it_ in range(16):
            modbg.item(0, it_)
        modbg.evac_A(0)
        modbg.plan_background()

    def modulate(l, src, Avec, sh_off, hT, blocks, xb, sq, tmp, after_block=None):
        atok = ("A1" if Avec is A1[l] else "A2", l)
        mtok = ("modtA" if sh_off == 0 else "modtB", l)
        for bi, (t0, n, s) in enumerate(blocks):
            x_ = xb[bi % len(xb)]
            xtok = ("xb", bi % len(xb))
            DMA("sp", x_[:, :, :n], src[:, :, t0:t0 + n], [("R", src.tensor.name)], [xtok])
            sq_ = sq[bi % len(sq)]
            sqtok = ("sq", bi % len(sq))
            ACT(sq_[:, :, :n], x_[:, :, :n], AF.Square, [xtok], [sqtok])
            for k in range(16):
                MM(ps[6][:, :n], ones_t[:, :], sq_[:, k, :n], k == 0, k == 15, [sqtok, "ones"], [P(6)])
            ro = (bi % 2) * 512
            rstd(ps[6][:, :n], n, 1.0 / D, [P(6)], off=ro)
            TT(x_[:, :, :n], x_[:, :, :n], rs_t[:, ro:ro + n].unsqueeze(1).broadcast_to([128, 16, n]), ALU.mult,
               [xtok, ("rs", ro)], [xtok])
            for k in range(16):
                ACT(hT[:, k, t0:t0 + n], x_[:, k, :n], AF.Identity, [xtok, atok, mtok], [("hT", t0 // 256)], relax=True,
                    scale=Avec[:, k * 2 + s:k * 2 + s + 1], bias=modt[l][:, sh_off + k * 2 + s:sh_off + k * 2 + s + 1])
            if after_block is not None:
                after_block()

    def hT_tokens(t0, n):
        return [("hT", b) for b in range(t0 // 256, (t0 + n + 255) // 256)]

    def mod_stats_and_apply(lm, Avec, atok, sh_off, x3, xtok, s, t0, n, sqm, hst, bank, ro):
        mtok = ("modtA" if sh_off == 0 else "modtB", lm)
        for k in range(16):
            MM(ps[bank][:, :n], ones_t[:, :], sqm[:, k, :n], k == 0, k == 15, ["sqm", "ones"], [P(bank)])
        rstd(ps[bank][:, :n], n, 1.0 / D, [P(bank)], off=ro)
        TT(x3, x3, rs_t[:, ro:ro + n].unsqueeze(1).broadcast_to([128, 16, n]), ALU.mult, [xtok, ("rs", ro)], [xtok])
        for k in range(16):
            ACT(hst[:, k, :n], x3[:, k, :], AF.Identity, [xtok, atok, mtok], ["hst"], relax=True,
                scale=Avec[:, k * 2 + s:k * 2 + s + 1], bias=modt[lm][:, sh_off + k * 2 + s:sh_off + k * 2 + s + 1])
        DMA("sp", HT[:, :, t0:t0 + n], hst[:, :, :n], ["hst"], ["HT"])

    def load_hT(hT, ntok):
        for c0 in range(0, ntok, 768):
            c1 = min(c0 + 768, ntok)
            DMA("sp", hT[:, :, c0:c1], HT[:, :, c0:c1], ["HT"], [("hT", b) for b in range(c0 // 256, c1 // 256)])

    def phase_A(l, src):
        last = (l == DEPTH - 1)
        AR.reset(base)
        hT = AR.alloc([128, 16, TA], BF16, "hT")
        xb = [AR.alloc([128, 16, 256], F32, "xb") for _ in range(2)]
        sq = [AR.alloc([128, 16, 256], BF16, "sq")]
        if l == 0:
            wm_t = [AR.alloc([128, 16, 256], BF16, "wm") for _ in range(2)]
        ropec = AR.alloc([128, T], F32, "ropec")
        ropes = AR.alloc([128, T], F32, "ropes")
        wb = [AR.alloc([128, 16, 128], BF16, "wqkf") for _ in range(3)]
        wv_t = AR.alloc([128, 16, 256], BF16, "wv")
        sqh = [AR.alloc([128, 512], BF16, "sqh") for _ in range(2)]
        qn = [AR.alloc([128, 512], BF16, "qn") for _ in range(2)]
        t1 = [AR.alloc([128, 512], F32, "t1") for _ in range(2)]
        t2 = [AR.alloc([128, 512], F32, "t2") for _ in range(2)]
        qo = [AR.alloc([128, 512], BF16, "qo") for _ in range(2)]
        fT = [AR.alloc([128, 512], BF16, "fT") for _ in range(2)]
        gst = [AR.alloc([128, 4, 256], BF16, "gst") for _ in range(2)]
        vst = AR.alloc([128, 18, 256], BF16, "vst")

        DMA("sp", ropec[:, :], ropec_d[:, :], [], ["ropec"])
        DMA("sp", ropes[:, :], ropes_d[:, :], [], ["ropes"])
        blocks = [(i * 256, 256, 0) for i in range(8)] + [(T, 256, 1)]
        keep = 41 if nlayers > 1 else 0
        if l == 0:
            mod_prologue(wm_t)
            modulate(l, src, A1[l], 0, hT, blocks, xb, sq, None, after_block=lambda: modbg.step(2 if len(modbg.todo) > keep + 2 else 0))
        elif NEXTMOD:
            load_hT(hT, TA)
        else:
            modulate(l, src, A1[l], 0, hT, blocks, xb, sq, None)
        def bgstep():
            if l == 0 and len(modbg.todo) > keep:
                modbg.step(1)

        chunks = [(0, 512), (512, 512), (1024, 512), (1536, 512), (T, 256)]
        it = 0
        DMA("pq", wv_t[:, :, :], wv[l], [], ["wv"])
        pipe = Pipe()

        def qk_stage2(i, pa, cb, kind, t0, n, is_ctx):
            pb = 3 + i % 2
            ro = (i % 2) * 512
            gcol = 0 if kind == "q" else 1
            b2 = i % 2
            MM(ps[pb][:, :n], ones_t[:, :], sqh[b2][:, :n], True, True, [("sqh", b2), "ones"], [P(pb)])
            rstd(ps[pb][:, :n], n, 1.0 / 128, [P(pb)], off=ro)
            dst = qn[b2] if not is_ctx else qo[b2]
            dtok = ("qn", b2) if not is_ctx else ("qo", b2)
            STT(dst[:, :n], ps[pa][:, :n], qkn_t[l][:, gcol:gcol + 1], rs_t[:, ro:ro + n], ALU.mult, ALU.mult,
                [P(pa), ("qkn", l), ("rs", ro)], [dtok])
            if is_ctx:
                DMA("sp", QK[:, cb, t0:t0 + n], qo[b2][:, :n], [("qo", b2)], [("QK", cb)])
            else:
                pipe.defer(1, lambda: qk_stage3(i, cb, t0, n))

        def qk_stage3(i, cb, t0, n):
            b2 = i % 2
            pc = 5 + i % 2
            MM(ps[pc][:, :n], swp_t[:, :], qn[b2][:, :n], True, True, [("qn", b2), "swp"], [P(pc)])
            TT(t1[b2][:, :n], qn[b2][:, :n], ropec[:, t0:t0 + n], ALU.mult, [("qn", b2), "ropec"], [("t1", b2)])
            TT(t2[b2][:, :n], ps[pc][:, :n], ropes[:, t0:t0 + n], ALU.mult, [P(pc), "ropes"], [("t2", b2)])
            TT(qo[b2][:, :n], t1[b2][:, :n], t2[b2][:, :n], ALU.add, [("t1", b2), ("t2", b2)], [("qo", b2)])
            DMA("sp", QK[:, cb, t0:t0 + n], qo[b2][:, :n], [("qo", b2)], [("QK", cb)])

        def f_stage2(i, g, t0, n):
            b2 = i % 2
            nt = n // 128
            for j in range(nt):
                pb = (3 + i % 2) if j < 2 else (5 + i % 2)
                jj = j % 2
                MM(ps[pb][:, jj * 256:(jj + 1) * 256], fT[b2][:, j * 128:(j + 1) * 128], cs_t[:, :], True, True,
                   [("fT", b2), "cs"], [P(pb)])
            CP(gst[b2][:, 0:2, :], ps[3 + i % 2][:, :].rearrange("p (a b) -> p a b", a=2), [P(3 + i % 2)], [("gst", b2)])
            if nt > 2:
                CP(gst[b2][:, 2:4, :], ps[5 + i % 2][:, :].rearrange("p (a b) -> p a b", a=2), [P(5 + i % 2)], [("gst", b2)])
            DMA("sp", GCS[:, g, t0 // 128:t0 // 128 + nt, :], gst[b2][:, 0:nt, :], [("gst", b2)], [("GCS", g)])

        for cb in range(18):
            w = wb[cb % 3]
            wtok = ("wqkf", cb % 3)
            DMA("pq", w[:, :, :], wqkf[l, cb], [], [wtok])
            kind = "q" if cb < 8 else ("k" if cb < 10 else "f")
            for (t0, n) in chunks:
                is_ctx = t0 >= T
                if is_ctx and last and kind != "k":
                    continue
                i = it
                pa = it % 3
                it += 1
                for k in range(16):
                    MM(ps[pa][:, :n], w[:, k, :], hT[:, k, t0:t0 + n], k == 0, k == 15,
                       [wtok] + hT_tokens(t0, n), [P(pa)])
                pipe.tick()
                bgstep()
                if kind in ("q", "k"):
                    ACT(sqh[i % 2][:, :n], ps[pa][:, :n], AF.Square, [P(pa)], [("sqh", i % 2)])
                    pipe.defer(1, lambda i=i, pa=pa, cb=cb, kind=kind, t0=t0, n=n, is_ctx=is_ctx:
                               qk_stage2(i, pa, cb, kind, t0, n, is_ctx))
                else:
                    ACT(fT[i % 2][:, :n], ps[pa][:, :n], AF.Copy, [P(pa)], [("fT", i % 2)])
                    pipe.defer(1, lambda i=i, g=cb - 10, t0=t0, n=n: f_stage2(i, g, t0, n))
        pipe.flush()
        if l == 0:
            while len(modbg.todo) > keep:
                modbg.step(1)
            modbg.drain_pending()
        for tt in range(18):
            pa = tt % 2
            for k in range(16):
                MM(ps[pa][:, 0:256], hT[:, k, tt * 128:(tt + 1) * 128], wv_t[:, k, :], k == 0, k == 15,
                   ["wv"] + hT_tokens(tt * 128, 128), [P(pa)])
            ACT(vst[:, tt, :], ps[pa][:, 0:256], AF.Copy, [P(pa)], ["vst"], relax=True)
        DMA("sp", VD[:, :, :], vst[:, :, :], ["vst"], ["VD"])
        S.barrier()

    def phase_F(l):
        last = (l == DEPTH - 1)
        AR.reset(base)
        gcs = AR.alloc([128, 8, 18, 256], BF16, "gcs")
        tab = [AR.alloc([128, 2, 16, 512], BF16, "tab") for _ in range(2)]
        tabc = AR.alloc([128, 2, 2, 256], BF16, "tabc")
        w4_t = AR.alloc([128, 8, 128], BF16, "w4")
        if l == 0 and nlayers > 1:
            fo = [AR.alloc([128, 8, 512], F32, "fo")] * 2
        else:
            fo = [AR.alloc([128, 8, 512], F32, "fo") for _ in range(2)]
        yst = AR.alloc([128, 8, 512], BF16, "yst")
        FTb = [AR.alloc([128, 512], BF16, "FT") for _ in range(2)]
        sqf = [AR.alloc([128, 512], BF16, "sqf") for _ in range(2)]
        DMA("pq", w4_t[:, :, :], w4[l], [], ["w4"])
        for g in range(8):
            DMA("sp", gcs[:, g, :, :], GCS[:, g, :, :], [("GCS", g)], [("gcs", g)])
        DMA("sp", tabc[:, 0, :, :], dftcc_d[:, :, :], [], ["tabc"])
        DMA("sp", tabc[:, 1, :, :], dftsc_d[:, :, :], [], ["tabc"])
        passes = [(kc * 512, 512, 16, 0) for kc in range(4)]
        if not last:
            passes.append((T, 256, 2, 16))
        def load_tab(pi):
            k0_, n_ = passes[pi][0], passes[pi][1]
            if k0_ >= T:
                return
            DMA("sp", tab[pi % 2][:, 0, :, :], dftc_d[:, :, k0_:k0_ + n_], [], [("tab", pi % 2)])
            DMA("sp", tab[pi % 2][:, 1, :, :], dfts_d[:, :, k0_:k0_ + n_], [], [("tab", pi % 2)])

        pipeF = Pipe()
        if l == 0 and modbg.todo:
            modbg.setup([AR.alloc([128, 16, 256], BF16, "wmF") for _ in range(2)])

        def f_stage2(g, pa, n, fo_, pi):
            MM(ps[2 + pa][:, :n], w4_t[:, g, :], FTb[pa][:, :n], True, True, [("FT", pa), "w4"], [P(2 + pa)])
            ACT(sqf[pa][:, :n], ps[2 + pa][:, :n], AF.Square, [P(2 + pa)], [("sqf", pa)])
            ACT(fo_[:, g, :n], ps[2 + pa][:, :n], AF.Copy, [P(2 + pa)], [("fo", id(fo_))])
            pipeF.defer(1, lambda: MM(ps[4 + pi % 2][:, :n], ones_t[:, :], sqf[pa][:, :n], g == 0, g == 7,
                                      [("sqf", pa), "ones"], [P(4 + pi % 2)]))

        load_tab(0)
        for pi, (k0, n, ntt, tt0) in enumerate(passes):
            is_ctx = k0 >= T
            if pi + 1 < len(passes):
                load_tab(pi + 1)
            tb = tab[pi % 2]
            tbtok = ("tab", pi % 2)
            fo_ = fo[pi % 2]
            for g in range(8):
                pa = g % 2
                for ti in range(ntt):
                    if is_ctx:
                        rc, rsn, rtok = tabc[:, 0, ti, :], tabc[:, 1, ti, :], "tabc"
                    else:
                        rc, rsn, rtok = tb[:, 0, ti, :], tb[:, 1, ti, :], tbtok
                    MM(ps[pa][:, :n], gcs[:, g, tt0 + ti, 0:128], rc, ti == 0, False, [("gcs", g), rtok], [P(pa)])
                    MM(ps[pa][:, :n], gcs[:, g, tt0 + ti, 128:256], rsn, False, ti == ntt - 1, [("gcs", g), rtok], [P(pa)])
                pipeF.tick()
                if l == 0:
                    modbg.step(1)
                ACT(FTb[pa][:, :n], ps[pa][:, :n], AF.Copy, [P(pa)], [("FT", pa)])
                pipeF.defer(1, lambda g=g, pa=pa, n=n, fo_=fo_, pi=pi: f_stage2(g, pa, n, fo_, pi))
            pipeF.flush()
            rstd(ps[4 + pi % 2][:, :n], n, 1.0 / 1024, [P(4 + pi % 2)])
            for g in range(8):
                STT(yst[:, g, :n], fo_[:, g, :n], gy_t[l][:, 8 + g:9 + g], rs_t[:, :n], ALU.mult, ALU.mult,
                    [("fo", id(fo_)), ("gy", l), ("rs", 0)], ["yst"], relax=True)
            DMA("sp", YT[:, 8:16, k0:k0 + n], yst[:, :, :n], ["yst"], [("YT", "f")])
        if l == 0:
            modbg.flush()
        S.barrier()

    def phase_T(l):
        last = (l == DEPTH - 1)
        AR.reset(base)
        kT = AR.alloc([128, 2, TA], BF16, "kT")
        vt = AR.alloc([128, 18, 256], BF16, "vt")
        qT = AR.alloc([128, 8, TA], BF16, "qT")
        pT = [[AR.alloc([128, 512], BF16, "pT") for _ in range(2)] for _ in range(2)]
        ao = [AR.alloc([128, 8, 512], F32, "ao") for _ in range(2)]
        sqa = AR.alloc([128, 8, 512], BF16, "sqa")
        rd = [AR.alloc([128, 512], F32, "rd") for _ in range(2)]
        yst = AR.alloc([128, 8, 512], BF16, "ysta")
        DMA("sp", kT[:, :, :], QK[:, 8:10, :], [("QK", 8), ("QK", 9)], ["kT"])
        DMA("sp", vt[:, :, :], VD[:, :, :], ["VD"], ["vt"])
        for h in range(8):
            DMA("sp", qT[:, h, :], QK[:, h, :], [("QK", h)], [("qT", h)])
        scale = float(1.0 / np.sqrt(128.0))
        qchunks = [(i * 512, 512, list(range(18))) for i in range(4)]
        if not last:
            qchunks.append((T, 256, [16, 17]))
        for qi, (t0, n, kts) in enumerate(qchunks):
            ao_ = ao[qi % 2]
            nk = len(kts)
            for hp in range(4):
                heads = (2 * hp, 2 * hp + 1)
                kvh = heads[0] // 4

                def s_step(w, i):
                    h = heads[w]
                    st_ = kts[i]
                    bank = 2 * w + i % 2
                    MM(ps[bank][:, :n], kT[:, kvh, st_ * 128:(st_ + 1) * 128], qT[:, h, t0:t0 + n], True, True,
                       ["kT", ("qT", h)], [P(bank)])
                    ACT(pT[w][i % 2][:, :n], ps[bank][:, :n], AF.Exp, [P(bank)], [("pT", w, i % 2)], scale=scale)

                def pv_step(w, i):
                    st_ = kts[i]
                    po, pd = 4 + 2 * w, 5 + 2 * w
                    MM(ps[po][:, :n], vt[:, st_, kvh * 128:(kvh + 1) * 128], pT[w][i % 2][:, :n], i == 0, i == nk - 1,
                       ["vt", ("pT", w, i % 2)], [P(po)])
                    MM(ps[pd][:, :n], ones_t[:, :], pT[w][i % 2][:, :n], i == 0, i == nk - 1,
                       ["ones", ("pT", w, i % 2)], [P(pd)])

                s_step(0, 0)
                s_step(1, 0)
                for i in range(nk):
                    if i + 1 < nk:
                        s_step(0, i + 1)
                        s_step(1, i + 1)
                    pv_step(0, i)
                    pv_step(1, i)
                for w in range(2):
                    h = heads[w]
                    po, pd = 4 + 2 * w, 5 + 2 * w
                    ACT(rd[w][:, :n], ps[pd][:, :n], AF.Ln, [P(pd)], [("rd", w)])
                    ACT(rd[w][:, :n], rd[w][:, :n], AF.Exp, [("rd", w)], [("rd", w)], scale=-1.0)
                    TT(ao_[:, h, :n], ps[po][:, :n], rd[w][:, :n], ALU.mult, [P(po), ("rd", w)], [("ao", qi % 2, h)])
                    ACT(sqa[:, h, :n], ao_[:, h, :n], AF.Square, [("ao", qi % 2, h)], [("sqa", h)])
            for h in range(8):
                MM(ps[0][:, :n], ones_t[:, :], sqa[:, h, :n], h == 0, h == 7, [("sqa", h), "ones"], [P(0)])
            rstd(ps[0][:, :n], n, 1.0 / 1024, [P(0)])
            for h in range(8):
                STT(yst[:, h, :n], ao_[:, h, :n], gy_t[l][:, h:h + 1], rs_t[:, :n], ALU.mult, ALU.mult,
                    [("ao", qi % 2, h), ("gy", l), ("rs", 0)], ["ysta"], relax=True)
            DMA("sp", YT[:, 0:8, t0:t0 + n], yst[:, :, :n], ["ysta"], [("YT", "a")])
        S.barrier()

    def phase_O(l, src, dst):
        last = (l == DEPTH - 1)
        AR.reset(base)
        wo = AR.alloc([128, 16, D], BF16, "wo")
        yb = [AR.alloc([128, 16, 256], BF16, "yb") for _ in range(3)]
        xb = [AR.alloc([128, 16, 256], F32, "xbo") for _ in range(3)]
        mixs2 = [AR.alloc([128, 16, 256], F32, "mixs") for _ in range(2)]
        sqb = [AR.alloc([128, 256], BF16, "sqb") for _ in range(2)]
        sqm = AR.alloc([128, 16, 256], BF16, "sqm")
        hst = AR.alloc([128, 16, 256], BF16, "hst")
        for k in range(16):
            DMA("pq", wo[:, k, :], wout[l, :, k, :], [], [("wo", k)])
        wotoks = [("wo", k) for k in range(16)]
        nb = 8 if last else 9
        def load_blk(b):
            DMA("sp", yb[b % 3][:, :, :], YT[:, :, b * 256:b * 256 + 256], [("YT", "a"), ("YT", "f")], [("yb", b % 3)])
            DMA("sp", xb[b % 3][:, :, :], src[:, :, b * 256:b * 256 + 256], [("R", src.tensor.name)], [("xbo", b % 3)])

        pipeO = Pipe()

        def tailO(n, pss, mixs, mb, x_, b, t0):
            ro = (b % 2) * 512
            rstd(ps[pss][:, :n], n, 1.0 / D, [P(pss)], off=ro)
            for dc in range(16):
                TT(mixs[:, dc, :], mixs[:, dc, :], rs_t[:, ro:ro + n], ALU.mult, [("mixs", mb, dc), ("rs", ro)], [("mixs", mb, dc)])
            for dc in range(16):
                TT(x_[:, dc, :], mixs[:, dc, :], x_[:, dc, :], ALU.add, [("mixs", mb, dc), ("xbo", b % 3)], [("xbo", b % 3)])
            DMA("sp", dst[:, :, t0:t0 + n], x_[:, :, :], [("xbo", b % 3)], [("R", dst.tensor.name)])
            x3 = x_[:, :, :n]
            xtok = ("xbo", b % 3)
            S.op("pool", lambda e: e.tensor_tensor(out=sqm[:, :, :n], in0=x3, in1=x3, op=ALU.mult), [xtok], ["sqm"])
            s_ = 1 if t0 >= T else 0
            ro2 = 256 + (b % 2) * 512
            mtok = ("modtB", l)

            def p_stats():
                for k in range(16):
                    MM(ps[6][:, :n], ones_t[:, :], sqm[:, k, :n], k == 0, k == 15, ["sqm", "ones"], [P(6)])

            def p_rstd():
                rstd(ps[6][:, :n], n, 1.0 / D, [P(6)], off=ro2)
                TT(x3, x3, rs_t[:, ro2:ro2 + n].unsqueeze(1).broadcast_to([128, 16, n]), ALU.mult, [xtok, ("rs", ro2)], [xtok])

            def p_apply(k0):
                for k in range(k0, k0 + 4):
                    ACT(hst[:, k, :n], x_[:, k, :n], AF.Identity, [xtok, ("A2", l), mtok], ["hst"], relax=True,
                        scale=A2[l][:, k * 2 + s_:k * 2 + s_ + 1], bias=modt[l][:, 96 + k * 2 + s_:96 + k * 2 + s_ + 1])

            def p_store():
                DMA("sp", HT[:, :, t0:t0 + n], hst[:, :, :n], ["hst"], ["HT"])
                if b + 3 < nb:
                    load_blk(b + 3)

            pipeO.defer(7, p_stats)
            pipeO.defer(8, p_rstd)
            for q_ in range(4):
                pipeO.defer(10 + q_, lambda k0=4 * q_: p_apply(k0))
            pipeO.defer(14, p_store)

        load_blk(0)
        if nb > 1:
            load_blk(1)
        if nb > 2:
            load_blk(2)
        for b in range(nb):
            t0 = b * 256
            n = 256
            pss = 4 + b % 2
            s = 1 if t0 >= T else 0
            y_ = yb[b % 3]
            x_ = xb[b % 3]
            mixs = mixs2[b % 2]
            mb = b % 2
            for dc in range(16):
                pa = dc % 4
                for k in range(16):
                    MM(ps[pa][:, :n], wo[:, k, dc * 128:(dc + 1) * 128], y_[:, k, :], k == 0, k == 15,
                       [("wo", k), ("yb", b % 3)], [P(pa)])
                pipeO.tick()
                ACT(sqb[dc % 2][:, :n], ps[pa][:, :n], AF.Square, [P(pa)], [("sqb", dc % 2)])
                ACT(mixs[:, dc, :], ps[pa][:, :n], AF.Identity, [P(pa), ("G1", l)], [("mixs", mb, dc)],
                    scale=G1[l][:, dc * 2 + s:dc * 2 + s + 1])
                pipeO.defer(1, lambda dc=dc, n=n, pss=pss: MM(ps[pss][:, :n], ones_t[:, :], sqb[dc % 2][:, :n], dc == 0, dc == 15,
                                                     [("sqb", dc % 2), "ones"], [P(pss)]))
            pipeO.defer(1, lambda n=n, pss=pss, mixs=mixs, mb=mb, x_=x_, b=b, t0=t0: tailO(n, pss, mixs, mb, x_, b, t0))
        pipeO.flush()
        S.barrier()

    def phase_U(l, src):
        last = (l == DEPTH - 1)
        AR.reset(base)
        hT = AR.alloc([128, 16, TA], BF16, "hTu")
        xb = [AR.alloc([128, 16, 256], F32, "xbu") for _ in range(2)]
        sq = [AR.alloc([128, 16, 256], BF16, "squ") for _ in range(2)]
        tmp = [AR.alloc([128, 256], F32, "tmpu") for _ in range(2)]
        cw_t = AR.alloc([128, NFC * 3], F32, "cw")
        cb_t = AR.alloc([128, NFC], F32, "cbt")
        wb = [AR.alloc([128, 2, 16, 128], BF16, "wup") for _ in range(2)]
        NT = T if last else TA
        gbuf = [AR.alloc([128, TA], F32, "gbuf") for _ in range(2)]
        vbuf = [AR.alloc([128, TA], BF16, "vbuf") for _ in range(2)]
        cbuf = [AR.alloc([128, TA], F32, "cbuf") for _ in range(2)]
        ast = [AR.alloc([128, TA], BF16, "ast") for _ in range(2)]
        DMA("sp", cw_t[:, :], cw[l], [], ["cw"])
        DMA("sp", cb_t[:, :], cbias[l], [], ["cbt"])
        blocks = [(i * 256, 256, 0) for i in range(8)]
        if not last:
            blocks.append((T, 256, 1))
        load_hT(hT, T if last else TA)
        chunks = [(0, 512), (512, 512), (1024, 512), (1536, 512)]
        segs = [(0, T)]
        if not last:
            chunks.append((T, 256))
            segs.append((T, TA))
        it = 0
        for fc in range(NFC):
            w = wb[fc % 2]
            wtok = ("wup", fc % 2)
            DMA("pq", w[:, :, :, :], wup[l, fc], [], [wtok])
            gb, vb, cbf, as_ = gbuf[fc % 2], vbuf[fc % 2], cbuf[fc % 2], ast[fc % 2]
            gt, vtk, ct, at = ("gbuf", fc % 2), ("vbuf", fc % 2), ("cbuf", fc % 2), ("ast", fc % 2)
            for (t0, n) in chunks:
                pa = it % 4
                pb = 4 + it % 4
                it += 1
                for k in range(16):
                    MM(ps[pa][:, :n], w[:, 0, k, :], hT[:, k, t0:t0 + n], k == 0, k == 15, [wtok] + hT_tokens(t0, n), [P(pa)])
                ACT(gb[:, t0:t0 + n], ps[pa][:, :n], AF.Copy, [P(pa)], [gt])
                for k in range(16):
                    MM(ps[pb][:, :n], w[:, 1, k, :], hT[:, k, t0:t0 + n], k == 0, k == 15, [wtok] + hT_tokens(t0, n), [P(pb)])
                CP(vb[:, t0:t0 + n], ps[pb][:, :n], [P(pb)], [vtk])
            c0 = cw_t[:, fc * 3 + 0:fc * 3 + 1]
            c1 = cw_t[:, fc * 3 + 1:fc * 3 + 2]
            c2 = cw_t[:, fc * 3 + 2:fc * 3 + 3]
            for (a, b) in segs:
                ACT(cbf[:, a:b], gb[:, a:b], AF.Identity, [gt, "cw", "cbt"], [ct], scale=c1, bias=cb_t[:, fc:fc + 1])
                STT(cbf[:, a + 1:b], gb[:, a:b - 1], c0, cbf[:, a + 1:b], ALU.mult, ALU.add, [gt, ct, "cw"], [ct])
                STT(cbf[:, a:b - 1], gb[:, a + 1:b], c2, cbf[:, a:b - 1], ALU.mult, ALU.add, [gt, ct, "cw"], [ct])
            if gelu_mode == "tanh_act":
                ACT(cbf[:, :NT], cbf[:, :NT], AF.Gelu_apprx_tanh, [ct], [ct])
            else:
                gsc = gb
                TT(gsc[:, :NT], cbf[:, :NT], cbf[:, :NT], ALU.mult, [ct], [gt])
                TS(gsc[:, :NT], gsc[:, :NT], 0.044715, 1.0, ALU.mult, ALU.add, [gt], [gt])
                TT(gsc[:, :NT], gsc[:, :NT], cbf[:, :NT], ALU.mult, [gt, ct], [gt])
                ACT(gsc[:, :NT], gsc[:, :NT], AF.Sigmoid, [gt], [gt], scale=float(2.0 * np.sqrt(2.0 / np.pi)))
                TT(cbf[:, :NT], cbf[:, :NT], gsc[:, :NT], ALU.mult, [gt, ct], [ct])
            TT(as_[:, :NT], cbf[:, :NT], vb[:, :NT], ALU.mult, [ct, vtk], [at])
            DMA("sp", ACTD[:, fc, :NT], as_[:, :NT], [at], [("ACTD", fc)])
        S.barrier()

    def phase_D(l, src, dst, dst_is_out):
        last = (l == DEPTH - 1)
        AR.reset(base)
        ab = AR.alloc([128, NFC, 768], BF16, "ab")
        ys = AR.alloc([128, 16, 768], F32, "ys")
        wb = [AR.alloc([128, NFC, 128], BF16, "wdn") for _ in range(2)]
        xall = AR.alloc([128, 16, 768], F32, "xall")
        sqd = [AR.alloc([128, 384], BF16, "sqd") for _ in range(2)]
        if NEXTMOD and l + 1 < nlayers:
            sqm = AR.alloc([128, 16, 128], BF16, "sqmd")
            hst = AR.alloc([128, 16, 128], BF16, "hstd")
        print("D arena slack", AR.hi - AR.off)
        blocks = [(0, 768), (768, 768), (1536, 512 if last else 768)]
        wi = 0

        def load_ab(bi):
            t0_, n_ = blocks[bi]
            for f0 in range(0, NFC, 11):
                DMA("sp", ab[:, f0:f0 + 11, :n_], ACTD[:, f0:f0 + 11, t0_:t0_ + n_],
                    [("ACTD", f) for f in range(f0, f0 + 11)], [("ab", f0)])

        def load_x(bi):
            t0_, n_ = blocks[bi]
            DMA("sp", xall[:, :, :n_], src[:, :, t0_:t0_ + n_], [("R", src.tensor.name)], ["xall"])

        pipeD = Pipe()
        load_ab(0)
        load_x(0)
        for bi, (t0, n) in enumerate(blocks):
            cn = n // 2
            it = 0
            rngs = []
            if t0 + n <= T:
                rngs.append((0, n, 0))
            else:
                if t0 < T:
                    rngs.append((0, T - t0, 0))
                rngs.append((max(T - t0, 0), n, 1))
            for dc in range(16):
                w = wb[wi % 2]
                wtok = ("wdn", wi % 2)
                wi += 1
                DMA("pq", w[:, :, :], wdn[l, dc], [], [wtok])
                for ci in range(2):
                    c0 = ci * cn
                    pa = it % 4
                    it += 1
                    for fc in range(NFC):
                        MM(ps[pa][:, :cn], w[:, fc, :], ab[:, fc, c0:c0 + cn], fc == 0, fc == NFC - 1,
                           [wtok, ("ab", (fc // 11) * 11)], [P(pa)])
                    pipeD.tick()
                    ACT(sqd[pa % 2][:, :cn], ps[pa][:, :cn], AF.Square, [P(pa)], [("sqd", pa % 2)])
                    for (u0, u1, s_) in rngs:
                        a0, a1 = max(u0, c0), min(u1, c0 + cn)
                        if a1 > a0:
                            ACT(ys[:, dc, a0:a1], ps[pa][:, a0 - c0:a1 - c0], AF.Identity, [P(pa), ("G2", l)], [("ys", dc)],
                                scale=G2[l][:, dc * 2 + s_:dc * 2 + s_ + 1])
                    pipeD.defer(1, lambda ci=ci, cn=cn, pa=pa, dc=dc: MM(ps[4 + ci][:, :cn], ones_t[:, :], sqd[pa % 2][:, :cn],
                                                               dc == 0, dc == 15, [("sqd", pa % 2), "ones"], [P(4 + ci)]))
            pipeD.flush()
            if bi + 1 < len(blocks):
                load_ab(bi + 1)
            for ci in range(2):
                rstd(ps[4 + ci][:, :cn], cn, 1.0 / D, [P(4 + ci)], off=ci * cn)
            rstoks = [("rs", 0), ("rs", cn)]
            for dc in range(16):
                TT(ys[:, dc, :n], ys[:, dc, :n], rs_t[:, :n], ALU.mult, [("ys", dc)] + rstoks, [("ys", dc)])
            for dc in range(16):
                TT(xall[:, dc, :n], ys[:, dc, :n], xall[:, dc, :n], ALU.add, [("ys", dc), "xall"], ["xall"])
            o = DMA("sp", dst[:, :, t0:t0 + n], xall[:, :, :n], ["xall"], [("R", dst.tensor.name)])
            if dst_is_out:
                finals.append(o)
            if NEXTMOD and l + 1 < nlayers:
                nsub = n // 128

                def submod(j, bi=bi, t0=t0, nsub=nsub):
                    u0 = j * 128
                    tg = t0 + u0
                    s_ = 1 if tg >= T else 0
                    ACT(sqm[:, :, :], xall[:, :, u0:u0 + 128], AF.Square, ["xall"], ["sqm"])

                    def sub2():
                        mod_stats_and_apply(l + 1, A1[l + 1], ("A1", l + 1), 0, xall[:, :, u0:u0 + 128], "xall",
                                            s_, tg, 128, sqm, hst, 6, 768)
                        if j + 1 < nsub:
                            submod(j + 1)
                        elif bi + 1 < len(blocks):
                            load_x(bi + 1)
                    pipeD.defer(1, sub2)
                pipeD.defer(4, lambda: submod(0))
            elif bi + 1 < len(blocks):
                load_x(bi + 1)
        pipeD.flush()
        S.barrier()

    finals = []

    def run_all():
        src = xt
        for l in range(nlayers):
            is_out = (l == DEPTH - 1)
            for nm, fn in (("A", lambda: phase_A(l, src)), ("F", lambda: phase_F(l)), ("T", lambda: phase_T(l)),
                           ("O", lambda: phase_O(l, src, R1)), ("U", lambda: phase_U(l, R1)),
                           ("D", lambda: phase_D(l, R1, out_d if is_out else R2, is_out))):
                fn()
                if stop in (nm, nm + str(l)) and (len(stop) == 2 or l == 0):
                    return
            src = R2

    run_all()
    if debug:
        dbgm = nc.dram_tensor("dbgm", [128, 192 + 4 * 32], F32, kind="ExternalOutput").ap()
        o = DMA("sp", dbgm[:, 0:192], modt[0][:, :], [("modtA", 0), ("modtB", 0)], ["dbgm"])
        finals.append(o)
        for i, tl in enumerate((A1, G1, A2, G2)):
            nm = ("A1", "G1", "A2", "G2")[i]
            finals.append(DMA("sp", dbgm[:, 192 + i * 32:192 + (i + 1) * 32], tl[0][:, :], [(nm, 0)], ["dbgm"]))
    S.emit(finals)
    return nc


def _bf16(a):
    return np.ascontiguousarray(a.astype(np.float32)).astype(ml_dtypes.bfloat16)


def _consts():
    c = {}
    c["ones"] = _bf16(np.ones((128, 128)))
    sw = np.zeros((128, 128), np.float32)
    for i in range(128):
        sw[i ^ 1, i] = 1.0
    c["swp"] = _bf16(sw)
    cidx = np.arange(128)[:, None].astype(np.float64)
    m = np.arange(128)[None, :].astype(np.float64)
    ang = 2 * np.pi * cidx * m / 128.0
    c["cs"] = _bf16(np.concatenate([np.cos(ang), -np.sin(ang)], axis=1) / np.sqrt(128.0))
    pos = np.arange(T)
    row = (pos // 64).astype(np.float32)
    col = (pos % 64).astype(np.float32)
    freqs = (np.float32(10000.0) ** (-np.arange(32, dtype=np.float32) / np.float32(32))).astype(np.float32)
    angp = np.concatenate([row[:, None] * freqs, col[:, None] * freqs], axis=-1).astype(np.float32)
    cosv = np.cos(angp).astype(np.float32)
    sinv = np.sin(angp).astype(np.float32)
    rc = np.zeros((128, T), np.float32)
    rsn = np.zeros((128, T), np.float32)
    for j in range(64):
        rc[2 * j] = cosv[:, j]
        rc[2 * j + 1] = cosv[:, j]
        rsn[2 * j] = -sinv[:, j]
        rsn[2 * j + 1] = sinv[:, j]
    c["ropec"] = rc
    c["ropes"] = rsn

    def dft(n):
        t = np.arange(n, dtype=np.int64)
        kt = (t[:, None] * t[None, :]) % n
        a = 2 * np.pi * kt.astype(np.float64) / n
        cm = np.cos(a) / np.sqrt(n)
        sm = np.sin(a) / np.sqrt(n)
        nt = n // 128
        cm = cm.reshape(nt, 128, n).transpose(1, 0, 2)
        sm = sm.reshape(nt, 128, n).transpose(1, 0, 2)
        return _bf16(cm), _bf16(sm)

    c["dftc"], c["dfts"] = dft(T)
    c["dftcc"], c["dftsc"] = dft(CTX)
    return c


def _fm(v):
    sh = v.shape
    n = sh[-1] // 128
    return np.ascontiguousarray(np.swapaxes(v.reshape(sh[:-1] + (n, 128)), -1, -2))


def _prep_shared(c_ctx, w_mod, b_mod, g_pre_mix, g_post_mix, g_pre_ffn, g_post_ffn, w_in, q_norm, k_norm,
                 w_four, g_attn_out, g_four_out, w_out, w_up, conv_w, conv_b, w_down):
    f = np.float32
    sh = {}
    sh["wmod"] = np.ascontiguousarray(w_mod.reshape(DEPTH, 16, 128, 48, 256).transpose(0, 3, 2, 1, 4))
    bm = _fm(b_mod)
    sh["bmod2"] = np.ascontiguousarray(np.repeat(bm, 2, axis=-1))
    gs = np.stack([_fm(g_pre_mix), _fm(g_post_mix), _fm(g_pre_ffn), _fm(g_post_ffn)], axis=1)
    sh["g2"] = np.ascontiguousarray(np.repeat(gs, 2, axis=-1))
    wi = w_in.reshape(DEPTH, 16, 128, 2560)
    cols = list(range(0, 1280, 128)) + list(range(1536, 2560, 128))
    sh["wqkf"] = np.ascontiguousarray(
        np.stack([wi[:, :, :, c0:c0 + 128] for c0 in cols], axis=1).transpose(0, 1, 3, 2, 4))
    sh["wv"] = np.ascontiguousarray(wi[:, :, :, 1280:1536].transpose(0, 2, 1, 3))
    sh["qkn"] = np.ascontiguousarray(np.stack([q_norm, k_norm], axis=-1))
    sh["w4"] = np.ascontiguousarray(w_four.transpose(0, 2, 1, 3))
    sh["gy"] = np.ascontiguousarray(np.concatenate([_fm(g_attn_out), _fm(g_four_out)], axis=-1))
    sh["wout"] = np.ascontiguousarray(w_out.reshape(DEPTH, 16, 128, D).transpose(0, 2, 1, 3))
    wu = w_up.reshape(DEPTH, 16, 128, 2, NFC, 128)
    sh["wup"] = np.ascontiguousarray(wu.transpose(0, 4, 2, 3, 1, 5))
    cwf = conv_w.reshape(DEPTH, 3, NFC, 128).transpose(0, 3, 2, 1)
    sh["cw"] = np.ascontiguousarray(cwf.reshape(DEPTH, 128, NFC * 3))
    sh["cb"] = _fm(conv_b)
    wd = w_down.reshape(DEPTH, NFC, 128, 16, 128)
    sh["wdn"] = np.ascontiguousarray(wd.transpose(0, 3, 2, 1, 4))
    for k in sh:
        assert sh[k].dtype == f, k
    sh.update(_consts())
    return sh


def _prep_core(b, x, c, ctx, c_ctx):
    xa = np.concatenate([x[b].T, ctx[b].T], axis=1)
    xt = np.ascontiguousarray(xa.reshape(16, 128, TA).transpose(1, 0, 2))
    cc = np.stack([_fm(c[b]), _fm(c_ctx)], axis=-1).reshape(128, 32)
    return {"xt": xt, "cc": np.ascontiguousarray(cc)}


_NC_CACHE = {}


def kernel(x, c, ctx, c_ctx, w_mod, b_mod, g_pre_mix, g_post_mix, g_pre_ffn, g_post_ffn,
           w_in, q_norm, k_norm, w_four, g_attn_out, g_four_out, w_out,
           w_up, conv_w, conv_b, w_down):
    args = [np.asarray(a, dtype=np.float32) for a in
            (x, c, ctx, c_ctx, w_mod, b_mod, g_pre_mix, g_post_mix, g_pre_ffn, g_post_ffn, w_in, q_norm, k_norm,
             w_four, g_attn_out, g_four_out, w_out, w_up, conv_w, conv_b, w_down)]
    (x, c, ctx, c_ctx, w_mod, b_mod, g_pre_mix, g_post_mix, g_pre_ffn, g_post_ffn, w_in, q_norm, k_norm,
     w_four, g_attn_out, g_four_out, w_out, w_up, conv_w, conv_b, w_down) = args
    shared = _prep_shared(c_ctx, w_mod, b_mod, g_pre_mix, g_post_mix, g_pre_ffn, g_post_ffn, w_in, q_norm, k_norm,
                          w_four, g_attn_out, g_four_out, w_out, w_up, conv_w, conv_b, w_down)
    in_maps = []
    for b in range(NCORES):
        m = dict(shared)
        m.update(_prep_core(b, x, c, ctx, c_ctx))
        in_maps.append(m)
    nc = build_nc()
    res = run_bass_kernel_spmd(nc, in_maps, core_ids=list(range(NCORES)))
    outs = []
    for b in range(NCORES):
        o = np.asarray(res.results[b]["out"], dtype=np.float32)
        outs.append(o.transpose(1, 0, 2).reshape(D, T).T)
    return np.ascontiguousarray(np.stack(outs, axis=0)).astype(np.float32)
```

```python
import contextlib
import numpy as np
import ml_dtypes
import concourse.bass as bass
import concourse.mybir as mybir
from concourse.bass_utils import run_bass_kernel_spmd

F32 = mybir.dt.float32
BF16 = mybir.dt.bfloat16
AF = mybir.ActivationFunctionType
ALU = mybir.AluOpType

D = 2048
T = 2048
CTX = 256
TA = T + CTX
DEPTH = 2
FFN = 5632
NFC = FFN // 128
EPS = 1e-6
NCORES = 8

COMPUTE = ("pe", "act", "dve", "pool")
DMAQ = ("sp", "pq")
NSLOT = 8
NEXTMOD = True


class Op:
    __slots__ = ("eng", "idx", "fn", "waits", "signal", "is_dma", "slot", "slot_val", "sigcount")

    def __init__(self, eng, idx, fn, is_dma):
        self.eng = eng
        self.idx = idx
        self.fn = fn
        self.waits = []
        self.signal = False
        self.is_dma = is_dma
        self.slot = None
        self.slot_val = None
        self.sigcount = None


class Sched:
    def __init__(self, nc):
        self.nc = nc
        self.streams = {e: [] for e in COMPUTE + ("sp",)}
        self.last_writer = {}
        self.readers = {}
        self.waited = {e: {} for e in COMPUTE + ("sp",)}
        self.waited_dma = {e: set() for e in COMPUTE + ("sp",)}
        self.dma_count = {q: 0 for q in DMAQ}
        self.dma_ops = {q: [] for q in DMAQ}
        self.last_compute = {e: None for e in COMPUTE}

    @staticmethod
    def _stream_of(eng):
        return "pool" if eng == "pq" else eng

    def op(self, eng, fn, reads=(), writes=(), relax=False):
        sname = self._stream_of(eng)
        stream = self.streams[sname]
        is_dma = eng in DMAQ
        o = Op(eng, len(stream), fn, is_dma)
        writes = list(writes) + [t for t in reads if isinstance(t, tuple) and t[0] == "ps" and t not in writes]
        deps = []
        for t in reads:
            w = self.last_writer.get(t)
            if w is not None:
                deps.append((w, "raw"))
        for t in writes:
            w = self.last_writer.get(t)
            if w is not None:
                deps.append((w, "waw"))
            for r in self.readers.get(t, ()):
                deps.append((r, "war"))
        if is_dma:
            n = self.dma_count[eng]
            o.slot = n % NSLOT
            o.slot_val = 16 * (n // NSLOT + 1)
            if n >= NSLOT:
                deps.append((self.dma_ops[eng][n - NSLOT], "slot"))
            self.dma_count[eng] = n + 1
            self.dma_ops[eng].append(o)
        best = {}
        for d, kind in deps:
            if d is o:
                continue
            if d.is_dma:
                self._add_wait(o, sname, d, kind)
                continue
            if d.eng == sname and not is_dma:
                if sname == "pe" or kind == "slot":
                    continue
                if relax and kind != "raw":
                    continue
            cur = best.get(d.eng)
            if cur is None or d.idx > cur.idx:
                best[d.eng] = d
        for d in best.values():
            self._add_wait(o, sname, d, "raw")
        stream.append(o)
        if not is_dma and eng in COMPUTE:
            self.last_compute[eng] = o
        for t in reads:
            self.readers.setdefault(t, []).append(o)
        for t in writes:
            self.last_writer[t] = o
            self.readers[t] = []
        return o

    def _add_wait(self, o, sname, d, kind):
        if d.is_dma:
            if d in self.waited_dma[sname]:
                return
            self.waited_dma[sname].add(d)
            o.waits.append(d)
            return
        dname = d.eng
        if dname == sname and not o.is_dma:
            if sname == "pe":
                return
            if kind == "slot":
                return
        if self.waited[sname].get(dname, -1) >= d.idx:
            return
        self.waited[sname][dname] = d.idx
        d.signal = True
        o.waits.append(d)

    def barrier(self):
        lasts = [o for o in self.last_compute.values() if o is not None]
        dmas = []
        for q in DMAQ:
            dmas += self.dma_ops[q][-NSLOT:]
        for sname in list(self.streams.keys()):
            stream = self.streams[sname]
            o = Op(sname, len(stream), lambda e: e.nop(), False)
            o.signal = False
            for d in lasts:
                if d.eng != sname:
                    self._add_wait(o, sname, d, "raw")
            for d in dmas:
                self._add_wait(o, sname, d, "raw")
            stream.append(o)
            o.eng = "__nop__"

    def emit(self, final_ops):
        nc = self.nc
        with contextlib.ExitStack() as st:
            sems = {e: st.enter_context(nc.semaphore("s_" + e)) for e in COMPUTE}
            dsems = {q: [st.enter_context(nc.semaphore(f"d_{q}{i}")) for i in range(NSLOT)] for q in DMAQ}
            for e in COMPUTE:
                c = 0
                for o in self.streams[e]:
                    if o.is_dma or o.eng == "__nop__":
                        continue
                    if o.signal:
                        c += 1
                        o.sigcount = c
            block = st.enter_context(nc.Block())
            engmap = {"pe": block.tensor, "act": block.scalar, "dve": block.vector,
                      "pool": block.gpsimd, "sp": block.sync}

            def make(sname):
                def body(eng):
                    for o in self.streams[sname]:
                        for d in o.waits:
                            if d.is_dma:
                                eng.wait_ge(dsems[d.eng][d.slot], d.slot_val)
                            else:
                                eng.wait_ge(sems[d.eng], d.sigcount)
                        ins = o.fn(eng)
                        if o.is_dma:
                            ins.then_inc(dsems[o.eng][o.slot], 16)
                        elif o.signal:
                            ins.then_inc(sems[o.eng], 1)
                    for d in final_ops:
                        if self._stream_of(d.eng) == sname:
                            eng.wait_ge(dsems[d.eng][d.slot], d.slot_val)
                return body

            for sname in ("pe", "act", "dve", "pool", "sp"):
                engmap[sname](make(sname))


class Arena:
    def __init__(self, nc, lo, hi):
        self.nc = nc
        self.lo = lo
        self.hi = hi
        self.off = lo
        self.n = 0

    def mark(self):
        return self.off

    def reset(self, to):
        self.off = to

    def alloc(self, shape, dtype, name="t"):
        size = int(np.prod(shape[1:])) * (4 if dtype == F32 else 2)
        off = (self.off + 63) // 64 * 64
        assert off + size <= self.hi, f"SBUF arena overflow: {name} {shape} off={off} size={size} hi={self.hi}"
        self.off = off + size
        self.n += 1
        return self.nc.alloc_sbuf_tensor_at(f"{name}_{self.n}", list(shape), dtype, offset=off)


class Pipe:
    def __init__(self):
        self.q = []

    def defer(self, delay, fn):
        self.q.append([delay, fn])

    def tick(self):
        cur, self.q = self.q, []
        for item in cur:
            item[0] -= 1
            if item[0] <= 0:
                item[1]()
            else:
                self.q.append(item)

    def flush(self):
        while self.q:
            self.tick()


def build_nc(nlayers=DEPTH, debug=False, gelu_mode="tanh_act", stop=None):
    nc = bass.Bass("TRN2", target_bir_lowering=False)

    def din(name, shape, dt=F32):
        return nc.dram_tensor(name, list(shape), dt, kind="ExternalInput").ap()

    def dscr(name, shape, dt):
        if debug:
            return nc.dram_tensor(name, list(shape), dt, kind="ExternalOutput").ap()
        return nc.dram_tensor(name, list(shape), dt).ap()

    xt = din("xt", [128, 16, TA])
    cc = din("cc", [128, 32])
    wmod = din("wmod", [DEPTH, 48, 128, 16, 256])
    bmod2 = din("bmod2", [DEPTH, 128, 192])
    g2 = din("g2", [DEPTH, 4, 128, 32])
    wqkf = din("wqkf", [DEPTH, 18, 128, 16, 128])
    wv = din("wv", [DEPTH, 128, 16, 256])
    qkn = din("qkn", [DEPTH, 128, 2])
    w4 = din("w4", [DEPTH, 128, 8, 128])
    gy = din("gy", [DEPTH, 128, 16])
    wout = din("wout", [DEPTH, 128, 16, D])
    wup = din("wup", [DEPTH, NFC, 128, 2, 16, 128])
    cw = din("cw", [DEPTH, 128, NFC * 3])
    cbias = din("cb", [DEPTH, 128, NFC])
    wdn = din("wdn", [DEPTH, 16, 128, NFC, 128])
    ones_d = din("ones", [128, 128], BF16)
    swp_d = din("swp", [128, 128], BF16)
    cs_d = din("cs", [128, 256], BF16)
    ropec_d = din("ropec", [128, T])
    ropes_d = din("ropes", [128, T])
    dftc_d = din("dftc", [128, 16, T], BF16)
    dfts_d = din("dfts", [128, 16, T], BF16)
    dftcc_d = din("dftcc", [128, 2, CTX], BF16)
    dftsc_d = din("dftsc", [128, 2, CTX], BF16)
    out_d = nc.dram_tensor("out", [128, 16, T], F32, kind="ExternalOutput").ap()

    R1 = dscr("R1", [128, 16, TA], F32)
    R2 = dscr("R2", [128, 16, TA], F32)
    QK = dscr("QK", [128, 10, TA], BF16)
    VD = dscr("VD", [128, 18, 256], BF16)
    GCS = dscr("GCS", [128, 8, 18, 256], BF16)
    YT = dscr("YT", [128, 16, TA], BF16)
    ACTD = dscr("ACTD", [128, NFC, TA], BF16)
    HT = dscr("HT", [128, 16, TA], BF16)

    S = Sched(nc)
    lo = (int(nc.sbuf_base) + 63) // 64 * 64
    hi = int(nc.sbuf_top)
    AR = Arena(nc, lo, hi)
    ps = [nc.alloc_psum_tensor(f"psb{i}", [128, 512], F32) for i in range(8)]

    def P(i):
        return ("ps", i)

    def MM(out, lhsT, rhs, start, stop, r, w):
        return S.op("pe", lambda e: e.matmul(out, lhsT, rhs, start=start, stop=stop), r, w)

    def ACT(out, in_, func, r, w, relax=False, **kw):
        return S.op("act", lambda e: e.activation(out=out, in_=in_, func=func, **kw), r, w, relax=relax)

    def TS(out, in0, s1, s2, op0, op1, r, w):
        if op1 is None:
            return S.op("dve", lambda e: e.tensor_scalar(out=out, in0=in0, scalar1=s1, scalar2=None, op0=op0), r, w)
        return S.op("dve", lambda e: e.tensor_scalar(out=out, in0=in0, scalar1=s1, scalar2=s2, op0=op0, op1=op1), r, w)

    def STT(out, in0, scalar, in1, op0, op1, r, w, relax=False):
        return S.op("dve", lambda e: e.scalar_tensor_tensor(out=out, in0=in0, scalar=scalar, in1=in1, op0=op0, op1=op1), r, w, relax=relax)

    def TT(out, in0, in1, op, r, w):
        return S.op("dve", lambda e: e.tensor_tensor(out=out, in0=in0, in1=in1, op=op), r, w)

    def CP(out, in_, r, w):
        return S.op("dve", lambda e: e.tensor_copy(out=out, in_=in_), r, w)

    def RCP(out, in_, r, w):
        return S.op("dve", lambda e: e.reciprocal(out=out, in_=in_), r, w)

    def DMA(q, out, in_, r, w):
        return S.op(q, lambda e: e.dma_start(out=out, in_=in_), r, w)

    uid = [0]

    def tok(name):
        uid[0] += 1
        return (name, uid[0])

    ones_t = AR.alloc([128, 128], BF16, "ones")
    swp_t = AR.alloc([128, 128], BF16, "swp")
    cs_t = AR.alloc([128, 256], BF16, "cs")
    modt = [AR.alloc([128, 192], F32, "modt") for _ in range(DEPTH)]
    A1 = [AR.alloc([128, 32], F32, "A1") for _ in range(DEPTH)]
    G1 = [AR.alloc([128, 32], F32, "G1") for _ in range(DEPTH)]
    A2 = [AR.alloc([128, 32], F32, "A2") for _ in range(DEPTH)]
    G2 = [AR.alloc([128, 32], F32, "G2") for _ in range(DEPTH)]
    qkn_t = [AR.alloc([128, 2], F32, "qkn") for _ in range(DEPTH)]
    gy_t = [AR.alloc([128, 16], F32, "gy") for _ in range(DEPTH)]
    silu_t = AR.alloc([128, 32], BF16, "silu")
    bm_t = [AR.alloc([128, 192], F32, "bm") for _ in range(DEPTH)]
    g2_t = [[AR.alloc([128, 32], F32, "g2") for _ in range(4)] for _ in range(DEPTH)]
    rs_t = AR.alloc([128, 1024], F32, "rs")
    base = AR.mark()

    DMA("sp", ones_t[:, :], ones_d[:, :], [], ["ones"])
    DMA("sp", swp_t[:, :], swp_d[:, :], [], ["swp"])
    DMA("sp", cs_t[:, :], cs_d[:, :], [], ["cs"])
    for l in range(DEPTH):
        DMA("sp", qkn_t[l][:, :], qkn[l], [], [("qkn", l)])
        DMA("sp", gy_t[l][:, :], gy[l], [], [("gy", l)])

    def rstd(ps_ap, n, inv_count, r, off=0):
        ACT(rs_t[:, off:off + n], ps_ap, AF.Ln, list(r) + ["eps"], [("rs", off)], scale=inv_count, bias=eps_t[:, 0:1])
        ACT(rs_t[:, off:off + n], rs_t[:, off:off + n], AF.Exp, [("rs", off)], [("rs", off)], scale=-0.5)

    eps_t = AR.alloc([128, 1], F32, "eps")
    base = AR.mark()
    S.op("dve", lambda e: e.memset(eps_t[:, :], EPS), [], ["eps"])

    class ModBG:
        def __init__(self):
            self.wt = None
            self.cnt = 0
            self.todo = []

        def setup(self, wt):
            self.wt = wt

        def load(self, l, it_):
            nb_ = len(self.wt)
            w = self.wt[self.cnt % nb_]
            wtok = ("wm", id(w))
            self.cnt += 1
            DMA("pq", w[:, :, :], wmod[l, it_], [], [wtok])
            return (w, wtok, it_)

        def compute(self, pend):
            w, wtok, it_ = pend
            for m in range(2):
                j = it_ * 2 + m
                for k in range(16):
                    MM(ps[7][:, j * 2:j * 2 + 2], w[:, k, m * 128:(m + 1) * 128], silu_t[:, k * 2:k * 2 + 2],
                       k == 0, k == 15, [wtok, "silu"], [P(7)])

        def item(self, l, it_):
            self.compute(self.load(l, it_))

        def evac_A(self, l):
            TT(modt[l][:, 0:64], ps[7][:, 0:64], bm_t[l][:, 0:64], ALU.add, [P(7), ("bm", l)], [("modtA", l)])
            STT(A1[l][:, :], modt[l][:, 32:64], 1.0, g2_t[l][0][:, :], ALU.add, ALU.mult,
                [("modtA", l), ("g2", l, 0)], [("A1", l)])

        def evac_B(self, l):
            TT(modt[l][:, 64:192], ps[7][:, 64:192], bm_t[l][:, 64:192], ALU.add, [P(7), ("bm", l)], [("modtB", l)])
            TT(G1[l][:, :], modt[l][:, 64:96], g2_t[l][1][:, :], ALU.mult, [("modtB", l), ("g2", l, 1)], [("G1", l)])
            STT(A2[l][:, :], modt[l][:, 128:160], 1.0, g2_t[l][2][:, :], ALU.add, ALU.mult,
                [("modtB", l), ("g2", l, 2)], [("A2", l)])
            TT(G2[l][:, :], modt[l][:, 160:192], g2_t[l][3][:, :], ALU.mult, [("modtB", l), ("g2", l, 3)], [("G2", l)])

        def plan_background(self):
            self.pending = []
            for it_ in range(16, 48):
                self.todo.append(("item", 0, it_))
            self.todo.append(("fn", lambda: self.evac_B(0)))
            for l in range(1, nlayers):
                for it_ in range(48):
                    self.todo.append(("item", l, it_))
                self.todo.append(("fn", lambda l=l: (self.evac_A(l), self.evac_B(l))))

        def step(self, n=1):
            depth = len(self.wt)
            for _ in range(n):
                if len(self.pending) >= depth:
                    self.compute(self.pending.pop(0))
                if self.todo:
                    e = self.todo.pop(0)
                    if e[0] == "item":
                        self.pending.append(self.load(e[1], e[2]))
                    else:
                        self.drain_pending()
                        e[1]()
                elif self.pending:
                    self.compute(self.pending.pop(0))

        def drain_pending(self):
            while self.pending:
                self.compute(self.pending.pop(0))

        def flush(self):
            while self.todo or self.pending:
                self.step(1)

    modbg = ModBG()

    def mod_prologue(wt):
        cc_t = AR.alloc([128, 32], F32, "cc")
        DMA("sp", cc_t[:, :], cc[:, :], [], ["cc"])
        ACT(silu_t[:, :], cc_t[:, :], AF.Silu, ["cc"], ["silu"])
        for l in range(nlayers):
            DMA("sp", bm_t[l][:, :], bmod2[l], [], [("bm", l)])
            for i in range(4):
                DMA("sp", g2_t[l][i][:, :], g2[l, i], [], [("g2", l, i)])
        modbg.setup(wt)
        for it_ in range(16):
            modbg.item(0, it_)
        modbg.evac_A(0)
        modbg.plan_background()

    def modulate(l, src, Avec, sh_off, hT, blocks, xb, sq, tmp, after_block=None):
        atok = ("A1" if Avec is A1[l] else "A2", l)
        mtok = ("modtA" if sh_off == 0 else "modtB", l)
        for bi, (t0, n, s) in enumerate(blocks):
            x_ = xb[bi % len(xb)]
            xtok = ("xb", bi % len(xb))
            DMA("sp", x_[:, :, :n], src[:, :, t0:t0 + n], [("R", src.tensor.name)], [xtok])
            sq_ = sq[bi % len(sq)]
            sqtok = ("sq", bi % len(sq))
            ACT(sq_[:, :, :n], x_[:, :, :n], AF.Square, [xtok], [sqtok])
            for k in range(16):
                MM(ps[6][:, :n], ones_t[:, :], sq_[:, k, :n], k == 0, k == 15, [sqtok, "ones"], [P(6)])
            ro = (bi % 2) * 512
            rstd(ps[6][:, :n], n, 1.0 / D, [P(6)], off=ro)
            TT(x_[:, :, :n], x_[:, :, :n], rs_t[:, ro:ro + n].unsqueeze(1).broadcast_to([128, 16, n]), ALU.mult,
               [xtok, ("rs", ro)], [xtok])
            for k in range(16):
                ACT(hT[:, k, t0:t0 + n], x_[:, k, :n], AF.Identity, [xtok, atok, mtok], [("hT", t0 // 256)], relax=True,
                    scale=Avec[:, k * 2 + s:k * 2 + s + 1], bias=modt[l][:, sh_off + k * 2 + s:sh_off + k * 2 + s + 1])
            if after_block is not None:
                after_block()

    def hT_tokens(t0, n):
        return [("hT", b) for b in range(t0 // 256, (t0 + n + 255) // 256)]

    def mod_stats_and_apply(lm, Avec, atok, sh_off, x3, xtok, s, t0, n, sqm, hst, bank, ro):
        mtok = ("modtA" if sh_off == 0 else "modtB", lm)
        for k in range(16):
            MM(ps[bank][:, :n], ones_t[:, :], sqm[:, k, :n], k == 0, k == 15, ["sqm", "ones"], [P(bank)])
        rstd(ps[bank][:, :n], n, 1.0 / D, [P(bank)], off=ro)
        TT(x3, x3, rs_t[:, ro:ro + n].unsqueeze(1).broadcast_to([128, 16, n]), ALU.mult, [xtok, ("rs", ro)], [xtok])
        for k in range(16):
            ACT(hst[:, k, :n], x3[:, k, :], AF.Identity, [xtok, atok, mtok], ["hst"], relax=True,
                scale=Avec[:, k * 2 + s:k * 2 + s + 1], bias=modt[lm][:, sh_off + k * 2 + s:sh_off + k * 2 + s + 1])
        DMA("sp", HT[:, :, t0:t0 + n], hst[:, :, :n], ["hst"], ["HT"])

    def load_hT(hT, ntok):
        for c0 in range(0, ntok, 768):
            c1 = min(c0 + 768, ntok)
            DMA("sp", hT[:, :, c0:c1], HT[:, :, c0:c1], ["HT"], [("hT", b) for b in range(c0 // 256, c1 // 256)])

    def phase_A(l, src):
        last = (l == DEPTH - 1)
        AR.reset(base)
        hT = AR.alloc([128, 16, TA], BF16, "hT")
        xb = [AR.alloc([128, 16, 256], F32, "xb") for _ in range(2)]
        sq = [AR.alloc([128, 16, 256], BF16, "sq")]
        if l == 0:
            wm_t = [AR.alloc([128, 16, 256], BF16, "wm") for _ in range(2)]
        ropec = AR.alloc([128, T], F32, "ropec")
        ropes = AR.alloc([128, T], F32, "ropes")
        wb = [AR.alloc([128, 16, 128], BF16, "wqkf") for _ in range(3)]
        wv_t = AR.alloc([128, 16, 256], BF16, "wv")
        sqh = [AR.alloc([128, 512], BF16, "sqh") for _ in range(2)]
        qn = [AR.alloc([128, 512], BF16, "qn") for _ in range(2)]
        t1 = [AR.alloc([128, 512], F32, "t1") for _ in range(2)]
        t2 = [AR.alloc([128, 512], F32, "t2") for _ in range(2)]
        qo = [AR.alloc([128, 512], BF16, "qo") for _ in range(2)]
        fT = [AR.alloc([128, 512], BF16, "fT") for _ in range(2)]
        gst = [AR.alloc([128, 4, 256], BF16, "gst") for _ in range(2)]
        vst = AR.alloc([128, 18, 256], BF16, "vst")

        DMA("sp", ropec[:, :], ropec_d[:, :], [], ["ropec"])
        DMA("sp", ropes[:, :], ropes_d[:, :], [], ["ropes"])
        blocks = [(i * 256, 256, 0) for i in range(8)] + [(T, 256, 1)]
        keep = 41 if nlayers > 1 else 0
        if l == 0:
            mod_prologue(wm_t)
            modulate(l, src, A1[l], 0, hT, blocks, xb, sq, None, after_block=lambda: modbg.step(2 if len(modbg.todo) > keep + 2 else 0))
        elif NEXTMOD:
            load_hT(hT, TA)
        else:
            modulate(l, src, A1[l], 0, hT, blocks, xb, sq, None)
        def bgstep():
            if l == 0 and len(modbg.todo) > keep:
                modbg.step(1)

        chunks = [(0, 512), (512, 512), (1024, 512), (1536, 512), (T, 256)]
        it = 0
        DMA("pq", wv_t[:, :, :], wv[l], [], ["wv"])
        pipe = Pipe()

        def qk_stage2(i, pa, cb, kind, t0, n, is_ctx):
            pb = 3 + i % 2
            ro = (i % 2) * 512
            gcol = 0 if kind == "q" else 1
            b2 = i % 2
            MM(ps[pb][:, :n], ones_t[:, :], sqh[b2][:, :n], True, True, [("sqh", b2), "ones"], [P(pb)])
            rstd(ps[pb][:, :n], n, 1.0 / 128, [P(pb)], off=ro)
            dst = qn[b2] if not is_ctx else qo[b2]
            dtok = ("qn", b2) if not is_ctx else ("qo", b2)
            STT(dst[:, :n], ps[pa][:, :n], qkn_t[l][:, gcol:gcol + 1], rs_t[:, ro:ro + n], ALU.mult, ALU.mult,
                [P(pa), ("qkn", l), ("rs", ro)], [dtok])
            if is_ctx:
                DMA("sp", QK[:, cb, t0:t0 + n], qo[b2][:, :n], [("qo", b2)], [("QK", cb)])
            else:
                pipe.defer(1, lambda: qk_stage3(i, cb, t0, n))

        def qk_stage3(i, cb, t0, n):
            b2 = i % 2
            pc = 5 + i % 2
            MM(ps[pc][:, :n], swp_t[:, :], qn[b2][:, :n], True, True, [("qn", b2), "swp"], [P(pc)])
            TT(t1[b2][:, :n], qn[b2][:, :n], ropec[:, t0:t0 + n], ALU.mult, [("qn", b2), "ropec"], [("t1", b2)])
            TT(t2[b2][:, :n], ps[pc][:, :n], ropes[:, t0:t0 + n], ALU.mult, [P(pc), "ropes"], [("t2", b2)])
            TT(qo[b2][:, :n], t1[b2][:, :n], t2[b2][:, :n], ALU.add, [("t1", b2), ("t2", b2)], [("qo", b2)])
            DMA("sp", QK[:, cb, t0:t0 + n], qo[b2][:, :n], [("qo", b2)], [("QK", cb)])

        def f_stage2(i, g, t0, n):
            b2 = i % 2
            nt = n // 128
            for j in range(nt):
                pb = (3 + i % 2) if j < 2 else (5 + i % 2)
                jj = j % 2
                MM(ps[pb][:, jj * 256:(jj + 1) * 256], fT[b2][:, j * 128:(j + 1) * 128], cs_t[:, :], True, True,
                   [("fT", b2), "cs"], [P(pb)])
            CP(gst[b2][:, 0:2, :], ps[3 + i % 2][:, :].rearrange("p (a b) -> p a b", a=2), [P(3 + i % 2)], [("gst", b2)])
            if nt > 2:
                CP(gst[b2][:, 2:4, :], ps[5 + i % 2][:, :].rearrange("p (a b) -> p a b", a=2), [P(5 + i % 2)], [("gst", b2)])
            DMA("sp", GCS[:, g, t0 // 128:t0 // 128 + nt, :], gst[b2][:, 0:nt, :], [("gst", b2)], [("GCS", g)])

        for cb in range(18):
            w = wb[cb % 3]
            wtok = ("wqkf", cb % 3)
            DMA("pq", w[:, :, :], wqkf[l, cb], [], [wtok])
            kind = "q" if cb < 8 else ("k" if cb < 10 else "f")
            for (t0, n) in chunks:
                is_ctx = t0 >= T
                if is_ctx and last and kind != "k":
                    continue
                i = it
                pa = it % 3
                it += 1
                for k in range(16):
                    MM(ps[pa][:, :n], w[:, k, :], hT[:, k, t0:t0 + n], k == 0, k == 15,
                       [wtok] + hT_tokens(t0, n), [P(pa)])
                pipe.tick()
                bgstep()
                if kind in ("q", "k"):
                    ACT(sqh[i % 2][:, :n], ps[pa][:, :n], AF.Square, [P(pa)], [("sqh", i % 2)])
                    pipe.defer(1, lambda i=i, pa=pa, cb=cb, kind=kind, t0=t0, n=n, is_ctx=is_ctx:
                               qk_stage2(i, pa, cb, kind, t0, n, is_ctx))
                else:
                    ACT(fT[i % 2][:, :n], ps[pa][:, :n], AF.Copy, [P(pa)], [("fT", i % 2)])
                    pipe.defer(1, lambda i=i, g=cb - 10, t0=t0, n=n: f_stage2(i, g, t0, n))
        pipe.flush()
        if l == 0:
            while len(modbg.todo) > keep:
                modbg.step(1)
            modbg.drain_pending()
        for tt in range(18):
            pa = tt % 2
            for k in range(16):
                MM(ps[pa][:, 0:256], hT[:, k, tt * 128:(tt + 1) * 128], wv_t[:, k, :], k == 0, k == 15,
                   ["wv"] + hT_tokens(tt * 128, 128), [P(pa)])
            ACT(vst[:, tt, :], ps[pa][:, 0:256], AF.Copy, [P(pa)], ["vst"], relax=True)
        DMA("sp", VD[:, :, :], vst[:, :, :], ["vst"], ["VD"])
        S.barrier()

    def phase_F(l):
        last = (l == DEPTH - 1)
        AR.reset(base)
        gcs = AR.alloc([128, 8, 18, 256], BF16, "gcs")
        tab = [AR.alloc([128, 2, 16, 512], BF16, "tab") for _ in range(2)]
        tabc = AR.alloc([128, 2, 2, 256], BF16, "tabc")
        w4_t = AR.alloc([128, 8, 128], BF16, "w4")
        if l == 0 and nlayers > 1:
            fo = [AR.alloc([128, 8, 512], F32, "fo")] * 2
        else:
            fo = [AR.alloc([128, 8, 512], F32, "fo") for _ in range(2)]
        yst = AR.alloc([128, 8, 512], BF16, "yst")
        FTb = [AR.alloc([128, 512], BF16, "FT") for _ in range(2)]
        sqf = [AR.alloc([128, 512], BF16, "sqf") for _ in range(2)]
        DMA("pq", w4_t[:, :, :], w4[l], [], ["w4"])
        for g in range(8):
            DMA("sp", gcs[:, g, :, :], GCS[:, g, :, :], [("GCS", g)], [("gcs", g)])
        DMA("sp", tabc[:, 0, :, :], dftcc_d[:, :, :], [], ["tabc"])
        DMA("sp", tabc[:, 1, :, :], dftsc_d[:, :, :], [], ["tabc"])
        passes = [(kc * 512, 512, 16, 0) for kc in range(4)]
        if not last:
            passes.append((T, 256, 2, 16))
        def load_tab(pi):
            k0_, n_ = passes[pi][0], passes[pi][1]
            if k0_ >= T:
                return
            DMA("sp", tab[pi % 2][:, 0, :, :], dftc_d[:, :, k0_:k0_ + n_], [], [("tab", pi % 2)])
            DMA("sp", tab[pi % 2][:, 1, :, :], dfts_d[:, :, k0_:k0_ + n_], [], [("tab", pi % 2)])

        pipeF = Pipe()
        if l == 0 and modbg.todo:
            modbg.setup([AR.alloc([128, 16, 256], BF16, "wmF") for _ in range(2)])

        def f_stage2(g, pa, n, fo_, pi):
            MM(ps[2 + pa][:, :n], w4_t[:, g, :], FTb[pa][:, :n], True, True, [("FT", pa), "w4"], [P(2 + pa)])
            ACT(sqf[pa][:, :n], ps[2 + pa][:, :n], AF.Square, [P(2 + pa)], [("sqf", pa)])
            ACT(fo_[:, g, :n], ps[2 + pa][:, :n], AF.Copy, [P(2 + pa)], [("fo", id(fo_))])
            pipeF.defer(1, lambda: MM(ps[4 + pi % 2][:, :n], ones_t[:, :], sqf[pa][:, :n], g == 0, g == 7,
                                      [("sqf", pa), "ones"], [P(4 + pi % 2)]))

        load_tab(0)
        for pi, (k0, n, ntt, tt0) in enumerate(passes):
            is_ctx = k0 >= T
            if pi + 1 < len(passes):
                load_tab(pi + 1)
            tb = tab[pi % 2]
            tbtok = ("tab", pi % 2)
            fo_ = fo[pi % 2]
            for g in range(8):
                pa = g % 2
                for ti in range(ntt):
                    if is_ctx:
                        rc, rsn, rtok = tabc[:, 0, ti, :], tabc[:, 1, ti, :], "tabc"
                    else:
                        rc, rsn, rtok = tb[:, 0, ti, :], tb[:, 1, ti, :], tbtok
                    MM(ps[pa][:, :n], gcs[:, g, tt0 + ti, 0:128], rc, ti == 0, False, [("gcs", g), rtok], [P(pa)])
                    MM(ps[pa][:, :n], gcs[:, g, tt0 + ti, 128:256], rsn, False, ti == ntt - 1, [("gcs", g), rtok], [P(pa)])
                pipeF.tick()
                if l == 0:
                    modbg.step(1)
                ACT(FTb[pa][:, :n], ps[pa][:, :n], AF.Copy, [P(pa)], [("FT", pa)])
                pipeF.defer(1, lambda g=g, pa=pa, n=n, fo_=fo_, pi=pi: f_stage2(g, pa, n, fo_, pi))
            pipeF.flush()
            rstd(ps[4 + pi % 2][:, :n], n, 1.0 / 1024, [P(4 + pi % 2)])
            for g in range(8):
                STT(yst[:, g, :n], fo_[:, g, :n], gy_t[l][:, 8 + g:9 + g], rs_t[:, :n], ALU.mult, ALU.mult,
                    [("fo", id(fo_)), ("gy", l), ("rs", 0)], ["yst"], relax=True)
            DMA("sp", YT[:, 8:16, k0:k0 + n], yst[:, :, :n], ["yst"], [("YT", "f")])
        if l == 0:
            modbg.flush()
        S.barrier()

    def phase_T(l):
        last = (l == DEPTH - 1)
        AR.reset(base)
        kT = AR.alloc([128, 2, TA], BF16, "kT")
        vt = AR.alloc([128, 18, 256], BF16, "vt")
        qT = AR.alloc([128, 8, TA], BF16, "qT")
        pT = [[AR.alloc([128, 512], BF16, "pT") for _ in range(2)] for _ in range(2)]
        ao = [AR.alloc([128, 8, 512], F32, "ao") for _ in range(2)]
        sqa = AR.alloc([128, 8, 512], BF16, "sqa")
        rd = [AR.alloc([128, 512], F32, "rd") for _ in range(2)]
        yst = AR.alloc([128, 8, 512], BF16, "ysta")
        DMA("sp", kT[:, :, :], QK[:, 8:10, :], [("QK", 8), ("QK", 9)], ["kT"])
        DMA("sp", vt[:, :, :], VD[:, :, :], ["VD"], ["vt"])
        for h in range(8):
            DMA("sp", qT[:, h, :], QK[:, h, :], [("QK", h)], [("qT", h)])
        scale = float(1.0 / np.sqrt(128.0))
        qchunks = [(i * 512, 512, list(range(18))) for i in range(4)]
        if not last:
            qchunks.append((T, 256, [16, 17]))
        for qi, (t0, n, kts) in enumerate(qchunks):
            ao_ = ao[qi % 2]
            nk = len(kts)
            for hp in range(4):
                heads = (2 * hp, 2 * hp + 1)
                kvh = heads[0] // 4

                def s_step(w, i):
                    h = heads[w]
                    st_ = kts[i]
                    bank = 2 * w + i % 2
                    MM(ps[bank][:, :n], kT[:, kvh, st_ * 128:(st_ + 1) * 128], qT[:, h, t0:t0 + n], True, True,
                       ["kT", ("qT", h)], [P(bank)])
                    ACT(pT[w][i % 2][:, :n], ps[bank][:, :n], AF.Exp, [P(bank)], [("pT", w, i % 2)], scale=scale)

                def pv_step(w, i):
                    st_ = kts[i]
                    po, pd = 4 + 2 * w, 5 + 2 * w
                    MM(ps[po][:, :n], vt[:, st_, kvh * 128:(kvh + 1) * 128], pT[w][i % 2][:, :n], i == 0, i == nk - 1,
                       ["vt", ("pT", w, i % 2)], [P(po)])
                    MM(ps[pd][:, :n], ones_t[:, :], pT[w][i % 2][:, :n], i == 0, i == nk - 1,
                       ["ones", ("pT", w, i % 2)], [P(pd)])

                s_step(0, 0)
                s_step(1, 0)
                for i in range(nk):
                    if i + 1 < nk:
                        s_step(0, i + 1)
                        s_step(1, i + 1)
                    pv_step(0, i)
                    pv_step(1, i)
                for w in range(2):
                    h = heads[w]
                    po, pd = 4 + 2 * w, 5 + 2 * w
                    ACT(rd[w][:, :n], ps[pd][:, :n], AF.Ln, [P(pd)], [("rd", w)])
                    ACT(rd[w][:, :n], rd[w][:, :n], AF.Exp, [("rd", w)], [("rd", w)], scale=-1.0)
                    TT(ao_[:, h, :n], ps[po][:, :n], rd[w][:, :n], ALU.mult, [P(po), ("rd", w)], [("ao", qi % 2, h)])
                    ACT(sqa[:, h, :n], ao_[:, h, :n], AF.Square, [("ao", qi % 2, h)], [("sqa", h)])
            for h in range(8):
                MM(ps[0][:, :n], ones_t[:, :], sqa[:, h, :n], h == 0, h == 7, [("sqa", h), "ones"], [P(0)])
            rstd(ps[0][:, :n], n, 1.0 / 1024, [P(0)])
            for h in range(8):
                STT(yst[:, h, :n], ao_[:, h, :n], gy_t[l][:, h:h + 1], rs_t[:, :n], ALU.mult, ALU.mult,
                    [("ao", qi % 2, h), ("gy", l), ("rs", 0)], ["ysta"], relax=True)
            DMA("sp", YT[:, 0:8, t0:t0 + n], yst[:, :, :n], ["ysta"], [("YT", "a")])
        S.barrier()

    def phase_O(l, src, dst):
        last = (l == DEPTH - 1)
        AR.reset(base)
        wo = AR.alloc([128, 16, D], BF16, "wo")
        yb = [AR.alloc([128, 16, 256], BF16, "yb") for _ in range(3)]
        xb = [AR.alloc([128, 16, 256], F32, "xbo") for _ in range(3)]
        mixs2 = [AR.alloc([128, 16, 256], F32, "mixs") for _ in range(2)]
        sqb = [AR.alloc([128, 256], BF16, "sqb") for _ in range(3)]
        sqm = AR.alloc([128, 16, 256], BF16, "sqm")
        hst = AR.alloc([128, 16, 256], BF16, "hst")
        for k in range(16):
            DMA("pq", wo[:, k, :], wout[l, :, k, :], [], [("wo", k)])
        wotoks = [("wo", k) for k in range(16)]
        nb = 8 if last else 9
        def load_blk(b):
            DMA("sp", yb[b % 3][:, :, :], YT[:, :, b * 256:b * 256 + 256], [("YT", "a"), ("YT", "f")], [("yb", b % 3)])
            DMA("sp", xb[b % 3][:, :, :], src[:, :, b * 256:b * 256 + 256], [("R", src.tensor.name)], [("xbo", b % 3)])

        pipeO = Pipe()

        def tailO(n, pss, mixs, mb, x_, b, t0, s):
            ro = (b % 2) * 512
            rstd(ps[pss][:, :n], n, 1.0 / D, [P(pss)], off=ro)
            for dc in range(16):
                TT(mixs[:, dc, :], mixs[:, dc, :], rs_t[:, ro:ro + n], ALU.mult, [("mixs", mb, dc), ("rs", ro)], [("mixs", mb, dc)])
            for dc in range(16):
                STT(x_[:, dc, :], mixs[:, dc, :], G1[l][:, dc * 2 + s:dc * 2 + s + 1], x_[:, dc, :], ALU.mult, ALU.add,
                    [("mixs", mb, dc), ("G1", l), ("xbo", b % 3)], [("xbo", b % 3)], relax=True)
            DMA("sp", dst[:, :, t0:t0 + n], x_[:, :, :], [("xbo", b % 3)], [("R", dst.tensor.name)])
            x3 = x_[:, :, :n]
            xtok = ("xbo", b % 3)
            def p_sq(k0):
                S.op("pool", lambda e: e.tensor_tensor(out=sqm[:, k0:k0 + 4, :n], in0=x_[:, k0:k0 + 4, :n], in1=x_[:, k0:k0 + 4, :n],
                                                       op=ALU.mult), [xtok], ["sqm"], relax=True)
            for q_ in range(4):
                pipeO.defer(6 + q_, lambda k0=4 * q_: p_sq(k0))
            s_ = 1 if t0 >= T else 0
            ro2 = 256 + (b % 2) * 512
            mtok = ("modtB", l)

            def p_stats():
                for k in range(16):
                    MM(ps[6][:, :n], ones_t[:, :], sqm[:, k, :n], k == 0, k == 15, ["sqm", "ones"], [P(6)])

            def p_rstd():
                rstd(ps[6][:, :n], n, 1.0 / D, [P(6)], off=ro2)
                TT(x3, x3, rs_t[:, ro2:ro2 + n].unsqueeze(1).broadcast_to([128, 16, n]), ALU.mult, [xtok, ("rs", ro2)], [xtok])

            def p_apply(k0):
                for k in range(k0, k0 + 4):
                    ACT(hst[:, k, :n], x_[:, k, :n], AF.Identity, [xtok, ("A2", l), mtok], ["hst"], relax=True,
                        scale=A2[l][:, k * 2 + s_:k * 2 + s_ + 1], bias=modt[l][:, 96 + k * 2 + s_:96 + k * 2 + s_ + 1])

            def p_store():
                DMA("sp", HT[:, :, t0:t0 + n], hst[:, :, :n], ["hst"], ["HT"])
                if b + 3 < nb:
                    load_blk(b + 3)

            pipeO.defer(10, p_stats)
            pipeO.defer(11, p_rstd)
            for q_ in range(4):
                pipeO.defer(12 + q_, lambda k0=4 * q_: p_apply(k0))
            pipeO.defer(16, p_store)

        load_blk(0)
        if nb > 1:
            load_blk(1)
        if nb > 2:
            load_blk(2)
        for b in range(nb):
            t0 = b * 256
            n = 256
            pss = 4 + b % 2
            s = 1 if t0 >= T else 0
            y_ = yb[b % 3]
            x_ = xb[b % 3]
            mixs = mixs2[b % 2]
            mb = b % 2
            for dc in range(16):
                pa = dc % 4
                for k in range(16):
                    MM(ps[pa][:, :n], wo[:, k, dc * 128:(dc + 1) * 128], y_[:, k, :], k == 0, k == 15,
                       [("wo", k), ("yb", b % 3)], [P(pa)])
                pipeO.tick()
                ACT(mixs[:, dc, :], ps[pa][:, :n], AF.Copy, [P(pa)], [("mixs", mb, dc)])
                S.op("pool", lambda e, dc=dc, mixs=mixs: e.tensor_tensor(out=sqb[dc % 3][:, :n], in0=mixs[:, dc, :], in1=mixs[:, dc, :],
                                                                        op=ALU.mult), [("mixs", mb, dc)], [("sqb", dc % 3)])
                pipeO.defer(2, lambda dc=dc, n=n, pss=pss: MM(ps[pss][:, :n], ones_t[:, :], sqb[dc % 3][:, :n], dc == 0, dc == 15,
                                                     [("sqb", dc % 3), "ones"], [P(pss)]))
            pipeO.defer(3, lambda n=n, pss=pss, mixs=mixs, mb=mb, x_=x_, b=b, t0=t0, s=s: tailO(n, pss, mixs, mb, x_, b, t0, s))
        pipeO.flush()
        S.barrier()

    def phase_U(l, src):
        last = (l == DEPTH - 1)
        AR.reset(base)
        hT = AR.alloc([128, 16, TA], BF16, "hTu")
        xb = [AR.alloc([128, 16, 256], F32, "xbu") for _ in range(2)]
        sq = [AR.alloc([128, 16, 256], BF16, "squ") for _ in range(2)]
        tmp = [AR.alloc([128, 256], F32, "tmpu") for _ in range(2)]
        cw_t = AR.alloc([128, NFC * 3], F32, "cw")
        cb_t = AR.alloc([128, NFC], F32, "cbt")
        wb = [AR.alloc([128, 2, 16, 128], BF16, "wup") for _ in range(2)]
        NT = T if last else TA
        gbuf = [AR.alloc([128, TA], F32, "gbuf") for _ in range(2)]
        vbuf = [AR.alloc([128, TA], BF16, "vbuf") for _ in range(2)]
        cbuf = [AR.alloc([128, TA], F32, "cbuf") for _ in range(2)]
        ast = [AR.alloc([128, TA], BF16, "ast") for _ in range(2)]
        DMA("sp", cw_t[:, :], cw[l], [], ["cw"])
        DMA("sp", cb_t[:, :], cbias[l], [], ["cbt"])
        blocks = [(i * 256, 256, 0) for i in range(8)]
        if not last:
            blocks.append((T, 256, 1))
        load_hT(hT, T if last else TA)
        chunks = [(0, 512), (512, 512), (1024, 512), (1536, 512)]
        segs = [(0, T)]
        if not last:
            chunks.append((T, 256))
            segs.append((T, TA))
        it = 0
        for fc in range(NFC):
            w = wb[fc % 2]
            wtok = ("wup", fc % 2)
            DMA("pq", w[:, :, :, :], wup[l, fc], [], [wtok])
            gb, vb, cbf, as_ = gbuf[fc % 2], vbuf[fc % 2], cbuf[fc % 2], ast[fc % 2]
            gt, vtk, ct, at = ("gbuf", fc % 2), ("vbuf", fc % 2), ("cbuf", fc % 2), ("ast", fc % 2)
            for (t0, n) in chunks:
                pa = it % 4
                pb = 4 + it % 4
                it += 1
                for k in range(16):
                    MM(ps[pa][:, :n], w[:, 0, k, :], hT[:, k, t0:t0 + n], k == 0, k == 15, [wtok] + hT_tokens(t0, n), [P(pa)])
                ACT(gb[:, t0:t0 + n], ps[pa][:, :n], AF.Copy, [P(pa)], [gt])
                for k in range(16):
                    MM(ps[pb][:, :n], w[:, 1, k, :], hT[:, k, t0:t0 + n], k == 0, k == 15, [wtok] + hT_tokens(t0, n), [P(pb)])
                CP(vb[:, t0:t0 + n], ps[pb][:, :n], [P(pb)], [vtk])
            c0 = cw_t[:, fc * 3 + 0:fc * 3 + 1]
            c1 = cw_t[:, fc * 3 + 1:fc * 3 + 2]
            c2 = cw_t[:, fc * 3 + 2:fc * 3 + 3]
            for (a, b) in segs:
                ACT(cbf[:, a:b], gb[:, a:b], AF.Identity, [gt, "cw", "cbt"], [ct], scale=c1, bias=cb_t[:, fc:fc + 1])
                STT(cbf[:, a + 1:b], gb[:, a:b - 1], c0, cbf[:, a + 1:b], ALU.mult, ALU.add, [gt, ct, "cw"], [ct])
                STT(cbf[:, a:b - 1], gb[:, a + 1:b], c2, cbf[:, a:b - 1], ALU.mult, ALU.add, [gt, ct, "cw"], [ct])
            if gelu_mode == "tanh_act":
                ACT(cbf[:, :NT], cbf[:, :NT], AF.Gelu_apprx_tanh, [ct], [ct])
            else:
                gsc = gb
                TT(gsc[:, :NT], cbf[:, :NT], cbf[:, :NT], ALU.mult, [ct], [gt])
                TS(gsc[:, :NT], gsc[:, :NT], 0.044715, 1.0, ALU.mult, ALU.add, [gt], [gt])
                TT(gsc[:, :NT], gsc[:, :NT], cbf[:, :NT], ALU.mult, [gt, ct], [gt])
                ACT(gsc[:, :NT], gsc[:, :NT], AF.Sigmoid, [gt], [gt], scale=float(2.0 * np.sqrt(2.0 / np.pi)))
                TT(cbf[:, :NT], cbf[:, :NT], gsc[:, :NT], ALU.mult, [gt, ct], [ct])
            TT(as_[:, :NT], cbf[:, :NT], vb[:, :NT], ALU.mult, [ct, vtk], [at])
            DMA("sp", ACTD[:, fc, :NT], as_[:, :NT], [at], [("ACTD", fc)])
        S.barrier()

    def phase_D(l, src, dst, dst_is_out):
        last = (l == DEPTH - 1)
        AR.reset(base)
        ab = AR.alloc([128, NFC, 768], BF16, "ab")
        ys = AR.alloc([128, 16, 768], F32, "ys")
        wb = [AR.alloc([128, NFC, 128], BF16, "wdn") for _ in range(2)]
        xall = AR.alloc([128, 16, 768], F32, "xall")
        sqd = [AR.alloc([128, 384], BF16, "sqd") for _ in range(2)]
        if NEXTMOD and l + 1 < nlayers:
            sqm = AR.alloc([128, 16, 128], BF16, "sqmd")
            hst = AR.alloc([128, 16, 128], BF16, "hstd")
        print("D arena slack", AR.hi - AR.off)
        blocks = [(0, 768), (768, 768), (1536, 512 if last else 768)]
        wi = 0

        def load_ab(bi):
            t0_, n_ = blocks[bi]
            for f0 in range(0, NFC, 11):
                DMA("sp", ab[:, f0:f0 + 11, :n_], ACTD[:, f0:f0 + 11, t0_:t0_ + n_],
                    [("ACTD", f) for f in range(f0, f0 + 11)], [("ab", f0)])

        def load_x(bi):
            t0_, n_ = blocks[bi]
            DMA("sp", xall[:, :, :n_], src[:, :, t0_:t0_ + n_], [("R", src.tensor.name)], ["xall"])

        pipeD = Pipe()
        load_ab(0)
        load_x(0)
        for bi, (t0, n) in enumerate(blocks):
            cn = n // 2
            it = 0
            rngs = []
            if t0 + n <= T:
                rngs.append((0, n, 0))
            else:
                if t0 < T:
                    rngs.append((0, T - t0, 0))
                rngs.append((max(T - t0, 0), n, 1))
            for dc in range(16):
                w = wb[wi % 2]
                wtok = ("wdn", wi % 2)
                wi += 1
                DMA("pq", w[:, :, :], wdn[l, dc], [], [wtok])
                for ci in range(2):
                    c0 = ci * cn
                    pa = it % 4
                    it += 1
                    for fc in range(NFC):
                        MM(ps[pa][:, :cn], w[:, fc, :], ab[:, fc, c0:c0 + cn], fc == 0, fc == NFC - 1,
                           [wtok, ("ab", (fc // 11) * 11)], [P(pa)])
                    pipeD.tick()
                    ACT(sqd[pa % 2][:, :cn], ps[pa][:, :cn], AF.Square, [P(pa)], [("sqd", pa % 2)])
                    for (u0, u1, s_) in rngs:
                        a0, a1 = max(u0, c0), min(u1, c0 + cn)
                        if a1 > a0:
                            ACT(ys[:, dc, a0:a1], ps[pa][:, a0 - c0:a1 - c0], AF.Identity, [P(pa), ("G2", l)], [("ys", dc)],
                                scale=G2[l][:, dc * 2 + s_:dc * 2 + s_ + 1])
                    pipeD.defer(1, lambda ci=ci, cn=cn, pa=pa, dc=dc: MM(ps[4 + ci][:, :cn], ones_t[:, :], sqd[pa % 2][:, :cn],
                                                               dc == 0, dc == 15, [("sqd", pa % 2), "ones"], [P(4 + ci)]))
            pipeD.flush()
            if bi + 1 < len(blocks):
                load_ab(bi + 1)
            for ci in range(2):
                rstd(ps[4 + ci][:, :cn], cn, 1.0 / D, [P(4 + ci)], off=ci * cn)
            rstoks = [("rs", 0), ("rs", cn)]
            for dc in range(16):
                TT(ys[:, dc, :n], ys[:, dc, :n], rs_t[:, :n], ALU.mult, [("ys", dc)] + rstoks, [("ys", dc)])
            for dc in range(16):
                TT(xall[:, dc, :n], ys[:, dc, :n], xall[:, dc, :n], ALU.add, [("ys", dc), "xall"], ["xall"])
            o = DMA("sp", dst[:, :, t0:t0 + n], xall[:, :, :n], ["xall"], [("R", dst.tensor.name)])
            if dst_is_out:
                finals.append(o)
            if NEXTMOD and l + 1 < nlayers:
                nsub = n // 128

                def submod(j, bi=bi, t0=t0, nsub=nsub):
                    u0 = j * 128
                    tg = t0 + u0
                    s_ = 1 if tg >= T else 0
                    ACT(sqm[:, :, :], xall[:, :, u0:u0 + 128], AF.Square, ["xall"], ["sqm"])

                    def sub2():
                        mod_stats_and_apply(l + 1, A1[l + 1], ("A1", l + 1), 0, xall[:, :, u0:u0 + 128], "xall",
                                            s_, tg, 128, sqm, hst, 6, 768)
                        if j + 1 < nsub:
                            submod(j + 1)
                        elif bi + 1 < len(blocks):
                            load_x(bi + 1)
                    pipeD.defer(1, sub2)
                pipeD.defer(4, lambda: submod(0))
            elif bi + 1 < len(blocks):
                load_x(bi + 1)
        pipeD.flush()
        S.barrier()

    finals = []

    def run_all():
        src = xt
        for l in range(nlayers):
            is_out = (l == DEPTH - 1)
            for nm, fn in (("A", lambda: phase_A(l, src)), ("F", lambda: phase_F(l)), ("T", lambda: phase_T(l)),
                           ("O", lambda: phase_O(l, src, R1)), ("U", lambda: phase_U(l, R1)),
                           ("D", lambda: phase_D(l, R1, out_d if is_out else R2, is_out))):
                fn()
                if stop in (nm, nm + str(l)) and (len(stop) == 2 or l == 0):
                    return
            src = R2

    run_all()
    if debug:
        dbgm = nc.dram_tensor("dbgm", [128, 192 + 4 * 32], F32, kind="ExternalOutput").ap()
        o = DMA("sp", dbgm[:, 0:192], modt[0][:, :], [("modtA", 0), ("modtB", 0)], ["dbgm"])
        finals.append(o)
        for i, tl in enumerate((A1, G1, A2, G2)):
            nm = ("A1", "G1", "A2", "G2")[i]
            finals.append(DMA("sp", dbgm[:, 192 + i * 32:192 + (i + 1) * 32], tl[0][:, :], [(nm, 0)], ["dbgm"]))
    S.emit(finals)
    return nc


def _bf16(a):
    return np.ascontiguousarray(a.astype(np.float32)).astype(ml_dtypes.bfloat16)


def _consts():
    c = {}
    c["ones"] = _bf16(np.ones((128, 128)))
    sw = np.zeros((128, 128), np.float32)
    for i in range(128):
        sw[i ^ 1, i] = 1.0
    c["swp"] = _bf16(sw)
    cidx = np.arange(128)[:, None].astype(np.float64)
    m = np.arange(128)[None, :].astype(np.float64)
    ang = 2 * np.pi * cidx * m / 128.0
    c["cs"] = _bf16(np.concatenate([np.cos(ang), -np.sin(ang)], axis=1) / np.sqrt(128.0))
    pos = np.arange(T)
    row = (pos // 64).astype(np.float32)
    col = (pos % 64).astype(np.float32)
    freqs = (np.float32(10000.0) ** (-np.arange(32, dtype=np.float32) / np.float32(32))).astype(np.float32)
    angp = np.concatenate([row[:, None] * freqs, col[:, None] * freqs], axis=-1).astype(np.float32)
    cosv = np.cos(angp).astype(np.float32)
    sinv = np.sin(angp).astype(np.float32)
    rc = np.zeros((128, T), np.float32)
    rsn = np.zeros((128, T), np.float32)
    for j in range(64):
        rc[2 * j] = cosv[:, j]
        rc[2 * j + 1] = cosv[:, j]
        rsn[2 * j] = -sinv[:, j]
        rsn[2 * j + 1] = sinv[:, j]
    c["ropec"] = rc
    c["ropes"] = rsn

    def dft(n):
        t = np.arange(n, dtype=np.int64)
        kt = (t[:, None] * t[None, :]) % n
        a = 2 * np.pi * kt.astype(np.float64) / n
        cm = np.cos(a) / np.sqrt(n)
        sm = np.sin(a) / np.sqrt(n)
        nt = n // 128
        cm = cm.reshape(nt, 128, n).transpose(1, 0, 2)
        sm = sm.reshape(nt, 128, n).transpose(1, 0, 2)
        return _bf16(cm), _bf16(sm)

    c["dftc"], c["dfts"] = dft(T)
    c["dftcc"], c["dftsc"] = dft(CTX)
    return c


def _fm(v):
    sh = v.shape
    n = sh[-1] // 128
    return np.ascontiguousarray(np.swapaxes(v.reshape(sh[:-1] + (n, 128)), -1, -2))


def _prep_shared(c_ctx, w_mod, b_mod, g_pre_mix, g_post_mix, g_pre_ffn, g_post_ffn, w_in, q_norm, k_norm,
                 w_four, g_attn_out, g_four_out, w_out, w_up, conv_w, conv_b, w_down):
    f = np.float32
    sh = {}
    sh["wmod"] = np.ascontiguousarray(w_mod.reshape(DEPTH, 16, 128, 48, 256).transpose(0, 3, 2, 1, 4))
    bm = _fm(b_mod)
    sh["bmod2"] = np.ascontiguousarray(np.repeat(bm, 2, axis=-1))
    gs = np.stack([_fm(g_pre_mix), _fm(g_post_mix), _fm(g_pre_ffn), _fm(g_post_ffn)], axis=1)
    sh["g2"] = np.ascontiguousarray(np.repeat(gs, 2, axis=-1))
    wi = w_in.reshape(DEPTH, 16, 128, 2560)
    cols = list(range(0, 1280, 128)) + list(range(1536, 2560, 128))
    sh["wqkf"] = np.ascontiguousarray(
        np.stack([wi[:, :, :, c0:c0 + 128] for c0 in cols], axis=1).transpose(0, 1, 3, 2, 4))
    sh["wv"] = np.ascontiguousarray(wi[:, :, :, 1280:1536].transpose(0, 2, 1, 3))
    sh["qkn"] = np.ascontiguousarray(np.stack([q_norm, k_norm], axis=-1))
    sh["w4"] = np.ascontiguousarray(w_four.transpose(0, 2, 1, 3))
    sh["gy"] = np.ascontiguousarray(np.concatenate([_fm(g_attn_out), _fm(g_four_out)], axis=-1))
    sh["wout"] = np.ascontiguousarray(w_out.reshape(DEPTH, 16, 128, D).transpose(0, 2, 1, 3))
    wu = w_up.reshape(DEPTH, 16, 128, 2, NFC, 128)
    sh["wup"] = np.ascontiguousarray(wu.transpose(0, 4, 2, 3, 1, 5))
    cwf = conv_w.reshape(DEPTH, 3, NFC, 128).transpose(0, 3, 2, 1)
    sh["cw"] = np.ascontiguousarray(cwf.reshape(DEPTH, 128, NFC * 3))
    sh["cb"] = _fm(conv_b)
    wd = w_down.reshape(DEPTH, NFC, 128, 16, 128)
    sh["wdn"] = np.ascontiguousarray(wd.transpose(0, 3, 2, 1, 4))
    for k in sh:
        assert sh[k].dtype == f, k
    sh.update(_consts())
    return sh


def _prep_core(b, x, c, ctx, c_ctx):
    xa = np.concatenate([x[b].T, ctx[b].T], axis=1)
    xt = np.ascontiguousarray(xa.reshape(16, 128, TA).transpose(1, 0, 2))
    cc = np.stack([_fm(c[b]), _fm(c_ctx)], axis=-1).reshape(128, 32)
    return {"xt": xt, "cc": np.ascontiguousarray(cc)}


_NC_CACHE = {}


def kernel(x, c, ctx, c_ctx, w_mod, b_mod, g_pre_mix, g_post_mix, g_pre_ffn, g_post_ffn,
           w_in, q_norm, k_norm, w_four, g_attn_out, g_four_out, w_out,
           w_up, conv_w, conv_b, w_down):
    args = [np.asarray(a, dtype=np.float32) for a in
            (x, c, ctx, c_ctx, w_mod, b_mod, g_pre_mix, g_post_mix, g_pre_ffn, g_post_ffn, w_in, q_norm, k_norm,
             w_four, g_attn_out, g_four_out, w_out, w_up, conv_w, conv_b, w_down)]
    (x, c, ctx, c_ctx, w_mod, b_mod, g_pre_mix, g_post_mix, g_pre_ffn, g_post_ffn, w_in, q_norm, k_norm,
     w_four, g_attn_out, g_four_out, w_out, w_up, conv_w, conv_b, w_down) = args
    shared = _prep_shared(c_ctx, w_mod, b_mod, g_pre_mix, g_post_mix, g_pre_ffn, g_post_ffn, w_in, q_norm, k_norm,
                          w_four, g_attn_out, g_four_out, w_out, w_up, conv_w, conv_b, w_down)
    in_maps = []
    for b in range(NCORES):
        m = dict(shared)
        m.update(_prep_core(b, x, c, ctx, c_ctx))
        in_maps.append(m)
    nc = build_nc()
    res = run_bass_kernel_spmd(nc, in_maps, core_ids=list(range(NCORES)))
    outs = []
    for b in range(NCORES):
        o = np.asarray(res.results[b]["out"], dtype=np.float32)
        outs.append(o.transpose(1, 0, 2).reshape(D, T).T)
    return np.ascontiguousarray(np.stack(outs, axis=0)).astype(np.float32)
```

```python
import contextlib
import numpy as np
import ml_dtypes
import concourse.bass as bass
import concourse.mybir as mybir
from concourse.bass_utils import run_bass_kernel_spmd

F32 = mybir.dt.float32
BF16 = mybir.dt.bfloat16
AF = mybir.ActivationFunctionType
ALU = mybir.AluOpType

D = 2048
T = 2048
CTX = 256
TA = T + CTX
DEPTH = 2
FFN = 5632
NFC = FFN // 128
EPS = 1e-6
NCORES = 8

COMPUTE = ("pe", "act", "dve", "pool")
DMAQ = ("sp", "pq")
NSLOT = 8
NEXTMOD = True


class Op:
    __slots__ = ("eng", "idx", "fn", "waits", "signal", "is_dma", "slot", "slot_val", "sigcount")

    def __init__(self, eng, idx, fn, is_dma):
        self.eng = eng
        self.idx = idx
        self.fn = fn
        self.waits = []
        self.signal = False
        self.is_dma = is_dma
        self.slot = None
        self.slot_val = None
        self.sigcount = None


class Sched:
    def __init__(self, nc):
        self.nc = nc
        self.streams = {e: [] for e in COMPUTE + ("sp",)}
        self.last_writer = {}
        self.readers = {}
        self.waited = {e: {} for e in COMPUTE + ("sp",)}
        self.waited_dma = {e: set() for e in COMPUTE + ("sp",)}
        self.dma_count = {q: 0 for q in DMAQ}
        self.dma_ops = {q: [] for q in DMAQ}
        self.last_compute = {e: None for e in COMPUTE}

    @staticmethod
    def _stream_of(eng):
        return "pool" if eng == "pq" else eng

    def op(self, eng, fn, reads=(), writes=(), relax=False):
        sname = self._stream_of(eng)
        stream = self.streams[sname]
        is_dma = eng in DMAQ
        o = Op(eng, len(stream), fn, is_dma)
        writes = list(writes) + [t for t in reads if isinstance(t, tuple) and t[0] == "ps" and t not in writes]
        deps = []
        for t in reads:
            w = self.last_writer.get(t)
            if w is not None:
                deps.append((w, "raw"))
        for t in writes:
            w = self.last_writer.get(t)
            if w is not None:
                deps.append((w, "waw"))
            for r in self.readers.get(t, ()):
                deps.append((r, "war"))
        if is_dma:
            n = self.dma_count[eng]
            o.slot = n % NSLOT
            o.slot_val = 16 * (n // NSLOT + 1)
            if n >= NSLOT:
                deps.append((self.dma_ops[eng][n - NSLOT], "slot"))
            self.dma_count[eng] = n + 1
            self.dma_ops[eng].append(o)
        best = {}
        for d, kind in deps:
            if d is o:
                continue
            if d.is_dma:
                self._add_wait(o, sname, d, kind)
                continue
            if d.eng == sname and not is_dma:
                if sname == "pe" or kind == "slot":
                    continue
                if relax and kind != "raw":
                    continue
            cur = best.get(d.eng)
            if cur is None or d.idx > cur.idx:
                best[d.eng] = d
        for d in best.values():
            self._add_wait(o, sname, d, "raw")
        stream.append(o)
        if not is_dma and eng in COMPUTE:
            self.last_compute[eng] = o
        for t in reads:
            self.readers.setdefault(t, []).append(o)
        for t in writes:
            self.last_writer[t] = o
            self.readers[t] = []
        return o

    def _add_wait(self, o, sname, d, kind):
        if d.is_dma:
            if d in self.waited_dma[sname]:
                return
            self.waited_dma[sname].add(d)
            o.waits.append(d)
            return
        dname = d.eng
        if dname == sname and not o.is_dma:
            if sname == "pe":
                return
            if kind == "slot":
                return
        if self.waited[sname].get(dname, -1) >= d.idx:
            return
        self.waited[sname][dname] = d.idx
        d.signal = True
        o.waits.append(d)

    def barrier(self):
        lasts = [o for o in self.last_compute.values() if o is not None]
        dmas = []
        for q in DMAQ:
            dmas += self.dma_ops[q][-NSLOT:]
        for sname in list(self.streams.keys()):
            stream = self.streams[sname]
            o = Op(sname, len(stream), lambda e: e.nop(), False)
            o.signal = False
            for d in lasts:
                if d.eng != sname:
                    self._add_wait(o, sname, d, "raw")
            for d in dmas:
                self._add_wait(o, sname, d, "raw")
            stream.append(o)
            o.eng = "__nop__"

    def emit(self, final_ops):
        nc = self.nc
        with contextlib.ExitStack() as st:
            sems = {e: st.enter_context(nc.semaphore("s_" + e)) for e in COMPUTE}
            dsems = {q: [st.enter_context(nc.semaphore(f"d_{q}{i}")) for i in range(NSLOT)] for q in DMAQ}
            for e in COMPUTE:
                c = 0
                for o in self.streams[e]:
                    if o.is_dma or o.eng == "__nop__":
                        continue
                    if o.signal:
                        c += 1
                        o.sigcount = c
            block = st.enter_context(nc.Block())
            engmap = {"pe": block.tensor, "act": block.scalar, "dve": block.vector,
                      "pool": block.gpsimd, "sp": block.sync}

            def make(sname):
                def body(eng):
                    for o in self.streams[sname]:
                        for d in o.waits:
                            if d.is_dma:
                                eng.wait_ge(dsems[d.eng][d.slot], d.slot_val)
                            else:
                                eng.wait_ge(sems[d.eng], d.sigcount)
                        ins = o.fn(eng)
                        if o.is_dma:
                            ins.then_inc(dsems[o.eng][o.slot], 16)
                        elif o.signal:
                            ins.then_inc(sems[o.eng], 1)
                    for d in final_ops:
                        if self._stream_of(d.eng) == sname:
                            eng.wait_ge(dsems[d.eng][d.slot], d.slot_val)
                return body

            for sname in ("pe", "act", "dve", "pool", "sp"):
                engmap[sname](make(sname))


class Arena:
    def __init__(self, nc, lo, hi):
        self.nc = nc
        self.lo = lo
        self.hi = hi
        self.off = lo
        self.n = 0

    def mark(self):
        return self.off

    def reset(self, to):
        self.off = to

    def alloc(self, shape, dtype, name="t"):
        size = int(np.prod(shape[1:])) * (4 if dtype == F32 else 2)
        off = (self.off + 63) // 64 * 64
        assert off + size <= self.hi, f"SBUF arena overflow: {name} {shape} off={off} size={size} hi={self.hi}"
        self.off = off + size
        self.n += 1
        return self.nc.alloc_sbuf_tensor_at(f"{name}_{self.n}", list(shape), dtype, offset=off)


class Pipe:
    def __init__(self):
        self.q = []

    def defer(self, delay, fn):
        self.q.append([delay, fn])

    def tick(self):
        cur, self.q = self.q, []
        for item in cur:
            item[0] -= 1
            if item[0] <= 0:
                item[1]()
            else:
                self.q.append(item)

    def flush(self):
        while self.q:
            self.tick()


def build_nc(nlayers=DEPTH, debug=False, gelu_mode="tanh_act", stop=None):
    nc = bass.Bass("TRN2", target_bir_lowering=False)

    def din(name, shape, dt=F32):
        return nc.dram_tensor(name, list(shape), dt, kind="ExternalInput").ap()

    def dscr(name, shape, dt):
        if debug:
            return nc.dram_tensor(name, list(shape), dt, kind="ExternalOutput").ap()
        return nc.dram_tensor(name, list(shape), dt).ap()

    xt = din("xt", [128, 16, TA])
    cc = din("cc", [128, 32])
    wmod = din("wmod", [DEPTH, 48, 128, 16, 256])
    bmod2 = din("bmod2", [DEPTH, 128, 192])
    g2 = din("g2", [DEPTH, 4, 128, 32])
    wqkf = din("wqkf", [DEPTH, 18, 128, 16, 128])
    wv = din("wv", [DEPTH, 128, 16, 256])
    qkn = din("qkn", [DEPTH, 128, 2])
    w4 = din("w4", [DEPTH, 128, 8, 128])
    gy = din("gy", [DEPTH, 128, 16])
    wout = din("wout", [DEPTH, 128, 16, D])
    wup = din("wup", [DEPTH, NFC, 128, 2, 16, 128])
    cw = din("cw", [DEPTH, 128, NFC * 3])
    cbias = din("cb", [DEPTH, 128, NFC])
    wdn = din("wdn", [DEPTH, 16, 128, NFC, 128])
    ones_d = din("ones", [128, 128], BF16)
    swp_d = din("swp", [128, 128], BF16)
    cs_d = din("cs", [128, 256], BF16)
    ropec_d = din("ropec", [128, T])
    ropes_d = din("ropes", [128, T])
    dftc_d = din("dftc", [128, 16, T], BF16)
    dfts_d = din("dfts", [128, 16, T], BF16)
    dftcc_d = din("dftcc", [128, 2, CTX], BF16)
    dftsc_d = din("dftsc", [128, 2, CTX], BF16)
    out_d = nc.dram_tensor("out", [128, 16, T], F32, kind="ExternalOutput").ap()

    R1 = dscr("R1", [128, 16, TA], F32)
    R2 = dscr("R2", [128, 16, TA], F32)
    QK = dscr("QK", [128, 10, TA], BF16)
    VD = dscr("VD", [128, 18, 256], BF16)
    GCS = dscr("GCS", [128, 8, 18, 256], BF16)
    YT = dscr("YT", [128, 16, TA], BF16)
    ACTD = dscr("ACTD", [128, NFC, TA], BF16)
    HT = dscr("HT", [128, 16, TA], BF16)

    S = Sched(nc)
    lo = (int(nc.sbuf_base) + 63) // 64 * 64
    hi = int(nc.sbuf_top)
    AR = Arena(nc, lo, hi)
    ps = [nc.alloc_psum_tensor(f"psb{i}", [128, 512], F32) for i in range(8)]

    def P(i):
        return ("ps", i)

    def MM(out, lhsT, rhs, start, stop, r, w):
        return S.op("pe", lambda e: e.matmul(out, lhsT, rhs, start=start, stop=stop), r, w)

    def ACT(out, in_, func, r, w, relax=False, **kw):
        return S.op("act", lambda e: e.activation(out=out, in_=in_, func=func, **kw), r, w, relax=relax)

    def TS(out, in0, s1, s2, op0, op1, r, w):
        if op1 is None:
            return S.op("dve", lambda e: e.tensor_scalar(out=out, in0=in0, scalar1=s1, scalar2=None, op0=op0), r, w)
        return S.op("dve", lambda e: e.tensor_scalar(out=out, in0=in0, scalar1=s1, scalar2=s2, op0=op0, op1=op1), r, w)

    def STT(out, in0, scalar, in1, op0, op1, r, w, relax=False):
        return S.op("dve", lambda e: e.scalar_tensor_tensor(out=out, in0=in0, scalar=scalar, in1=in1, op0=op0, op1=op1), r, w, relax=relax)

    def TT(out, in0, in1, op, r, w):
        return S.op("dve", lambda e: e.tensor_tensor(out=out, in0=in0, in1=in1, op=op), r, w)

    def CP(out, in_, r, w):
        return S.op("dve", lambda e: e.tensor_copy(out=out, in_=in_), r, w)

    def RCP(out, in_, r, w):
        return S.op("dve", lambda e: e.reciprocal(out=out, in_=in_), r, w)

    def DMA(q, out, in_, r, w):
        return S.op(q, lambda e: e.dma_start(out=out, in_=in_), r, w)

    uid = [0]

    def tok(name):
        uid[0] += 1
        return (name, uid[0])

    ones_t = AR.alloc([128, 128], BF16, "ones")
    swp_t = AR.alloc([128, 128], BF16, "swp")
    cs_t = AR.alloc([128, 256], BF16, "cs")
    modt = [AR.alloc([128, 192], F32, "modt") for _ in range(DEPTH)]
    A1 = [AR.alloc([128, 32], F32, "A1") for _ in range(DEPTH)]
    G1 = [AR.alloc([128, 32], F32, "G1") for _ in range(DEPTH)]
    A2 = [AR.alloc([128, 32], F32, "A2") for _ in range(DEPTH)]
    G2 = [AR.alloc([128, 32], F32, "G2") for _ in range(DEPTH)]
    qkn_t = [AR.alloc([128, 2], F32, "qkn") for _ in range(DEPTH)]
    gy_t = [AR.alloc([128, 16], F32, "gy") for _ in range(DEPTH)]
    silu_t = AR.alloc([128, 32], BF16, "silu")
    bm_t = [AR.alloc([128, 192], F32, "bm") for _ in range(DEPTH)]
    g2_t = [[AR.alloc([128, 32], F32, "g2") for _ in range(4)] for _ in range(DEPTH)]
    rs_t = AR.alloc([128, 1024], F32, "rs")
    base = AR.mark()

    DMA("sp", ones_t[:, :], ones_d[:, :], [], ["ones"])
    DMA("sp", swp_t[:, :], swp_d[:, :], [], ["swp"])
    DMA("sp", cs_t[:, :], cs_d[:, :], [], ["cs"])
    for l in range(DEPTH):
        DMA("sp", qkn_t[l][:, :], qkn[l], [], [("qkn", l)])
        DMA("sp", gy_t[l][:, :], gy[l], [], [("gy", l)])

    def rstd(ps_ap, n, inv_count, r, off=0):
        ACT(rs_t[:, off:off + n], ps_ap, AF.Ln, list(r) + ["eps"], [("rs", off)], scale=inv_count, bias=eps_t[:, 0:1])
        ACT(rs_t[:, off:off + n], rs_t[:, off:off + n], AF.Exp, [("rs", off)], [("rs", off)], scale=-0.5)

    eps_t = AR.alloc([128, 1], F32, "eps")
    base = AR.mark()
    S.op("dve", lambda e: e.memset(eps_t[:, :], EPS), [], ["eps"])

    class ModBG:
        def __init__(self):
            self.wt = None
            self.cnt = 0
            self.todo = []

        def setup(self, wt):
            self.wt = wt

        def load(self, l, it_):
            nb_ = len(self.wt)
            w = self.wt[self.cnt % nb_]
            wtok = ("wm", id(w))
            self.cnt += 1
            DMA("pq", w[:, :, :], wmod[l, it_], [], [wtok])
            return (w, wtok, it_)

        def compute(self, pend):
            w, wtok, it_ = pend
            for m in range(2):
                j = it_ * 2 + m
                for k in range(16):
                    MM(ps[7][:, j * 2:j * 2 + 2], w[:, k, m * 128:(m + 1) * 128], silu_t[:, k * 2:k * 2 + 2],
                       k == 0, k == 15, [wtok, "silu"], [P(7)])

        def item(self, l, it_):
            self.compute(self.load(l, it_))

        def evac_A(self, l):
            TT(modt[l][:, 0:64], ps[7][:, 0:64], bm_t[l][:, 0:64], ALU.add, [P(7), ("bm", l)], [("modtA", l)])
            STT(A1[l][:, :], modt[l][:, 32:64], 1.0, g2_t[l][0][:, :], ALU.add, ALU.mult,
                [("modtA", l), ("g2", l, 0)], [("A1", l)])

        def evac_B(self, l):
            TT(modt[l][:, 64:192], ps[7][:, 64:192], bm_t[l][:, 64:192], ALU.add, [P(7), ("bm", l)], [("modtB", l)])
            TT(G1[l][:, :], modt[l][:, 64:96], g2_t[l][1][:, :], ALU.mult, [("modtB", l), ("g2", l, 1)], [("G1", l)])
            STT(A2[l][:, :], modt[l][:, 128:160], 1.0, g2_t[l][2][:, :], ALU.add, ALU.mult,
                [("modtB", l), ("g2", l, 2)], [("A2", l)])
            TT(G2[l][:, :], modt[l][:, 160:192], g2_t[l][3][:, :], ALU.mult, [("modtB", l), ("g2", l, 3)], [("G2", l)])

        def plan_background(self):
            self.pending = []
            for it_ in range(16, 48):
                self.todo.append(("item", 0, it_))
            self.todo.append(("fn", lambda: self.evac_B(0)))
            for l in range(1, nlayers):
                for it_ in range(48):
                    self.todo.append(("item", l, it_))
                self.todo.append(("fn", lambda l=l: (self.evac_A(l), self.evac_B(l))))

        def step(self, n=1):
            depth = len(self.wt)
            for _ in range(n):
                if len(self.pending) >= depth:
                    self.compute(self.pending.pop(0))
                if self.todo:
                    e = self.todo.pop(0)
                    if e[0] == "item":
                        self.pending.append(self.load(e[1], e[2]))
                    else:
                        self.drain_pending()
                        e[1]()
                elif self.pending:
                    self.compute(self.pending.pop(0))

        def drain_pending(self):
            while self.pending:
                self.compute(self.pending.pop(0))

        def flush(self):
            while self.todo or self.pending:
                self.step(1)

    modbg = ModBG()

    def mod_prologue(wt):
        cc_t = AR.alloc([128, 32], F32, "cc")
        DMA("sp", cc_t[:, :], cc[:, :], [], ["cc"])
        ACT(silu_t[:, :], cc_t[:, :], AF.Silu, ["cc"], ["silu"])
        for l in range(nlayers):
            DMA("sp", bm_t[l][:, :], bmod2[l], [], [("bm", l)])
            for i in range(4):
                DMA("sp", g2_t[l][i][:, :], g2[l, i], [], [("g2", l, i)])
        modbg.setup(wt)
        for it_ in range(16):
            modbg.item(0, it_)
        modbg.evac_A(0)
        modbg.plan_background()

    def modulate(l, src, Avec, sh_off, hT, blocks, xb, sq, tmp, after_block=None):
        atok = ("A1" if Avec is A1[l] else "A2", l)
        mtok = ("modtA" if sh_off == 0 else "modtB", l)
        for bi, (t0, n, s) in enumerate(blocks):
            x_ = xb[bi % len(xb)]
            xtok = ("xb", bi % len(xb))
            DMA("sp", x_[:, :, :n], src[:, :, t0:t0 + n], [("R", src.tensor.name)], [xtok])
            sq_ = sq[bi % len(sq)]
            sqtok = ("sq", bi % len(sq))
            ACT(sq_[:, :, :n], x_[:, :, :n], AF.Square, [xtok], [sqtok])
            for k in range(16):
                MM(ps[6][:, :n], ones_t[:, :], sq_[:, k, :n], k == 0, k == 15, [sqtok, "ones"], [P(6)])
            ro = (bi % 2) * 512
            rstd(ps[6][:, :n], n, 1.0 / D, [P(6)], off=ro)
            TT(x_[:, :, :n], x_[:, :, :n], rs_t[:, ro:ro + n].unsqueeze(1).broadcast_to([128, 16, n]), ALU.mult,
               [xtok, ("rs", ro)], [xtok])
            for k in range(16):
                ACT(hT[:, k, t0:t0 + n], x_[:, k, :n], AF.Identity, [xtok, atok, mtok], [("hT", t0 // 256)], relax=True,
                    scale=Avec[:, k * 2 + s:k * 2 + s + 1], bias=modt[l][:, sh_off + k * 2 + s:sh_off + k * 2 + s + 1])
            if after_block is not None:
                after_block()

    def hT_tokens(t0, n):
        return [("hT", b) for b in range(t0 // 256, (t0 + n + 255) // 256)]

    def mod_stats_and_apply(lm, Avec, atok, sh_off, x3, xtok, s, t0, n, sqm, hst, bank, ro):
        mtok = ("modtA" if sh_off == 0 else "modtB", lm)
        for k in range(16):
            MM(ps[bank][:, :n], ones_t[:, :], sqm[:, k, :n], k == 0, k == 15, ["sqm", "ones"], [P(bank)])
        rstd(ps[bank][:, :n], n, 1.0 / D, [P(bank)], off=ro)
        TT(x3, x3, rs_t[:, ro:ro + n].unsqueeze(1).broadcast_to([128, 16, n]), ALU.mult, [xtok, ("rs", ro)], [xtok])
        for k in range(16):
            ACT(hst[:, k, :n], x3[:, k, :], AF.Identity, [xtok, atok, mtok], ["hst"], relax=True,
                scale=Avec[:, k * 2 + s:k * 2 + s + 1], bias=modt[lm][:, sh_off + k * 2 + s:sh_off + k * 2 + s + 1])
        DMA("sp", HT[:, :, t0:t0 + n], hst[:, :, :n], ["hst"], ["HT"])

    def load_hT(hT, ntok):
        for c0 in range(0, ntok, 768):
            c1 = min(c0 + 768, ntok)
            DMA("sp", hT[:, :, c0:c1], HT[:, :, c0:c1], ["HT"], [("hT", b) for b in range(c0 // 256, c1 // 256)])

    def phase_A(l, src):
        last = (l == DEPTH - 1)
        AR.reset(base)
        hT = AR.alloc([128, 16, TA], BF16, "hT")
        xb = [AR.alloc([128, 16, 256], F32, "xb") for _ in range(2)]
        sq = [AR.alloc([128, 16, 256], BF16, "sq")]
        if l == 0:
            wm_t = [AR.alloc([128, 16, 256], BF16, "wm") for _ in range(2)]
        ropec = AR.alloc([128, T], F32, "ropec")
        ropes = AR.alloc([128, T], F32, "ropes")
        wb = [AR.alloc([128, 16, 128], BF16, "wqkf") for _ in range(3)]
        wv_t = AR.alloc([128, 16, 256], BF16, "wv")
        sqh = [AR.alloc([128, 512], BF16, "sqh") for _ in range(2)]
        qn = [AR.alloc([128, 512], BF16, "qn") for _ in range(2)]
        t1 = [AR.alloc([128, 512], F32, "t1") for _ in range(2)]
        t2 = [AR.alloc([128, 512], F32, "t2") for _ in range(2)]
        qo = [AR.alloc([128, 512], BF16, "qo") for _ in range(2)]
        fT = [AR.alloc([128, 512], BF16, "fT") for _ in range(2)]
        gst = [AR.alloc([128, 4, 256], BF16, "gst") for _ in range(2)]
        vst = AR.alloc([128, 18, 256], BF16, "vst")

        DMA("sp", ropec[:, :], ropec_d[:, :], [], ["ropec"])
        DMA("sp", ropes[:, :], ropes_d[:, :], [], ["ropes"])
        blocks = [(i * 256, 256, 0) for i in range(8)] + [(T, 256, 1)]
        keep = 41 if nlayers > 1 else 0
        if l == 0:
            mod_prologue(wm_t)
            modulate(l, src, A1[l], 0, hT, blocks, xb, sq, None, after_block=lambda: modbg.step(2 if len(modbg.todo) > keep + 2 else 0))
        elif NEXTMOD:
            load_hT(hT, TA)
        else:
            modulate(l, src, A1[l], 0, hT, blocks, xb, sq, None)
        def bgstep():
            if l == 0 and len(modbg.todo) > keep:
                modbg.step(1)

        chunks = [(0, 512), (512, 512), (1024, 512), (1536, 512), (T, 256)]
        it = 0
        DMA("pq", wv_t[:, :, :], wv[l], [], ["wv"])
        pipe = Pipe()

        def qk_stage2(i, pa, cb, kind, t0, n, is_ctx):
            pb = 3 + i % 2
            ro = (i % 2) * 512
            gcol = 0 if kind == "q" else 1
            b2 = i % 2
            MM(ps[pb][:, :n], ones_t[:, :], sqh[b2][:, :n], True, True, [("sqh", b2), "ones"], [P(pb)])
            rstd(ps[pb][:, :n], n, 1.0 / 128, [P(pb)], off=ro)
            dst = qn[b2] if not is_ctx else qo[b2]
            dtok = ("qn", b2) if not is_ctx else ("qo", b2)
            STT(dst[:, :n], ps[pa][:, :n], qkn_t[l][:, gcol:gcol + 1], rs_t[:, ro:ro + n], ALU.mult, ALU.mult,
                [P(pa), ("qkn", l), ("rs", ro)], [dtok])
            if is_ctx:
                DMA("sp", QK[:, cb, t0:t0 + n], qo[b2][:, :n], [("qo", b2)], [("QK", cb)])
            else:
                pipe.defer(1, lambda: qk_stage3(i, cb, t0, n))

        def qk_stage3(i, cb, t0, n):
            b2 = i % 2
            pc = 5 + i % 2
            MM(ps[pc][:, :n], swp_t[:, :], qn[b2][:, :n], True, True, [("qn", b2), "swp"], [P(pc)])
            TT(t1[b2][:, :n], qn[b2][:, :n], ropec[:, t0:t0 + n], ALU.mult, [("qn", b2), "ropec"], [("t1", b2)])
            TT(t2[b2][:, :n], ps[pc][:, :n], ropes[:, t0:t0 + n], ALU.mult, [P(pc), "ropes"], [("t2", b2)])
            TT(qo[b2][:, :n], t1[b2][:, :n], t2[b2][:, :n], ALU.add, [("t1", b2), ("t2", b2)], [("qo", b2)])
            DMA("sp", QK[:, cb, t0:t0 + n], qo[b2][:, :n], [("qo", b2)], [("QK", cb)])

        def f_stage2(i, g, t0, n):
            b2 = i % 2
            nt = n // 128
            for j in range(nt):
                pb = (3 + i % 2) if j < 2 else (5 + i % 2)
                jj = j % 2
                MM(ps[pb][:, jj * 256:(jj + 1) * 256], fT[b2][:, j * 128:(j + 1) * 128], cs_t[:, :], True, True,
                   [("fT", b2), "cs"], [P(pb)])
            CP(gst[b2][:, 0:2, :], ps[3 + i % 2][:, :].rearrange("p (a b) -> p a b", a=2), [P(3 + i % 2)], [("gst", b2)])
            if nt > 2:
                CP(gst[b2][:, 2:4, :], ps[5 + i % 2][:, :].rearrange("p (a b) -> p a b", a=2), [P(5 + i % 2)], [("gst", b2)])
            DMA("sp", GCS[:, g, t0 // 128:t0 // 128 + nt, :], gst[b2][:, 0:nt, :], [("gst", b2)], [("GCS", g)])

        for cb in range(18):
            w = wb[cb % 3]
            wtok = ("wqkf", cb % 3)
            DMA("pq", w[:, :, :], wqkf[l, cb], [], [wtok])
            kind = "q" if cb < 8 else ("k" if cb < 10 else "f")
            for (t0, n) in chunks:
                is_ctx = t0 >= T
                if is_ctx and last and kind != "k":
                    continue
                i = it
                pa = it % 3
                it += 1
                for k in range(16):
                    MM(ps[pa][:, :n], w[:, k, :], hT[:, k, t0:t0 + n], k == 0, k == 15,
                       [wtok] + hT_tokens(t0, n), [P(pa)])
                pipe.tick()
                bgstep()
                if kind in ("q", "k"):
                    ACT(sqh[i % 2][:, :n], ps[pa][:, :n], AF.Square, [P(pa)], [("sqh", i % 2)])
                    pipe.defer(1, lambda i=i, pa=pa, cb=cb, kind=kind, t0=t0, n=n, is_ctx=is_ctx:
                               qk_stage2(i, pa, cb, kind, t0, n, is_ctx))
                else:
                    ACT(fT[i % 2][:, :n], ps[pa][:, :n], AF.Copy, [P(pa)], [("fT", i % 2)])
                    pipe.defer(1, lambda i=i, g=cb - 10, t0=t0, n=n: f_stage2(i, g, t0, n))
        pipe.flush()
        if l == 0:
            while len(modbg.todo) > keep:
                modbg.step(1)
            modbg.drain_pending()
        for tt in range(18):
            pa = tt % 2
            for k in range(16):
                MM(ps[pa][:, 0:256], hT[:, k, tt * 128:(tt + 1) * 128], wv_t[:, k, :], k == 0, k == 15,
                   ["wv"] + hT_tokens(tt * 128, 128), [P(pa)])
            ACT(vst[:, tt, :], ps[pa][:, 0:256], AF.Copy, [P(pa)], ["vst"], relax=True)
        DMA("sp", VD[:, :, :], vst[:, :, :], ["vst"], ["VD"])
        S.barrier()

    def phase_F(l):
        last = (l == DEPTH - 1)
        AR.reset(base)
        gcs = AR.alloc([128, 8, 18, 256], BF16, "gcs")
        tab = [AR.alloc([128, 2, 16, 512], BF16, "tab") for _ in range(2)]
        tabc = AR.alloc([128, 2, 2, 256], BF16, "tabc")
        w4_t = AR.alloc([128, 8, 128], BF16, "w4")
        if l == 0 and nlayers > 1:
            fo = [AR.alloc([128, 8, 512], F32, "fo")] * 2
        else:
            fo = [AR.alloc([128, 8, 512], F32, "fo") for _ in range(2)]
        yst = AR.alloc([128, 8, 512], BF16, "yst")
        FTb = [AR.alloc([128, 512], BF16, "FT") for _ in range(2)]
        sqf = [AR.alloc([128, 512], BF16, "sqf") for _ in range(2)]
        DMA("pq", w4_t[:, :, :], w4[l], [], ["w4"])
        for g in range(8):
            DMA("sp", gcs[:, g, :, :], GCS[:, g, :, :], [("GCS", g)], [("gcs", g)])
        DMA("sp", tabc[:, 0, :, :], dftcc_d[:, :, :], [], ["tabc"])
        DMA("sp", tabc[:, 1, :, :], dftsc_d[:, :, :], [], ["tabc"])
        passes = [(kc * 512, 512, 16, 0) for kc in range(4)]
        if not last:
            passes.append((T, 256, 2, 16))
        def load_tab(pi):
            k0_, n_ = passes[pi][0], passes[pi][1]
            if k0_ >= T:
                return
            DMA("sp", tab[pi % 2][:, 0, :, :], dftc_d[:, :, k0_:k0_ + n_], [], [("tab", pi % 2)])
            DMA("sp", tab[pi % 2][:, 1, :, :], dfts_d[:, :, k0_:k0_ + n_], [], [("tab", pi % 2)])

        pipeF = Pipe()
        if l == 0 and modbg.todo:
            modbg.setup([AR.alloc([128, 16, 256], BF16, "wmF") for _ in range(2)])

        def f_stage2(g, pa, n, fo_, pi):
            MM(ps[2 + pa][:, :n], w4_t[:, g, :], FTb[pa][:, :n], True, True, [("FT", pa), "w4"], [P(2 + pa)])
            ACT(sqf[pa][:, :n], ps[2 + pa][:, :n], AF.Square, [P(2 + pa)], [("sqf", pa)])
            ACT(fo_[:, g, :n], ps[2 + pa][:, :n], AF.Copy, [P(2 + pa)], [("fo", id(fo_))])
            pipeF.defer(1, lambda: MM(ps[4 + pi % 2][:, :n], ones_t[:, :], sqf[pa][:, :n], g == 0, g == 7,
                                      [("sqf", pa), "ones"], [P(4 + pi % 2)]))

        load_tab(0)
        for pi, (k0, n, ntt, tt0) in enumerate(passes):
            is_ctx = k0 >= T
            if pi + 1 < len(passes):
                load_tab(pi + 1)
            tb = tab[pi % 2]
            tbtok = ("tab", pi % 2)
            fo_ = fo[pi % 2]
            for g in range(8):
                pa = g % 2
                for ti in range(ntt):
                    if is_ctx:
                        rc, rsn, rtok = tabc[:, 0, ti, :], tabc[:, 1, ti, :], "tabc"
                    else:
                        rc, rsn, rtok = tb[:, 0, ti, :], tb[:, 1, ti, :], tbtok
                    MM(ps[pa][:, :n], gcs[:, g, tt0 + ti, 0:128], rc, ti == 0, False, [("gcs", g), rtok], [P(pa)])
                    MM(ps[pa][:, :n], gcs[:, g, tt0 + ti, 128:256], rsn, False, ti == ntt - 1, [("gcs", g), rtok], [P(pa)])
                pipeF.tick()
                if l == 0:
                    modbg.step(1)
                ACT(FTb[pa][:, :n], ps[pa][:, :n], AF.Copy, [P(pa)], [("FT", pa)])
                pipeF.defer(1, lambda g=g, pa=pa, n=n, fo_=fo_, pi=pi: f_stage2(g, pa, n, fo_, pi))
            pipeF.flush()
            rstd(ps[4 + pi % 2][:, :n], n, 1.0 / 1024, [P(4 + pi % 2)])
            for g in range(8):
                STT(yst[:, g, :n], fo_[:, g, :n], gy_t[l][:, 8 + g:9 + g], rs_t[:, :n], ALU.mult, ALU.mult,
                    [("fo", id(fo_)), ("gy", l), ("rs", 0)], ["yst"], relax=True)
            DMA("sp", YT[:, 8:16, k0:k0 + n], yst[:, :, :n], ["yst"], [("YT", "f")])
        if l == 0:
            modbg.flush()
        S.barrier()

    def phase_T(l):
        last = (l == DEPTH - 1)
        AR.reset(base)
        kT = AR.alloc([128, 2, TA], BF16, "kT")
        vt = AR.alloc([128, 18, 256], BF16, "vt")
        qT = AR.alloc([128, 8, TA], BF16, "qT")
        pT = [[AR.alloc([128, 512], BF16, "pT") for _ in range(2)] for _ in range(2)]
        ao = [AR.alloc([128, 8, 512], F32, "ao") for _ in range(2)]
        sqa = AR.alloc([128, 8, 512], BF16, "sqa")
        rd = [AR.alloc([128, 512], F32, "rd") for _ in range(2)]
        yst = AR.alloc([128, 8, 512], BF16, "ysta")
        DMA("sp", kT[:, :, :], QK[:, 8:10, :], [("QK", 8), ("QK", 9)], ["kT"])
        DMA("sp", vt[:, :, :], VD[:, :, :], ["VD"], ["vt"])
        for h in range(8):
            DMA("sp", qT[:, h, :], QK[:, h, :], [("QK", h)], [("qT", h)])
        scale = float(1.0 / np.sqrt(128.0))
        qchunks = [(i * 512, 512, list(range(18))) for i in range(4)]
        if not last:
            qchunks.append((T, 256, [16, 17]))
        for qi, (t0, n, kts) in enumerate(qchunks):
            ao_ = ao[qi % 2]
            nk = len(kts)
            for hp in range(4):
                heads = (2 * hp, 2 * hp + 1)
                kvh = heads[0] // 4

                def s_step(w, i):
                    h = heads[w]
                    st_ = kts[i]
                    bank = 2 * w + i % 2
                    MM(ps[bank][:, :n], kT[:, kvh, st_ * 128:(st_ + 1) * 128], qT[:, h, t0:t0 + n], True, True,
                       ["kT", ("qT", h)], [P(bank)])
                    ACT(pT[w][i % 2][:, :n], ps[bank][:, :n], AF.Exp, [P(bank)], [("pT", w, i % 2)], scale=scale)

                def pv_step(w, i):
                    st_ = kts[i]
                    po, pd = 4 + 2 * w, 5 + 2 * w
                    MM(ps[po][:, :n], vt[:, st_, kvh * 128:(kvh + 1) * 128], pT[w][i % 2][:, :n], i == 0, i == nk - 1,
                       ["vt", ("pT", w, i % 2)], [P(po)])
                    MM(ps[pd][:, :n], ones_t[:, :], pT[w][i % 2][:, :n], i == 0, i == nk - 1,
                       ["ones", ("pT", w, i % 2)], [P(pd)])

                s_step(0, 0)
                s_step(1, 0)
                for i in range(nk):
                    if i + 1 < nk:
                        s_step(0, i + 1)
                        s_step(1, i + 1)
                    pv_step(0, i)
                    pv_step(1, i)
                for w in range(2):
                    h = heads[w]
                    po, pd = 4 + 2 * w, 5 + 2 * w
                    ACT(rd[w][:, :n], ps[pd][:, :n], AF.Ln, [P(pd)], [("rd", w)])
                    ACT(rd[w][:, :n], rd[w][:, :n], AF.Exp, [("rd", w)], [("rd", w)], scale=-1.0)
                    TT(ao_[:, h, :n], ps[po][:, :n], rd[w][:, :n], ALU.mult, [P(po), ("rd", w)], [("ao", qi % 2, h)])
                    TT(sqa[:, h, :n], ao_[:, h, :n], ao_[:, h, :n], ALU.mult, [("ao", qi % 2, h)], [("sqa", h)])
            for h in range(8):
                MM(ps[0][:, :n], ones_t[:, :], sqa[:, h, :n], h == 0, h == 7, [("sqa", h), "ones"], [P(0)])
            rstd(ps[0][:, :n], n, 1.0 / 1024, [P(0)])
            for h in range(8):
                STT(yst[:, h, :n], ao_[:, h, :n], gy_t[l][:, h:h + 1], rs_t[:, :n], ALU.mult, ALU.mult,
                    [("ao", qi % 2, h), ("gy", l), ("rs", 0)], ["ysta"], relax=True)
            DMA("sp", YT[:, 0:8, t0:t0 + n], yst[:, :, :n], ["ysta"], [("YT", "a")])
        S.barrier()

    def phase_O(l, src, dst):
        last = (l == DEPTH - 1)
        AR.reset(base)
        wo = AR.alloc([128, 16, D], BF16, "wo")
        yb = [AR.alloc([128, 16, 256], BF16, "yb") for _ in range(3)]
        xb = [AR.alloc([128, 16, 256], F32, "xbo") for _ in range(3)]
        mixs2 = [AR.alloc([128, 16, 256], F32, "mixs") for _ in range(2)]
        sqb = [AR.alloc([128, 256], BF16, "sqb") for _ in range(2)]
        sqm = AR.alloc([128, 16, 256], BF16, "sqm")
        hst = AR.alloc([128, 16, 256], BF16, "hst")
        for k in range(16):
            DMA("pq", wo[:, k, :], wout[l, :, k, :], [], [("wo", k)])
        wotoks = [("wo", k) for k in range(16)]
        nb = 8 if last else 9
        def load_blk(b):
            DMA("sp", yb[b % 3][:, :, :], YT[:, :, b * 256:b * 256 + 256], [("YT", "a"), ("YT", "f")], [("yb", b % 3)])
            DMA("sp", xb[b % 3][:, :, :], src[:, :, b * 256:b * 256 + 256], [("R", src.tensor.name)], [("xbo", b % 3)])

        pipeO = Pipe()

        def tailO(n, pss, mixs, mb, x_, b, t0):
            ro = (b % 2) * 512
            rstd(ps[pss][:, :n], n, 1.0 / D, [P(pss)], off=ro)
            for dc in range(16):
                TT(mixs[:, dc, :], mixs[:, dc, :], rs_t[:, ro:ro + n], ALU.mult, [("mixs", mb, dc), ("rs", ro)], [("mixs", mb, dc)])
            for dc in range(16):
                TT(x_[:, dc, :], mixs[:, dc, :], x_[:, dc, :], ALU.add, [("mixs", mb, dc), ("xbo", b % 3)], [("xbo", b % 3)])
            DMA("sp", dst[:, :, t0:t0 + n], x_[:, :, :], [("xbo", b % 3)], [("R", dst.tensor.name)])
            x3 = x_[:, :, :n]
            xtok = ("xbo", b % 3)
            S.op("pool", lambda e: e.tensor_tensor(out=sqm[:, :, :n], in0=x3, in1=x3, op=ALU.mult), [xtok], ["sqm"])
            s_ = 1 if t0 >= T else 0
            ro2 = 256 + (b % 2) * 512
            mtok = ("modtB", l)

            def p_stats():
                for k in range(16):
                    MM(ps[6][:, :n], ones_t[:, :], sqm[:, k, :n], k == 0, k == 15, ["sqm", "ones"], [P(6)])

            def p_rstd():
                rstd(ps[6][:, :n], n, 1.0 / D, [P(6)], off=ro2)
                TT(x3, x3, rs_t[:, ro2:ro2 + n].unsqueeze(1).broadcast_to([128, 16, n]), ALU.mult, [xtok, ("rs", ro2)], [xtok])

            def p_apply(k0):
                for k in range(k0, k0 + 4):
                    ACT(hst[:, k, :n], x_[:, k, :n], AF.Identity, [xtok, ("A2", l), mtok], ["hst"], relax=True,
                        scale=A2[l][:, k * 2 + s_:k * 2 + s_ + 1], bias=modt[l][:, 96 + k * 2 + s_:96 + k * 2 + s_ + 1])

            def p_store():
                DMA("sp", HT[:, :, t0:t0 + n], hst[:, :, :n], ["hst"], ["HT"])
                if b + 3 < nb:
                    load_blk(b + 3)

            pipeO.defer(11, p_stats)
            pipeO.defer(12, p_rstd)
            for q_ in range(4):
                pipeO.defer(13 + q_, lambda k0=4 * q_: p_apply(k0))
            pipeO.defer(17, p_store)

        load_blk(0)
        if nb > 1:
            load_blk(1)
        if nb > 2:
            load_blk(2)
        for b in range(nb):
            t0 = b * 256
            n = 256
            pss = 4 + b % 2
            s = 1 if t0 >= T else 0
            y_ = yb[b % 3]
            x_ = xb[b % 3]
            mixs = mixs2[b % 2]
            mb = b % 2
            for dc in range(16):
                pa = dc % 4
                for k in range(16):
                    MM(ps[pa][:, :n], wo[:, k, dc * 128:(dc + 1) * 128], y_[:, k, :], k == 0, k == 15,
                       [("wo", k), ("yb", b % 3)], [P(pa)])
                pipeO.tick()
                ACT(sqb[dc % 2][:, :n], ps[pa][:, :n], AF.Square, [P(pa)], [("sqb", dc % 2)])
                ACT(mixs[:, dc, :], ps[pa][:, :n], AF.Identity, [P(pa), ("G1", l)], [("mixs", mb, dc)],
                    scale=G1[l][:, dc * 2 + s:dc * 2 + s + 1])
                pipeO.defer(1, lambda dc=dc, n=n, pss=pss: MM(ps[pss][:, :n], ones_t[:, :], sqb[dc % 2][:, :n], dc == 0, dc == 15,
                                                     [("sqb", dc % 2), "ones"], [P(pss)]))
            pipeO.defer(1, lambda n=n, pss=pss, mixs=mixs, mb=mb, x_=x_, b=b, t0=t0: tailO(n, pss, mixs, mb, x_, b, t0))
        pipeO.flush()
        S.barrier()

    def phase_U(l, src):
        last = (l == DEPTH - 1)
        AR.reset(base)
        hT = AR.alloc([128, 16, TA], BF16, "hTu")
        xb = [AR.alloc([128, 16, 256], F32, "xbu") for _ in range(2)]
        sq = [AR.alloc([128, 16, 256], BF16, "squ") for _ in range(2)]
        tmp = [AR.alloc([128, 256], F32, "tmpu") for _ in range(2)]
        cw_t = AR.alloc([128, NFC * 3], F32, "cw")
        cb_t = AR.alloc([128, NFC], F32, "cbt")
        wb = [AR.alloc([128, 2, 16, 128], BF16, "wup") for _ in range(2)]
        NT = T if last else TA
        gbuf = [AR.alloc([128, TA], F32, "gbuf") for _ in range(2)]
        vbuf = [AR.alloc([128, TA], BF16, "vbuf") for _ in range(2)]
        cbuf = [AR.alloc([128, TA], F32, "cbuf") for _ in range(2)]
        ast = [AR.alloc([128, TA], BF16, "ast") for _ in range(2)]
        DMA("sp", cw_t[:, :], cw[l], [], ["cw"])
        DMA("sp", cb_t[:, :], cbias[l], [], ["cbt"])
        blocks = [(i * 256, 256, 0) for i in range(8)]
        if not last:
            blocks.append((T, 256, 1))
        load_hT(hT, T if last else TA)
        chunks = [(0, 512), (512, 512), (1024, 512), (1536, 512)]
        segs = [(0, T)]
        if not last:
            chunks.append((T, 256))
            segs.append((T, TA))
        it = 0
        for fc in range(NFC):
            w = wb[fc % 2]
            wtok = ("wup", fc % 2)
            DMA("pq", w[:, :, :, :], wup[l, fc], [], [wtok])
            gb, vb, cbf, as_ = gbuf[fc % 2], vbuf[fc % 2], cbuf[fc % 2], ast[fc % 2]
            gt, vtk, ct, at = ("gbuf", fc % 2), ("vbuf", fc % 2), ("cbuf", fc % 2), ("ast", fc % 2)
            for (t0, n) in chunks:
                pa = it % 4
                pb = 4 + it % 4
                it += 1
                for k in range(16):
                    MM(ps[pa][:, :n], w[:, 0, k, :], hT[:, k, t0:t0 + n], k == 0, k == 15, [wtok] + hT_tokens(t0, n), [P(pa)])
                ACT(gb[:, t0:t0 + n], ps[pa][:, :n], AF.Copy, [P(pa)], [gt])
                for k in range(16):
                    MM(ps[pb][:, :n], w[:, 1, k, :], hT[:, k, t0:t0 + n], k == 0, k == 15, [wtok] + hT_tokens(t0, n), [P(pb)])
                CP(vb[:, t0:t0 + n], ps[pb][:, :n], [P(pb)], [vtk])
            c0 = cw_t[:, fc * 3 + 0:fc * 3 + 1]
            c1 = cw_t[:, fc * 3 + 1:fc * 3 + 2]
            c2 = cw_t[:, fc * 3 + 2:fc * 3 + 3]
            for (a, b) in segs:
                ACT(cbf[:, a:b], gb[:, a:b], AF.Identity, [gt, "cw", "cbt"], [ct], scale=c1, bias=cb_t[:, fc:fc + 1])
                STT(cbf[:, a + 1:b], gb[:, a:b - 1], c0, cbf[:, a + 1:b], ALU.mult, ALU.add, [gt, ct, "cw"], [ct])
                STT(cbf[:, a:b - 1], gb[:, a + 1:b], c2, cbf[:, a:b - 1], ALU.mult, ALU.add, [gt, ct, "cw"], [ct])
            if gelu_mode == "tanh_act":
                ACT(cbf[:, :NT], cbf[:, :NT], AF.Gelu_apprx_tanh, [ct], [ct])
            else:
                gsc = gb
                TT(gsc[:, :NT], cbf[:, :NT], cbf[:, :NT], ALU.mult, [ct], [gt])
                TS(gsc[:, :NT], gsc[:, :NT], 0.044715, 1.0, ALU.mult, ALU.add, [gt], [gt])
                TT(gsc[:, :NT], gsc[:, :NT], cbf[:, :NT], ALU.mult, [gt, ct], [gt])
                ACT(gsc[:, :NT], gsc[:, :NT], AF.Sigmoid, [gt], [gt], scale=float(2.0 * np.sqrt(2.0 / np.pi)))
                TT(cbf[:, :NT], cbf[:, :NT], gsc[:, :NT], ALU.mult, [gt, ct], [ct])
            TT(as_[:, :NT], cbf[:, :NT], vb[:, :NT], ALU.mult, [ct, vtk], [at])
            DMA("sp", ACTD[:, fc, :NT], as_[:, :NT], [at], [("ACTD", fc)])
        S.barrier()

    def phase_D(l, src, dst, dst_is_out):
        last = (l == DEPTH - 1)
        AR.reset(base)
        ab = AR.alloc([128, NFC, 768], BF16, "ab")
        ys = AR.alloc([128, 16, 768], F32, "ys")
        wb = [AR.alloc([128, NFC, 128], BF16, "wdn") for _ in range(2)]
        xall = AR.alloc([128, 16, 768], F32, "xall")
        sqd = [AR.alloc([128, 384], BF16, "sqd") for _ in range(2)]
        if NEXTMOD and l + 1 < nlayers:
            sqm = AR.alloc([128, 16, 128], BF16, "sqmd")
            hst = AR.alloc([128, 16, 128], BF16, "hstd")
        print("D arena slack", AR.hi - AR.off)
        blocks = [(0, 768), (768, 768), (1536, 512 if last else 768)]
        wi = 0

        def load_ab(bi):
            t0_, n_ = blocks[bi]
            for f0 in range(0, NFC, 11):
                DMA("sp", ab[:, f0:f0 + 11, :n_], ACTD[:, f0:f0 + 11, t0_:t0_ + n_],
                    [("ACTD", f) for f in range(f0, f0 + 11)], [("ab", f0)])

        def load_x(bi):
            t0_, n_ = blocks[bi]
            DMA("sp", xall[:, :, :n_], src[:, :, t0_:t0_ + n_], [("R", src.tensor.name)], ["xall"])

        pipeD = Pipe()
        load_ab(0)
        load_x(0)
        for bi, (t0, n) in enumerate(blocks):
            cn = n // 2
            it = 0
            rngs = []
            if t0 + n <= T:
                rngs.append((0, n, 0))
            else:
                if t0 < T:
                    rngs.append((0, T - t0, 0))
                rngs.append((max(T - t0, 0), n, 1))
            for dc in range(16):
                w = wb[wi % 2]
                wtok = ("wdn", wi % 2)
                wi += 1
                DMA("pq", w[:, :, :], wdn[l, dc], [], [wtok])
                for ci in range(2):
                    c0 = ci * cn
                    pa = it % 4
                    it += 1
                    for fc in range(NFC):
                        MM(ps[pa][:, :cn], w[:, fc, :], ab[:, fc, c0:c0 + cn], fc == 0, fc == NFC - 1,
                           [wtok, ("ab", (fc // 11) * 11)], [P(pa)])
                    pipeD.tick()
                    ACT(sqd[pa % 2][:, :cn], ps[pa][:, :cn], AF.Square, [P(pa)], [("sqd", pa % 2)])
                    for (u0, u1, s_) in rngs:
                        a0, a1 = max(u0, c0), min(u1, c0 + cn)
                        if a1 > a0:
                            ACT(ys[:, dc, a0:a1], ps[pa][:, a0 - c0:a1 - c0], AF.Identity, [P(pa), ("G2", l)], [("ys", dc)],
                                scale=G2[l][:, dc * 2 + s_:dc * 2 + s_ + 1])
                    pipeD.defer(1, lambda ci=ci, cn=cn, pa=pa, dc=dc: MM(ps[4 + ci][:, :cn], ones_t[:, :], sqd[pa % 2][:, :cn],
                                                               dc == 0, dc == 15, [("sqd", pa % 2), "ones"], [P(4 + ci)]))
            pipeD.flush()
            if bi + 1 < len(blocks):
                load_ab(bi + 1)
            for ci in range(2):
                rstd(ps[4 + ci][:, :cn], cn, 1.0 / D, [P(4 + ci)], off=ci * cn)
            rstoks = [("rs", 0), ("rs", cn)]
            for dc in range(16):
                TT(ys[:, dc, :n], ys[:, dc, :n], rs_t[:, :n], ALU.mult, [("ys", dc)] + rstoks, [("ys", dc)])
            for dc in range(16):
                TT(xall[:, dc, :n], ys[:, dc, :n], xall[:, dc, :n], ALU.add, [("ys", dc), "xall"], ["xall"])
            o = DMA("sp", dst[:, :, t0:t0 + n], xall[:, :, :n], ["xall"], [("R", dst.tensor.name)])
            if dst_is_out:
                finals.append(o)
            if NEXTMOD and l + 1 < nlayers:
                nsub = n // 128

                def submod(j, bi=bi, t0=t0, nsub=nsub):
                    u0 = j * 128
                    tg = t0 + u0
                    s_ = 1 if tg >= T else 0
                    ACT(sqm[:, :, :], xall[:, :, u0:u0 + 128], AF.Square, ["xall"], ["sqm"])

                    def sub2():
                        mod_stats_and_apply(l + 1, A1[l + 1], ("A1", l + 1), 0, xall[:, :, u0:u0 + 128], "xall",
                                            s_, tg, 128, sqm, hst, 6, 768)
                        if j + 1 < nsub:
                            submod(j + 1)
                        elif bi + 1 < len(blocks):
                            load_x(bi + 1)
                    pipeD.defer(1, sub2)
                pipeD.defer(4, lambda: submod(0))
            elif bi + 1 < len(blocks):
                load_x(bi + 1)
        pipeD.flush()
        S.barrier()

    finals = []

    def run_all():
        src = xt
        for l in range(nlayers):
            is_out = (l == DEPTH - 1)
            for nm, fn in (("A", lambda: phase_A(l, src)), ("F", lambda: phase_F(l)), ("T", lambda: phase_T(l)),
                           ("O", lambda: phase_O(l, src, R1)), ("U", lambda: phase_U(l, R1)),
                           ("D", lambda: phase_D(l, R1, out_d if is_out else R2, is_out))):
                fn()
                if stop in (nm, nm + str(l)) and (len(stop) == 2 or l == 0):
                    return
            src = R2

    run_all()
    if debug:
        dbgm = nc.dram_tensor("dbgm", [128, 192 + 4 * 32], F32, kind="ExternalOutput").ap()
        o = DMA("sp", dbgm[:, 0:192], modt[0][:, :], [("modtA", 0), ("modtB", 0)], ["dbgm"])
        finals.append(o)
        for i, tl in enumerate((A1, G1, A2, G2)):
            nm = ("A1", "G1", "A2", "G2")[i]
            finals.append(DMA("sp", dbgm[:, 192 + i * 32:192 + (i + 1) * 32], tl[0][:, :], [(nm, 0)], ["dbgm"]))
    S.emit(finals)
    return nc


def _bf16(a):
    return np.ascontiguousarray(a.astype(np.float32)).astype(ml_dtypes.bfloat16)


def _consts():
    c = {}
    c["ones"] = _bf16(np.ones((128, 128)))
    sw = np.zeros((128, 128), np.float32)
    for i in range(128):
        sw[i ^ 1, i] = 1.0
    c["swp"] = _bf16(sw)
    cidx = np.arange(128)[:, None].astype(np.float64)
    m = np.arange(128)[None, :].astype(np.float64)
    ang = 2 * np.pi * cidx * m / 128.0
    c["cs"] = _bf16(np.concatenate([np.cos(ang), -np.sin(ang)], axis=1) / np.sqrt(128.0))
    pos = np.arange(T)
    row = (pos // 64).astype(np.float32)
    col = (pos % 64).astype(np.float32)
    freqs = (np.float32(10000.0) ** (-np.arange(32, dtype=np.float32) / np.float32(32))).astype(np.float32)
    angp = np.concatenate([row[:, None] * freqs, col[:, None] * freqs], axis=-1).astype(np.float32)
    cosv = np.cos(angp).astype(np.float32)
    sinv = np.sin(angp).astype(np.float32)
    rc = np.zeros((128, T), np.float32)
    rsn = np.zeros((128, T), np.float32)
    for j in range(64):
        rc[2 * j] = cosv[:, j]
        rc[2 * j + 1] = cosv[:, j]
        rsn[2 * j] = -sinv[:, j]
        rsn[2 * j + 1] = sinv[:, j]
    c["ropec"] = rc
    c["ropes"] = rsn

    def dft(n):
        t = np.arange(n, dtype=np.int64)
        kt = (t[:, None] * t[None, :]) % n
        a = 2 * np.pi * kt.astype(np.float64) / n
        cm = np.cos(a) / np.sqrt(n)
        sm = np.sin(a) / np.sqrt(n)
        nt = n // 128
        cm = cm.reshape(nt, 128, n).transpose(1, 0, 2)
        sm = sm.reshape(nt, 128, n).transpose(1, 0, 2)
        return _bf16(cm), _bf16(sm)

    c["dftc"], c["dfts"] = dft(T)
    c["dftcc"], c["dftsc"] = dft(CTX)
    return c


def _fm(v):
    sh = v.shape
    n = sh[-1] // 128
    return np.ascontiguousarray(np.swapaxes(v.reshape(sh[:-1] + (n, 128)), -1, -2))


def _prep_shared(c_ctx, w_mod, b_mod, g_pre_mix, g_post_mix, g_pre_ffn, g_post_ffn, w_in, q_norm, k_norm,
                 w_four, g_attn_out, g_four_out, w_out, w_up, conv_w, conv_b, w_down):
    f = np.float32
    sh = {}
    sh["wmod"] = np.ascontiguousarray(w_mod.reshape(DEPTH, 16, 128, 48, 256).transpose(0, 3, 2, 1, 4))
    bm = _fm(b_mod)
    sh["bmod2"] = np.ascontiguousarray(np.repeat(bm, 2, axis=-1))
    gs = np.stack([_fm(g_pre_mix), _fm(g_post_mix), _fm(g_pre_ffn), _fm(g_post_ffn)], axis=1)
    sh["g2"] = np.ascontiguousarray(np.repeat(gs, 2, axis=-1))
    wi = w_in.reshape(DEPTH, 16, 128, 2560)
    cols = list(range(0, 1280, 128)) + list(range(1536, 2560, 128))
    sh["wqkf"] = np.ascontiguousarray(
        np.stack([wi[:, :, :, c0:c0 + 128] for c0 in cols], axis=1).transpose(0, 1, 3, 2, 4))
    sh["wv"] = np.ascontiguousarray(wi[:, :, :, 1280:1536].transpose(0, 2, 1, 3))
    sh["qkn"] = np.ascontiguousarray(np.stack([q_norm, k_norm], axis=-1))
    sh["w4"] = np.ascontiguousarray(w_four.transpose(0, 2, 1, 3))
    sh["gy"] = np.ascontiguousarray(np.concatenate([_fm(g_attn_out), _fm(g_four_out)], axis=-1))
    sh["wout"] = np.ascontiguousarray(w_out.reshape(DEPTH, 16, 128, D).transpose(0, 2, 1, 3))
    wu = w_up.reshape(DEPTH, 16, 128, 2, NFC, 128)
    sh["wup"] = np.ascontiguousarray(wu.transpose(0, 4, 2, 3, 1, 5))
    cwf = conv_w.reshape(DEPTH, 3, NFC, 128).transpose(0, 3, 2, 1)
    sh["cw"] = np.ascontiguousarray(cwf.reshape(DEPTH, 128, NFC * 3))
    sh["cb"] = _fm(conv_b)
    wd = w_down.reshape(DEPTH, NFC, 128, 16, 128)
    sh["wdn"] = np.ascontiguousarray(wd.transpose(0, 3, 2, 1, 4))
    for k in sh:
        assert sh[k].dtype == f, k
    sh.update(_consts())
    return sh


def _prep_core(b, x, c, ctx, c_ctx):
    xa = np.concatenate([x[b].T, ctx[b].T], axis=1)
    xt = np.ascontiguousarray(xa.reshape(16, 128, TA).transpose(1, 0, 2))
    cc = np.stack([_fm(c[b]), _fm(c_ctx)], axis=-1).reshape(128, 32)
    return {"xt": xt, "cc": np.ascontiguousarray(cc)}


_NC_CACHE = {}


def kernel(x, c, ctx, c_ctx, w_mod, b_mod, g_pre_mix, g_post_mix, g_pre_ffn, g_post_ffn,
           w_in, q_norm, k_norm, w_four, g_attn_out, g_four_out, w_out,
           w_up, conv_w, conv_b, w_down):
    args = [np.asarray(a, dtype=np.float32) for a in
            (x, c, ctx, c_ctx, w_mod, b_mod, g_pre_mix, g_post_mix, g_pre_ffn, g_post_ffn, w_in, q_norm, k_norm,
             w_four, g_attn_out, g_four_out, w_out, w_up, conv_w, conv_b, w_down)]
    (x, c, ctx, c_ctx, w_mod, b_mod, g_pre_mix, g_post_mix, g_pre_ffn, g_post_ffn, w_in, q_norm, k_norm,
     w_four, g_attn_out, g_four_out, w_out, w_up, conv_w, conv_b, w_down) = args
    shared = _prep_shared(c_ctx, w_mod, b_mod, g_pre_mix, g_post_mix, g_pre_ffn, g_post_ffn, w_in, q_norm, k_norm,
                          w_four, g_attn_out, g_four_out, w_out, w_up, conv_w, conv_b, w_down)
    in_maps = []
    for b in range(NCORES):
        m = dict(shared)
        m.update(_prep_core(b, x, c, ctx, c_ctx))
        in_maps.append(m)
    nc = build_nc()
    res = run_bass_kernel_spmd(nc, in_maps, core_ids=list(range(NCORES)))
    outs = []
    for b in range(NCORES):
        o = np.asarray(res.results[b]["out"], dtype=np.float32)
        outs.append(o.transpose(1, 0, 2).reshape(D, T).T)
    return np.ascontiguousarray(np.stack(outs, axis=0)).astype(np.float32)
```
